# Optimizing a Trainium2 kernel written in Bass

```python
import math
import jax, jax.numpy as jnp
from jax import lax
import numpy as np

D_MODEL = 1024
BATCH = 4
SEQ = 4096
DEPTH = 4

DA_HEADS = 4
DA_HEAD_DIM = 64
DA_Q_BLOCK = 128
SW_HEADS = 8
SW_KV_HEADS = 2
SW_HEAD_DIM = 64
SW_WINDOW = 128
D_FF = 2816
CONV_WIDTH = 3
EPS = 1e-6
NEG_INF = -1e30
N_BRANCHES = 2

DA_QK_WIDTH = DA_HEADS * 2 * DA_HEAD_DIM
DA_V_WIDTH = DA_HEADS * 2 * DA_HEAD_DIM
SW_Q_WIDTH = SW_HEADS * SW_HEAD_DIM
SW_KV_WIDTH = SW_KV_HEADS * SW_HEAD_DIM
IN_SPLITS = (DA_QK_WIDTH, DA_QK_WIDTH, DA_V_WIDTH, SW_Q_WIDTH, SW_KV_WIDTH, SW_KV_WIDTH, D_MODEL, D_MODEL)
IN_COLS = 3 * DA_QK_WIDTH + SW_Q_WIDTH + 2 * SW_KV_WIDTH + N_BRANCHES * D_MODEL

kernel_name = "hybrid_diffattn_swa_sinks_convffn"


def rmsnorm(x, w):
    xf = x.astype(jnp.float32)
    y = xf * lax.rsqrt(jnp.mean(xf * xf, axis=-1, keepdims=True) + EPS)
    return (y * w.astype(jnp.float32)).astype(x.dtype)


def alibi_slopes(n_heads):
    h = jnp.arange(1, n_heads + 1, dtype=jnp.float32)
    return jnp.exp2(-8.0 * h / n_heads)


def split_cols(proj):
    idx = []
    acc = 0
    for w in IN_SPLITS[:-1]:
        acc += w
        idx.append(acc)
    return jnp.split(proj, idx, axis=-1)


def diff_attention(q, k, v, lam, lam_init, subln_w):
    B, S, _ = q.shape
    H, d = DA_HEADS, DA_HEAD_DIM
    nb = S // DA_Q_BLOCK
    q = q.reshape(B, S, H, 2, d)
    k = k.reshape(B, S, H, 2, d)
    v = v.reshape(B, S, H, 2 * d)
    slopes = alibi_slopes(H)
    qb = jnp.moveaxis(q.reshape(B, nb, DA_Q_BLOCK, H, 2, d), 1, 0)
    key_pos = jnp.arange(S)
    scale = d ** -0.5

    def block(args):
        q_blk, i = args
        s = jnp.einsum('bqhcd,bshcd->bhcqs', q_blk, k, preferred_element_type=jnp.float32) * scale
        q_pos = i * DA_Q_BLOCK + jnp.arange(DA_Q_BLOCK)
        dist = q_pos[:, None] - key_pos[None, :]
        logits = s - slopes[:, None, None, None] * dist.astype(jnp.float32)
        logits = jnp.where(dist >= 0, logits, NEG_INF)
        p = jax.nn.softmax(logits, axis=-1)
        a = p[:, :, 0] - lam * p[:, :, 1]
        return jnp.einsum('bhqs,bshe->bqhe', a.astype(v.dtype), v)

    o = lax.map(block, (qb, jnp.arange(nb)))
    o = jnp.moveaxis(o, 0, 1).reshape(B, S, H, 2 * d)
    o = rmsnorm(o, subln_w) * (1.0 - lam_init)
    return o.reshape(B, S, H * 2 * d)


def sliding_window_attention(q, k, v, sinks):
    B, S, _ = q.shape
    W, KVH, d = SW_WINDOW, SW_KV_HEADS, SW_HEAD_DIM
    G = SW_HEADS // KVH
    nb = S // W
    qb = q.reshape(B, nb, W, KVH, G, d)

    def windows(t):
        tb = t.reshape(B, nb, W, KVH, d)
        tpad = jnp.concatenate([jnp.zeros_like(tb[:, :1]), tb], axis=1)
        return jnp.concatenate([tpad[:, :-1], tpad[:, 1:]], axis=2)

    kw, vw = windows(k), windows(v)
    s = jnp.einsum('bnqkgd,bnskd->bnkgqs', qb, kw, preferred_element_type=jnp.float32) * (d ** -0.5)
    q_loc = jnp.arange(W) + W
    k_loc = jnp.arange(2 * W)
    dist = q_loc[:, None] - k_loc[None, :]
    key_abs = jnp.arange(nb)[:, None] * W + k_loc[None, :] - W
    valid = ((dist >= 0) & (dist < W))[None, :, :] & (key_abs >= 0)[:, None, :]
    slopes = alibi_slopes(SW_HEADS).reshape(KVH, G)
    logits = s - slopes[:, :, None, None] * dist.astype(jnp.float32)
    logits = jnp.where(valid[None, :, None, None], logits, NEG_INF)
    sink = jnp.broadcast_to(sinks.astype(jnp.float32).reshape(KVH, G, 1, 1), logits.shape[:-1] + (1,))
    p = jax.nn.softmax(jnp.concatenate([logits, sink], axis=-1), axis=-1)[..., :-1]
    o = jnp.einsum('bnkgqs,bnskd->bnqkgd', p.astype(v.dtype), vw)
    return o.reshape(B, S, KVH * G * d)


def conv_ffn(xn, w_up, conv_w, conv_b, w_down):
    S = xn.shape[1]
    u = xn @ w_up
    upad = jnp.pad(u, ((0, 0), (CONV_WIDTH - 1, 0), (0, 0)))
    c = conv_b
    for j in range(CONV_WIDTH):
        c = c + upad[:, j:j + S] * conv_w[j]
    gate, val = jnp.split(c, 2, axis=-1)
    return (jax.nn.silu(gate) * val) @ w_down


def setup_inputs(seed: int = 0) -> dict:
    key = jax.random.key(seed)
    ks = jax.random.split(key, 18)
    f32 = jnp.float32
    nrm = lambda k, shape, s: jax.random.normal(k, shape, f32) * s
    return {
        "x": nrm(ks[0], (BATCH, SEQ, D_MODEL), 1.0),
        "norm_mix_w": 1.0 + nrm(ks[1], (DEPTH, D_MODEL), 0.02),
        "w_in": nrm(ks[2], (DEPTH, D_MODEL, IN_COLS), D_MODEL ** -0.5),
        "lambda_q1": nrm(ks[3], (DEPTH, DA_HEAD_DIM), 0.1),
        "lambda_k1": nrm(ks[4], (DEPTH, DA_HEAD_DIM), 0.1),
        "lambda_q2": nrm(ks[5], (DEPTH, DA_HEAD_DIM), 0.1),
        "lambda_k2": nrm(ks[6], (DEPTH, DA_HEAD_DIM), 0.1),
        "subln_w": 1.0 + nrm(ks[7], (DEPTH, 2 * DA_HEAD_DIM), 0.02),
        "sinks": nrm(ks[8], (DEPTH, SW_HEADS), 0.5),
        "w_br_da": nrm(ks[9], (DEPTH, DA_V_WIDTH, D_MODEL), DA_V_WIDTH ** -0.5),
        "w_br_sw": nrm(ks[10], (DEPTH, SW_Q_WIDTH, D_MODEL), SW_Q_WIDTH ** -0.5),
        "w_mix_out": nrm(ks[11], (DEPTH, D_MODEL, D_MODEL), D_MODEL ** -0.5),
        "norm_ffn_w": 1.0 + nrm(ks[12], (DEPTH, D_MODEL), 0.02),
        "w_up": nrm(ks[13], (DEPTH, D_MODEL, 2 * D_FF), D_MODEL ** -0.5),
        "conv_w": nrm(ks[14], (DEPTH, CONV_WIDTH, 2 * D_FF), CONV_WIDTH ** -0.5),
        "conv_b": nrm(ks[15], (DEPTH, 2 * D_FF), 0.02),
        "w_down": nrm(ks[16], (DEPTH, D_FF, D_MODEL), D_FF ** -0.5),
        "norm_final_w": 1.0 + nrm(ks[17], (D_MODEL,), 0.02),
    }


def reference(x, norm_mix_w, w_in, lambda_q1, lambda_k1, lambda_q2, lambda_k2, subln_w, sinks,
              w_br_da, w_br_sw, w_mix_out, norm_ffn_w, w_up, conv_w, conv_b, w_down, norm_final_w):
    for l in range(DEPTH):
        lam_init = 0.8 - 0.6 * math.exp(-0.3 * l)
        xn = rmsnorm(x, norm_mix_w[l])
        proj = xn @ w_in[l]
        da_q, da_k, da_v, sw_q, sw_k, sw_v, g_da, g_sw = split_cols(proj)
        lam = (jnp.exp(jnp.sum(lambda_q1[l].astype(jnp.float32) * lambda_k1[l].astype(jnp.float32)))
               - jnp.exp(jnp.sum(lambda_q2[l].astype(jnp.float32) * lambda_k2[l].astype(jnp.float32)))
               + lam_init)
        y_da = diff_attention(da_q, da_k, da_v, lam, lam_init, subln_w[l])
        y_sw = sliding_window_attention(sw_q, sw_k, sw_v, sinks[l])
        merged = jax.nn.sigmoid(g_da) * (y_da @ w_br_da[l]) + jax.nn.sigmoid(g_sw) * (y_sw @ w_br_sw[l])
        x = x + merged @ w_mix_out[l]
        xn = rmsnorm(x, norm_ffn_w[l])
        x = x + conv_ffn(xn, w_up[l], conv_w[l], conv_b[l], w_down[l])
    return rmsnorm(x, norm_final_w)
```

```python
import math
import numpy as np
import ml_dtypes
import concourse.bass as bass
import concourse.mybir as mybir
from concourse.bass_utils import run_bass_kernel_spmd

F32 = mybir.dt.float32
BF16 = mybir.dt.bfloat16
AF = mybir.ActivationFunctionType
ALU = mybir.AluOpType
BF = ml_dtypes.bfloat16

D = 1024
S = 4096
B = 4
DEPTH = 4
T = 2048
NT = T // 128
NS = T // 512
INC = 4352
DFF = 2816
EPS = 1e-6
NEG = -30000.0

ENGS = ("tensor", "vector", "scalar", "gpsimd", "sync")


class Prog:
    def __init__(self, nc, n_dma_sems=24):
        self.nc = nc
        self.streams = {e: [] for e in ENGS}
        self.count = {e: 0 for e in ENGS}
        self.n_dma = n_dma_sems
        self.dma_cnt = [0] * n_dma_sems
        self.pool = {"gpsimd": list(range(0, 8)), "sync": list(range(8, n_dma_sems - 4)), "scalar": list(range(n_dma_sems - 4, n_dma_sems))}
        self.dma_rr = {"gpsimd": 0, "sync": 0, "scalar": 0}
        self.all_dma = []

    def op(self, eng, fn, deps=()):
        self.count[eng] += 1
        tok = ("e", eng, self.count[eng])
        self.streams[eng].append(("op", fn, [d for d in deps if d is not None]))
        return tok

    def dma(self, queue, out, in_, deps=(), slow=False):
        pl = self.pool[queue]
        i = pl[self.dma_rr[queue] % len(pl)]
        self.dma_rr[queue] += 1
        prev = ("d", i, self.dma_cnt[i]) if self.dma_cnt[i] > 0 else None
        self.dma_cnt[i] += 1
        tok = ("d", i, self.dma_cnt[i])
        dl = [d for d in deps if d is not None]
        if prev is not None:
            dl.append(prev)
        self.streams[queue].append(("dma", (out, in_, i, slow), dl))
        self.all_dma.append(tok)
        return tok

    def wait_only(self, eng, deps):
        self.streams[eng].append(("wait", None, [d for d in deps if d is not None]))

    def emit(self, block):
        nc = self.nc
        import contextlib
        with contextlib.ExitStack() as es:
            esem = {e: es.enter_context(nc.semaphore("s_" + e)) for e in ENGS}
            dsem = [es.enter_context(nc.semaphore("d_%d" % i)) for i in range(self.n_dma)]
            blk = es.enter_context(nc.Block())

            def replay(name, eng):
                waited = {}

                def do_waits(deps):
                    for d in deps:
                        if d[0] == "e":
                            key = ("e", d[1])
                            val = d[2]
                            sem = esem[d[1]]
                        else:
                            key = ("d", d[1])
                            val = d[2] * 16
                            sem = dsem[d[1]]
                        if waited.get(key, 0) >= val:
                            continue
                        waited[key] = val
                        eng.wait_ge(sem, val)

                for kind, payload, deps in self.streams[name]:
                    do_waits(deps)
                    if kind == "op":
                        payload(eng).then_inc(esem[name], 1)
                    elif kind == "dma":
                        out, in_, i, slow = payload
                        if slow:
                            eng.dma_start(out=out, in_=in_, allow_slow_non_contiguous=True).then_inc(dsem[i], 16)
                        else:
                            eng.dma_start(out=out, in_=in_).then_inc(dsem[i], 16)

            final = list(self.all_dma)

            @blk.tensor
            def _(e):
                replay("tensor", e)

            @blk.vector
            def _(e):
                replay("vector", e)

            @blk.scalar
            def _(e):
                replay("scalar", e)

            @blk.gpsimd
            def _(e):
                replay("gpsimd", e)

            @blk.sync
            def _(e):
                replay("sync", e)
                for i in range(self.n_dma):
                    if self.dma_cnt[i] > 0:
                        e.wait_ge(dsem[i], self.dma_cnt[i] * 16)
                for en in ENGS:
                    if en != "sync" and self.count[en] > 0:
                        e.wait_ge(esem[en], self.count[en])


def _run(nc, in_maps):
    res = run_bass_kernel_spmd(nc, in_maps, core_ids=list(range(len(in_maps))))
    return res.results


def emit_rmsnorm_T(P, xt_ap, xt_tok, scr, ss, xn, pst, identb, wn, dst_fn, eps_t, n_feat=1024, defer=False):
    t1 = P.op("scalar", lambda e: e.activation(out=scr, in_=xt_ap, func=AF.Square, accum_out=ss[:, 0:1]), [xt_tok])
    t2 = P.op("scalar", lambda e: e.activation(out=ss[:, 1:2], in_=ss[:, 0:1], func=AF.Ln, bias=eps_t[:, 0:1], scale=1.0 / n_feat), [t1])
    t3 = P.op("scalar", lambda e: e.activation(out=ss[:, 2:3], in_=ss[:, 1:2], func=AF.Exp, scale=-0.5), [t2])
    t4 = P.op("vector", lambda e: e.tensor_scalar(out=xn, in0=xt_ap, scalar1=ss[:, 2:3], scalar2=None, op0=ALU.mult), [t3, xt_tok])

    def finish(extra_pe_deps=(), extra_ev_deps=()):
        toks = []
        tp = []
        for c in range(n_feat // 128):
            tp.append(P.op("tensor", lambda e, c=c: e.transpose(pst[:, c * 128:(c + 1) * 128], xn[:, c * 128:(c + 1) * 128], identb),
                           [t4] + (list(extra_pe_deps) if c == 0 else [])))
        for c in range(n_feat // 128):
            toks.append(P.op("vector", lambda e, c=c: e.tensor_scalar(out=dst_fn(c), in0=pst[:, c * 128:(c + 1) * 128], scalar1=wn[:, c:c + 1], scalar2=None, op0=ALU.mult),
                             [tp[-1]] + (list(extra_ev_deps) if c == 0 else [])))
        return toks

    if defer:
        return finish, t4
    return finish(), t4


def da_slopes():
    return [2.0 ** (-2 * (h + 1)) for h in range(4)]


def sw_slopes():
    return [2.0 ** (-(h + 1)) for h in range(8)]


def nkb(s):
    return 16 + 4 * (s + 1)


def make_tables():
    jj = np.arange(128, dtype=np.float64)[:, None]
    ii = np.arange(512, dtype=np.float64)[None, :]
    dg = np.zeros((128, 4, 512), np.float32)
    dd = np.zeros((128, 4, 4, 512), np.float32)
    for h, sl in enumerate(da_slopes()):
        dg[:, h, :] = np.exp(sl * (jj - 127 - ii))
        for d in range(4):
            rel = 128 * d + jj - ii
            dd[:, h, d, :] = np.where(rel <= 0, np.exp(sl * np.minimum(rel, 0)), 0.0)
    i2 = np.arange(128, dtype=np.float64)[None, :]
    dsw = np.zeros((128, 2, 2, 2, 2, 128), np.float32)
    for g in range(2):
        for hf in range(2):
            for j in range(2):
                hh = 4 * g + hf + 2 * j
                sl = sw_slopes()[hh]
                dist_p = 128 + i2 - jj
                dsw[:, g, hf, 0, j, :] = np.where(jj > i2, np.exp(-sl * dist_p), 0.0)
                dist_d = i2 - jj
                dsw[:, g, hf, 1, j, :] = np.where(jj <= i2, np.exp(-sl * np.maximum(dist_d, 0)), 0.0)
    return dg.astype(BF), dd.astype(BF), dsw.astype(BF)


def make_btab(half):
    cols = []
    for s in range(NS):
        for h, sl in enumerate(da_slopes()):
            for kb in range(nkb(s)):
                if kb < 16:
                    if half == 0:
                        v = NEG
                    else:
                        v = sl * (128 * kb - 2048 - 512 * s + 127)
                else:
                    m = kb - 16
                    if m < 4 * s:
                        v = sl * (128 * m - 512 * s + 127)
                    else:
                        v = 0.0
                cols.append(v)
    cols.append(NEG if half == 0 else 0.0)
    t = np.tile(np.asarray(cols, np.float32)[None, :], (128, 1))
    return np.ascontiguousarray(t)


TF = 4096
NTF = TF // 128
NSF = TF // 512


def nkbf(s):
    return 4 * (s + 1)


DEAD = -100.0


def kb_first(s, h):
    sl = da_slopes()[h]
    m = 0
    while m < 4 * s and sl * (128 * m - 512 * s + 127) < DEAD:
        m += 1
    return m


NBTF = sum(nkbf(s) - kb_first(s, h) for s in range(NSF) for h in range(4)) + 1


def make_btab_full():
    cols = []
    for s in range(NSF):
        for h, sl in enumerate(da_slopes()):
            for m in range(kb_first(s, h), nkbf(s)):
                cols.append(sl * (128 * m - 512 * s + 127) if m < 4 * s else 0.0)
    cols.append(NEG)
    return np.ascontiguousarray(np.tile(np.asarray(cols, np.float32)[None, :], (128, 1)))


class Arena:
    def __init__(self, base_ap, nbytes):
        self.base = base_ap
        self.nbytes = nbytes
        self.off = 0
        self.mark = 0

    def alloc(self, shape, dt):
        assert shape[0] == 128
        esz = 4 if dt == F32 else 2
        n = 1
        for v in shape[1:]:
            n *= v
        nb = (n * esz + 63) // 64 * 64
        assert self.off + nb <= self.nbytes, ("arena overflow", self.off, nb, self.nbytes)
        ap = self.base[:, self.off // 4:(self.off + nb) // 4]
        self.off += nb
        if dt != F32:
            ap = ap.bitcast(dt)
        ap = ap[:, 0:n]
        if len(shape) == 3:
            ap = ap.rearrange("p (a b) -> p a b", b=shape[2])
        elif len(shape) == 4:
            ap = ap.rearrange("p (a b c) -> p a b c", b=shape[2], c=shape[3])
        return ap

    def set_mark(self):
        self.mark = self.off

    def reset(self):
        self.off = self.mark


def prog_barrier(P, bar):
    deps = [("e", e, P.count[e]) for e in ENGS if P.count[e] > 0] + [("d", i, P.dma_cnt[i]) for i in range(P.n_dma) if P.dma_cnt[i] > 0]
    tok = P.dma("sync", bar[1:2, :], P.bar_src, deps)
    for e in ("tensor", "vector", "scalar", "gpsimd"):
        P.wait_only(e, [tok])
    return tok


FM_CHUNKS = [(j, "q") for j in range(0, 4)] + [(j, "k") for j in range(4, 8)] + \
            [(j, "q") for j in range(12, 16)] + [(16, "k")] + [(j, "g") for j in range(18, 34)]


def build_fused(depth=DEPTH, debug_out=None):
    nc = bass.Bass("TRN2", target_bir_lowering=False)
    dt_in = lambda name, shape, dt: nc.dram_tensor(name, shape, dt, kind="ExternalInput").ap()
    dt_sc = lambda name, shape, dt: nc.dram_tensor(name, shape, dt, kind="Internal").ap()
    x_ext = dt_in("x", [TF, D], F32)
    w_in_d = dt_in("w_in", [depth, D, INC], F32)
    nw1_d = dt_in("nw1", [depth, D], F32)
    lam4_d = dt_in("lam4", [depth, 4, 64], F32)
    subln_d = dt_in("subln", [depth, 128], F32)
    sinks_d = dt_in("sinks", [depth, 8], F32)
    wbda_d = dt_in("w_br_da", [depth, 512, D], F32)
    wbsw_d = dt_in("w_br_sw", [depth, 512, D], F32)
    wmix_d = dt_in("w_mix", [depth, D, D], F32)
    nw2_d = dt_in("nw2", [depth, D], F32)
    wup_d = dt_in("w_up", [depth, D, 2 * DFF], F32)
    cw_d = dt_in("conv_w", [depth, 3, 2 * DFF], F32)
    cb_d = dt_in("conv_b", [depth, 2 * DFF], F32)
    wdn_d = dt_in("w_down", [depth, DFF, D], F32)
    fnw_d = dt_in("fnw", [D], F32)
    ident_d = dt_in("ident", [128, 128], F32)
    dg_d = dt_in("dg", [128, 4, 512], BF16)
    dd_d = dt_in("dd", [128, 4, 4, 512], BF16)
    dsw_d = dt_in("dsw", [128, 16, 128], BF16)
    btab_d = dt_in("btab", [128, NBTF], F32)
    out_d = nc.dram_tensor("out", [TF, D], F32, kind="ExternalOutput").ap()
    xA = dt_sc("xA", [TF, D], F32)
    xB = dt_sc("xB", [TF, D], F32)
    fm_d = dt_sc("fm", [29 * 128, TF], BF16)
    vda_d = dt_sc("vda", [TF, 516], BF16)
    vsw_d = dt_sc("vsw", [TF, 130], BF16)
    yda_d = dt_sc("yda", [512, TF], BF16)
    ysw_d = dt_sc("ysw", [512, TF], BF16)
    xn2_d = dt_sc("xn2", [D, TF], BF16)
    bar = dt_sc("bar", [2, 16], F32)
    ARENA_BYTES = 206 * 1024
    import contextlib
    with contextlib.ExitStack() as es:
        arena_t = es.enter_context(nc.sbuf_tensor("arena", [128, ARENA_BYTES // 4], F32))
        bank = [es.enter_context(nc.psum_tensor("bank%d" % i, [128, 512], F32)) for i in range(8)]
        bankf = [b[:] for b in bank]
        bankb = [b[:].bitcast(BF16) for b in bank]
        A = Arena(arena_t[:], ARENA_BYTES)
        P = Prog(nc)
        identb = A.alloc([128, 128], BF16)
        eps_t = A.alloc([128, 16], F32)
        A.set_mark()
        P.bar_src = ident_d[0:1, 0:16]
        c_id = P.dma("gpsimd", identb, ident_d)
        c_eps = P.op("vector", lambda e: e.memset(eps_t[:, 0:1], EPS))
        P.wait_only("tensor", [c_id])
        P.wait_only("scalar", [c_eps])
        P.wait_only("gpsimd", [c_eps])

        def stage_proj(l, x_in):
            A.reset()
            NT_, NS_ = NTF, NSF
            wb = A.alloc([128, 8, INC], BF16)
            xnT = A.alloc([128, 8, TF], BF16)
            xt = [A.alloc([128, D], F32) for i in range(2)]
            scr = A.alloc([128, D], F32)
            xn = [A.alloc([128, D], BF16) for i in range(4)]
            ss = [A.alloc([128, 4], F32) for i in range(4)]
            wn = A.alloc([128, 8], F32)
            stg = [A.alloc([128, 512], BF16) for i in range(4)]
            vst = [A.alloc([128, 4 * 129], BF16) for i in range(2)]
            vsst = [A.alloc([128, 2 * 65], BF16) for i in range(2)]
            ps = bankf[0:6]
            pst = bankb[6:8]
            t_wn = P.dma("sync", wn, nw1_d[l].rearrange("(c p) -> p c", p=128), slow=True)
            t_ones = [P.op("vector", lambda e, i=i: e.memset(vst[i], 1.0)) for i in range(2)]
            t_ones2 = [P.op("vector", lambda e, i=i: e.memset(vsst[i], 1.0)) for i in range(2)]
            wtok = {}
            for q in range(4):
                for k in range(8):
                    c0 = q * 1088
                    wtok[(k, q)] = P.dma("gpsimd", wb[:, k, c0:c0 + 1088], w_in_d[l, k * 128:(k + 1) * 128, c0:c0 + 1088])

            def wdeps(c0, c1):
                qs = sorted(set([c0 // 1088, (c1 - 1) // 1088]))
                return [wtok[(k, q)] for k in range(8) for q in qs]
            P.wait_only("vector", [t_wn])
            xnT_tok = [None] * NT_
            st1 = {"xt_free": [None, None], "xn_free": [None, None], "pst_free": [None, None], "psi": 0, "si": 0,
                   "ps_free": [None] * 6, "stg_free": [None] * 4, "v_free": [None, None], "vs_free": [None, None]}

            st1["xn_free"] = [None] * 4
            st1["fins"] = {}

            def norm1_slot(s):
                for tt in range(4 * s, 4 * s + 4):
                    b = tt % 2
                    q = tt % 4
                    ld = P.dma("sync", xt[b], x_in[tt * 128:(tt + 1) * 128, :], [st1["xt_free"][b]])
                    P.wait_only("scalar", [st1["xn_free"][q]])
                    fin, t4 = emit_rmsnorm_T(P, xt[b], ld, scr, ss[q], xn[q], pst[b], identb, wn,
                                             lambda c, tt=tt: xnT[:, c, tt * 128:(tt + 1) * 128], eps_t, defer=True)
                    st1["fins"][tt] = fin
                    st1["xt_free"][b] = t4

            def finish_slot(s):
                for tt in range(4 * s, 4 * s + 4):
                    b = tt % 2
                    q = tt % 4
                    toks = st1["fins"].pop(tt)([st1["pst_free"][b]])
                    xnT_tok[tt] = toks
                    st1["xn_free"][q] = toks[-1]
                    st1["pst_free"][b] = toks[-1]

            def proj_slot(s):
                for ci, (j, kind) in enumerate(FM_CHUNKS):
                    if ci == 14 and s + 1 < NS_:
                        finish_slot(s + 1)
                    pb = st1["psi"] % 6
                    st1["psi"] += 1
                    deps = wdeps(j * 128, (j + 1) * 128) + [st1["ps_free"][pb]]
                    for tt in range(s * 4, s * 4 + 4):
                        deps += xnT_tok[tt]
                    mm = None
                    for k in range(8):
                        mm = P.op("tensor", lambda e, k=k, pb=pb, j=j, s=s: e.matmul(
                            ps[pb], lhsT=wb[:, k, j * 128:(j + 1) * 128], rhs=xnT[:, k, s * 512:(s + 1) * 512],
                            start=(k == 0), stop=(k == 7)), deps if k == 0 else [])
                    sb_i = st1["si"] % 4
                    st1["si"] += 1
                    if kind == "q":
                        ev = P.op("scalar", lambda e, pb=pb, sb_i=sb_i: e.activation(out=stg[sb_i], in_=ps[pb], func=AF.Copy, scale=0.125), [mm, st1["stg_free"][sb_i]])
                    elif kind == "k":
                        ev = P.op("vector", lambda e, pb=pb, sb_i=sb_i: e.tensor_copy(out=stg[sb_i], in_=ps[pb]), [mm, st1["stg_free"][sb_i]])
                    else:
                        ev = P.op("scalar", lambda e, pb=pb, sb_i=sb_i: e.activation(out=stg[sb_i], in_=ps[pb], func=AF.Sigmoid), [mm, st1["stg_free"][sb_i]])
                    st1["ps_free"][pb] = ev
                    st1["stg_free"][sb_i] = P.dma("sync", fm_d[ci * 128:(ci + 1) * 128, s * 512:(s + 1) * 512], stg[sb_i], [ev])
                for tt in range(4 * s, 4 * s + 4):
                    b = tt % 2
                    pb = st1["psi"] % 6
                    st1["psi"] += 1
                    deps = wdeps(1024, 1536) + [st1["ps_free"][pb]] + xnT_tok[tt]
                    mm = None
                    for k in range(8):
                        mm = P.op("tensor", lambda e, k=k, pb=pb, tt=tt: e.matmul(
                            ps[pb], lhsT=xnT[:, k, tt * 128:(tt + 1) * 128], rhs=wb[:, k, 1024:1536],
                            start=(k == 0), stop=(k == 7)), deps if k == 0 else [])
                    ev = P.op("vector", lambda e, pb=pb, b=b: e.tensor_copy(
                        out=vst[b].rearrange("p (h e) -> p h e", e=129)[:, :, 0:128],
                        in_=ps[pb].rearrange("p (h e) -> p h e", e=128)), [mm, st1["v_free"][b], t_ones[b]])
                    st1["ps_free"][pb] = ev
                    st1["v_free"][b] = P.dma("sync", vda_d[tt * 128:(tt + 1) * 128, :], vst[b], [ev])
                    pb = st1["psi"] % 6
                    st1["psi"] += 1
                    deps = wdeps(2176, 2304) + [st1["ps_free"][pb]] + xnT_tok[tt]
                    for k in range(8):
                        mm = P.op("tensor", lambda e, k=k, pb=pb, tt=tt: e.matmul(
                            ps[pb][:, 0:128], lhsT=xnT[:, k, tt * 128:(tt + 1) * 128], rhs=wb[:, k, 2176:2304],
                            start=(k == 0), stop=(k == 7)), deps if k == 0 else [])
                    ev = P.op("vector", lambda e, pb=pb, b=b: e.tensor_copy(
                        out=vsst[b].rearrange("p (h e) -> p h e", e=65)[:, :, 0:64],
                        in_=ps[pb][:, 0:128].rearrange("p (h e) -> p h e", e=64)), [mm, st1["vs_free"][b], t_ones2[b]])
                    st1["ps_free"][pb] = ev
                    st1["vs_free"][b] = P.dma("sync", vsw_d[tt * 128:(tt + 1) * 128, :], vsst[b], [ev])

            norm1_slot(0)
            finish_slot(0)
            for s in range(NS_):
                if s + 1 < NS_:
                    norm1_slot(s + 1)
                proj_slot(s)
            prog_barrier(P, bar)

        def stage_attn(l, lam_init):
            A.reset()
            NS_ = NSF
            dg = A.alloc([128, 4, 512], BF16)
            dd = A.alloc([128, 4, 4, 512], BF16)
            dsw = A.alloc([128, 16, 128], BF16)
            btab = A.alloc([128, NBTF], F32)
            c_ld = [P.dma("sync", dg, dg_d), P.dma("sync", dd, dd_d), P.dma("sync", dsw, dsw_d), P.dma("sync", btab, btab_d)]
            P.wait_only("vector", c_ld)
            P.wait_only("scalar", c_ld)
            kda = A.alloc([128, 4, TF], BF16)
            vda = A.alloc([128, 32, 516], BF16)
            ksw = A.alloc([128, 2, TF], BF16)
            vsw = A.alloc([128, 32, 130], BF16)
            qs = [A.alloc([128, 4, 512], BF16) for i in range(2)]
            qsw = [A.alloc([128, 4, 512], BF16) for i in range(2)]
            lamb = A.alloc([128, 4, 64], F32)
            lscr = A.alloc([128, 64], F32)
            lsum = A.alloc([128, 4], F32)
            neglam = A.alloc([128, 1], F32)
            slnw = A.alloc([128, 1], F32)
            esk = A.alloc([128, 8], F32)
            NPB = 6
            pbuf = [A.alloc([128, 512], BF16) for i in range(NPB)]
            accs = [A.alloc([128, 9, 129], F32) for i in range(2)]
            rc = [A.alloc([128, 8], F32) for i in range(2)]
            a_t = [A.alloc([128, 4, 128], F32) for i in range(2)]
            ssq = [A.alloc([128, 12], F32) for i in range(2)]
            junk = A.alloc([128, 8, 128], F32)
            y_t = [A.alloc([128, 4, 128], BF16) for i in range(2)]
            ysg = [A.alloc([128, 512], BF16) for i in range(2)]
            saccs = [A.alloc([128, 8, 65], F32) for i in range(2)]
            sden = [A.alloc([128, 8], F32) for i in range(2)]
            ysw_t = [A.alloc([128, 8, 64], BF16) for i in range(2)]
            yswg = [A.alloc([128, 4, 128], BF16) for i in range(2)]
            NSB = 4
            psS = bankf[0:4]
            psA = bankf[4:7]
            psT = [bankb[7], bankb[7]]
            ld = {}
            ld["lam"] = P.dma("sync", lamb.rearrange("p a b -> p (a b)"), lam4_d[l].rearrange("a b -> (a b)").partition_broadcast(128))
            ld["subln"] = P.dma("sync", slnw, subln_d[l].rearrange("(p o) -> p o", o=1))
            ld["sinks"] = P.dma("sync", esk, sinks_d[l].partition_broadcast(128))
            q_src = fm_d[0:512, :].rearrange("(h p) t -> p h t", p=128)
            qsw_src = fm_d[8 * 128:12 * 128, :].rearrange("(h p) t -> p h t", p=128)
            vda_src = vda_d.rearrange("(n p) e -> p n e", p=128)
            ld["kda"] = [P.dma("sync", kda[:, 0, :], fm_d[4 * 128:5 * 128, :])]
            lq_first = P.dma("sync", qs[0], q_src[:, :, 0:512])
            ld["vda"] = [P.dma("sync", vda[:, 0:8, :], vda_src[:, 0:8, :])]
            ld["kda"] += [P.dma("sync", kda[:, h, :], fm_d[(4 + h) * 128:(5 + h) * 128, :]) for h in range(1, 4)]
            ld["vda"] += [P.dma("sync", vda[:, q * 8:(q + 1) * 8, :], vda_src[:, q * 8:(q + 1) * 8, :]) for q in range(1, 4)]
            ld["ksw"] = [P.dma("sync", ksw[hf * 64:(hf + 1) * 64, g, :], fm_d[12 * 128 + g * 64:12 * 128 + (g + 1) * 64, :]) for hf in range(2) for g in range(2)]
            ld["vsw"] = P.dma("sync", vsw, vsw_d.rearrange("(n p) e -> p n e", p=128))
            tl = []
            for i in range(2):
                tl.append(P.op("vector", lambda e, i=i: e.scalar_tensor_tensor(
                    out=lscr, in0=lamb[:, 2 * i, :], scalar=1.0, in1=lamb[:, 2 * i + 1, :], op0=ALU.mult, op1=ALU.mult,
                    accum_out=lsum[:, i:i + 1]), [ld["lam"]] + tl))
            te = P.op("scalar", lambda e: e.activation(out=lsum[:, 2:4], in_=lsum[:, 0:2], func=AF.Exp), tl)
            tn = P.op("vector", lambda e: e.tensor_tensor(out=neglam, in0=lsum[:, 3:4], in1=lsum[:, 2:3], op=ALU.subtract), [te])
            tn = P.op("vector", lambda e: e.tensor_scalar(out=neglam, in0=neglam, scalar1=-float(lam_init), scalar2=None, op0=ALU.add), [tn])
            tsl = P.op("vector", lambda e: e.tensor_scalar(out=slnw, in0=slnw, scalar1=float(1.0 - lam_init), scalar2=None, op0=ALU.mult), [ld["subln"]])
            tes = P.op("scalar", lambda e: e.activation(out=esk, in_=esk, func=AF.Exp), [ld["sinks"]])
            state = {"it": 0, "s_free": [None] * NSB, "p_free": [None] * NPB, "acc_free": None, "grp": 0,
                     "pst_free": [None], "deferred": None, "ysg_free": [None, None], "q_free": [None, None]}

            def acc_ap(c, r):
                a = c * 4 + r
                return psA[a // 3][:, (a % 3) * 129:(a % 3) * 129 + 129]

            def finalize_A(par, last_pv):
                f1 = []
                for bk in range(3):
                    n = 3 if bk < 2 else 2
                    f1.append(P.op("vector", lambda e, bk=bk, n=n: e.tensor_copy(
                        out=accs[par][:, bk * 3:bk * 3 + n, :].rearrange("p a b -> p (a b)"), in_=psA[bk][:, 0:n * 129]), [last_pv] if bk == 0 else []))
                f2 = P.op("vector", lambda e: e.reciprocal(out=rc[par], in_=accs[par][:, 0:8, 128]), [f1[-1]])
                f3 = P.op("vector", lambda e: e.tensor_scalar(out=rc[par][:, 4:8], in0=rc[par][:, 4:8], scalar1=neglam[:, 0:1], scalar2=None, op0=ALU.mult), [f2, tn])
                f5 = []
                for r in range(4):
                    f4 = P.op("vector", lambda e, r=r: e.tensor_scalar(out=a_t[par][:, r, :], in0=accs[par][:, r, 0:128], scalar1=rc[par][:, r:r + 1], scalar2=None, op0=ALU.mult), [f3])
                    f5.append(P.op("vector", lambda e, r=r: e.scalar_tensor_tensor(
                        out=a_t[par][:, r, :], in0=accs[par][:, 4 + r, 0:128], scalar=rc[par][:, 4 + r:5 + r], in1=a_t[par][:, r, :],
                        op0=ALU.mult, op1=ALU.add), [f4]))
                return f1[-1], f5

            def finalize_B(par, f5, h, s):
                f6 = []
                for r in range(4):
                    f6.append(P.op("scalar", lambda e, r=r: e.activation(out=junk[:, par * 4 + r, :], in_=a_t[par][:, r, :], func=AF.Square, accum_out=ssq[par][:, r:r + 1]), [f5[r]]))
                f7 = P.op("scalar", lambda e: e.activation(out=ssq[par][:, 4:8], in_=ssq[par][:, 0:4], func=AF.Ln, bias=eps_t[:, 0:1], scale=1.0 / 128), [f6[-1]])
                f8 = P.op("scalar", lambda e: e.activation(out=ssq[par][:, 8:12], in_=ssq[par][:, 4:8], func=AF.Exp, scale=-0.5), [f7])
                f9 = []
                for r in range(4):
                    f9.append(P.op("vector", lambda e, r=r: e.tensor_scalar(out=y_t[par][:, r, :], in0=a_t[par][:, r, :], scalar1=ssq[par][:, 8 + r:9 + r], scalar2=None, op0=ALU.mult), [f8]))
                tp = None
                for r in range(4):
                    tp = P.op("tensor", lambda e, r=r: e.transpose(psT[par][:, r * 128:(r + 1) * 128], y_t[par][:, r, :], identb),
                              [f9[r]] + ([state["pst_free"][0]] if r == 0 else []))
                f10 = P.op("vector", lambda e: e.tensor_scalar(out=ysg[par], in0=psT[par][:, 0:512], scalar1=slnw[:, 0:1], scalar2=None, op0=ALU.mult), [tp, tsl, state["ysg_free"][par]])
                state["pst_free"][0] = f10
                state["ysg_free"][par] = P.dma("sync", yda_d[h * 128:(h + 1) * 128, s * 512:(s + 1) * 512], ysg[par], [f10])

            bcol = 0
            for s in range(NS_):
                qb = s % 2
                lq = lq_first if s == 0 else P.dma("sync", qs[qb], q_src[:, :, s * 512:(s + 1) * 512], [state["q_free"][qb]])
                last_qk_slot = None
                for h in range(4):
                    par = state["grp"] % 2
                    kb0 = kb_first(s, h)
                    n_it = (nkbf(s) - kb0) * 2
                    items = [(kb, c) for kb in range(kb0, nkbf(s)) for c in range(2)]
                    mul_tok = {}
                    qk_tok = {}

                    def emit_qk(i, h=h, s=s, qb=qb, lq=lq):
                        kb, c = items[i]
                        g_i = state["it"] + i
                        sbi = g_i % NSB
                        deps = [state["s_free"][sbi]]
                        if i == 0:
                            deps += [lq, ld["kda"][h]]
                        qk_tok[i] = P.op("tensor", lambda e: e.matmul(
                            psS[sbi], lhsT=kda[c * 64:(c + 1) * 64, h, kb * 128:(kb + 1) * 128],
                            rhs=qs[qb][c * 64:(c + 1) * 64, h, :], start=True, stop=True), deps)
                        return qk_tok[i]

                    def emit_soft(i, h=h, s=s, bcol_base=bcol, kb0=kb0):
                        kb, c = items[i]
                        g_i = state["it"] + i
                        sbi = g_i % NSB
                        pbi = g_i % NPB
                        col = bcol_base + kb - kb0
                        ex = P.op("scalar", lambda e: e.activation(out=pbuf[pbi], in_=psS[sbi], func=AF.Exp, bias=btab[:, col:col + 1], scale=1.0),
                                  [qk_tok[i], state["p_free"][pbi]])
                        state["s_free"][sbi] = ex
                        if kb >= 4 * s:
                            dtile = dd[:, h, kb - 4 * s, :]
                        else:
                            dtile = dg[:, h, :]
                        mul_tok[i] = P.op("vector", lambda e: e.tensor_tensor(out=pbuf[pbi], in0=pbuf[pbi], in1=dtile, op=ALU.mult), [ex])

                    def emit_pv(i, h=h, s=s, kb0=kb0):
                        kb, c = items[i]
                        g_i = state["it"] + i
                        pbi = g_i % NPB
                        pv = None
                        for r in range(4):
                            deps = [mul_tok[i]] if r == 0 else []
                            if i < 2 and r == 0:
                                deps += [state["acc_free"]]
                            if r == 0:
                                deps += [ld["vda"][kb // 8]]
                            first = (kb == kb0) and ((c * 4 + r) % 3 == 0)
                            pv = P.op("tensor", lambda e, r=r, first=first: e.matmul(
                                acc_ap(c, r), lhsT=pbuf[pbi][:, r * 128:(r + 1) * 128], rhs=vda[:, kb, h * 129:(h + 1) * 129],
                                start=first, stop=False, skip_group_check=True), deps)
                        state["p_free"][pbi] = pv
                        return pv

                    LOOK = 2
                    for i in range(min(LOOK, n_it)):
                        last_qk_slot = emit_qk(i)
                    last_pv = None
                    for i in range(n_it):
                        if i % 2 == 0:
                            for j in (i + LOOK, i + LOOK + 1):
                                if j < n_it:
                                    last_qk_slot = emit_qk(j)
                        emit_soft(i)
                        last_pv = emit_pv(i)
                        if i == 5 and state["deferred"] is not None:
                            finalize_B(*state["deferred"])
                            state["deferred"] = None
                    if state["deferred"] is not None:
                        finalize_B(*state["deferred"])
                        state["deferred"] = None
                    accf, f5 = finalize_A(par, last_pv)
                    state["acc_free"] = accf
                    state["deferred"] = (par, f5, h, s)
                    state["it"] += n_it
                    state["grp"] += 1
                    bcol += nkbf(s) - kb0
                state["q_free"][qb] = last_qk_slot
            if state["deferred"] is not None:
                finalize_B(*state["deferred"])
                state["deferred"] = None
            steps = [(n, g, hf) for n in range(4 * NS_) for g in range(2) for hf in range(2)]
            sw = {"accsw_free": state["acc_free"], "qsw_free": [None, None], "yswg_free": [None, None], "lqs": {}, "qk": {}, "mu": {}, "last_pv": None, "pendB": None, "ysw_t_free": [None, None]}
            it0 = state["it"]

            def sw_qk(idx):
                n, g, hf = steps[idx]
                sbi = (it0 + idx) % NSB
                s = n // 4
                qb = s % 2
                nl = n % 4
                if nl == 0 and g == 0 and hf == 0:
                    sw["lqs"][s] = P.dma("sync", qsw[qb], qsw_src[:, :, s * 512:(s + 1) * 512], [sw["qsw_free"][qb]])
                deps = [state["s_free"][sbi], sw["lqs"][s]]
                if idx == 0:
                    deps += ld["ksw"] + [ld["vsw"]]
                t = None
                for kind in range(2):
                    if n == 0 and kind == 0:
                        continue
                    kblk = n - 1 + kind
                    t = P.op("tensor", lambda e, g=g, hf=hf, kblk=kblk, sbi=sbi, nl=nl, qb=qb, kind=kind: e.matmul(
                        psS[sbi][:, kind * 256:(kind + 1) * 256], lhsT=ksw[hf * 64:(hf + 1) * 64, g, kblk * 128:(kblk + 1) * 128],
                        rhs=qsw[qb][hf * 64:(hf + 1) * 64, 2 * g:2 * g + 2, nl * 128:(nl + 1) * 128], start=True, stop=True), deps if t is None else [])
                sw["qk"][idx] = t
                if nl == 3 and g == 1 and hf == 1:
                    sw["qsw_free"][qb] = t

            def sw_soft(idx):
                n, g, hf = steps[idx]
                sbi = (it0 + idx) % NSB
                pbi = (it0 + idx) % NPB
                c0 = 256 if n == 0 else 0
                ex = P.op("scalar", lambda e: e.activation(out=pbuf[pbi][:, c0:512], in_=psS[sbi][:, c0:512], func=AF.Exp), [sw["qk"][idx], state["p_free"][pbi]])
                state["s_free"][sbi] = ex
                base = (g * 2 + hf) * 4
                dt_ap = dsw[:, base:base + 4, :].rearrange("p a b -> p (a b)")
                sw["mu"][idx] = P.op("vector", lambda e: e.tensor_tensor(out=pbuf[pbi][:, c0:512], in0=pbuf[pbi][:, c0:512], in1=dt_ap[:, c0:512], op=ALU.mult), [ex])

            def sw_pv(idx):
                n, g, hf = steps[idx]
                pbi = (it0 + idx) % NPB
                pv = None
                for kind in range(2):
                    if n == 0 and kind == 0:
                        continue
                    kblk = n - 1 + kind
                    for j in range(2):
                        gi = hf + 2 * j
                        first = (hf == 0 and pv is None)
                        deps = []
                        if pv is None:
                            deps = [sw["mu"][idx]]
                            if hf == 0:
                                deps += [sw["accsw_free"]]
                        pv = P.op("tensor", lambda e, g=g, gi=gi, j=j, kblk=kblk, kind=kind, first=first: e.matmul(
                            psA[g][:, gi * 65:(gi + 1) * 65], lhsT=pbuf[pbi][:, kind * 256 + j * 128:kind * 256 + (j + 1) * 128], rhs=vsw[:, kblk, g * 65:(g + 1) * 65],
                            start=first, stop=False, skip_group_check=True), deps)
                state["p_free"][pbi] = pv
                sw["last_pv"] = pv

            def sw_final(n):
                par = n % 2
                f1 = None
                for g in range(2):
                    f1 = P.op("vector", lambda e, g=g: e.tensor_copy(out=saccs[par][:, g * 4:(g + 1) * 4, :].rearrange("p a b -> p (a b)"), in_=psA[g][:, 0:260]), [sw["last_pv"]] if g == 0 else [])
                sw["accsw_free"] = f1
                f2 = P.op("vector", lambda e: e.tensor_tensor(out=sden[par], in0=saccs[par][:, :, 64], in1=esk, op=ALU.add), [f1, tes])
                f3 = P.op("vector", lambda e: e.reciprocal(out=sden[par], in_=sden[par]), [f2])
                f4 = None
                for hh in range(8):
                    f4 = P.op("vector", lambda e, hh=hh: e.tensor_scalar(out=ysw_t[par][:, hh, :], in0=saccs[par][:, hh, 0:64], scalar1=sden[par][:, hh:hh + 1], scalar2=None, op0=ALU.mult),
                              [f3] + ([sw["ysw_t_free"][par]] if hh == 0 else []))
                def partB():
                    tp = None
                    for cc in range(4):
                        tp = P.op("tensor", lambda e, cc=cc: e.transpose(psT[par][:, cc * 128:(cc + 1) * 128], ysw_t[par][:, 2 * cc:2 * cc + 2, :].rearrange("p a b -> p (a b)"), identb),
                                  [f4] + ([state["pst_free"][0]] if cc == 0 else []))
                    f5 = P.op("vector", lambda e: e.tensor_copy(out=yswg[par], in_=psT[par][:, 0:512].rearrange("p (a b) -> p a b", b=128)), [tp, sw["yswg_free"][par]])
                    state["pst_free"][0] = f5
                    sw["ysw_t_free"][par] = f5
                    sw["yswg_free"][par] = P.dma("sync", ysw_d.rearrange("(c p) t -> p c t", p=128)[:, :, n * 128:(n + 1) * 128], yswg[par], [f5])
                sw["pendB"] = partB

            sw_qk(0)
            sw_qk(1)
            for idx in range(len(steps)):
                if idx + 2 < len(steps):
                    sw_qk(idx + 2)
                sw_soft(idx)
                sw_pv(idx)
                n, g, hf = steps[idx]
                if g == 0 and hf == 0 and sw["pendB"] is not None:
                    sw["pendB"]()
                    sw["pendB"] = None
                if g == 1 and hf == 1:
                    sw_final(n)
            if sw["pendB"] is not None:
                sw["pendB"]()
                sw["pendB"] = None
            state["it"] += len(steps)
            prog_barrier(P, bar)

        def stage_mix(l, x_in, x_mid):
            A.reset()
            NS_ = NSF
            wbda = A.alloc([128, 4, D], BF16)
            wbsw = A.alloc([128, 4, D], BF16)
            wmix = A.alloc([128, 8, D], BF16)
            yds = [A.alloc([128, 4, 512], BF16) for i in range(2)]
            yss = [A.alloc([128, 4, 512], BF16) for i in range(2)]
            sg = [A.alloc([128, 16, 512], BF16) for i in range(2)]
            mg = [A.alloc([128, 8, 512], BF16) for i in range(2)]
            m1 = [A.alloc([128, 512], F32) for i in range(2)]
            m2 = [A.alloc([128, 512], F32) for i in range(2)]
            xt = [A.alloc([128, D], F32) for i in range(2)]
            xm = [A.alloc([128, D], F32) for i in range(2)]
            scr = A.alloc([128, D], F32)
            xn = [A.alloc([128, D], BF16) for i in range(2)]
            ss = [A.alloc([128, 4], F32) for i in range(2)]
            xnT = [A.alloc([128, 8, 128], BF16) for i in range(2)]
            wn = A.alloc([128, 8], F32)
            psA = bankf[0:2]
            psB = bankf[2:4]
            psM = bankf[4:6]
            pst = bankb[6:8]
            t_wn = P.dma("sync", wn, nw2_d[l].rearrange("(c p) -> p c", p=128), slow=True)
            lw = []
            for k in range(4):
                lw.append(P.dma("gpsimd", wbda[:, k, :], wbda_d[l, k * 128:(k + 1) * 128, :]))
                lw.append(P.dma("gpsimd", wbsw[:, k, :], wbsw_d[l, k * 128:(k + 1) * 128, :]))
            lm = [P.dma("gpsimd", wmix[:, k, :], wmix_d[l, k * 128:(k + 1) * 128, :]) for k in range(8)]
            P.wait_only("tensor", lw)
            P.wait_only("vector", [t_wn])
            sg_src = fm_d[13 * 128:29 * 128, :].rearrange("(c p) t -> p c t", p=128)
            yd_src = yda_d.rearrange("(c p) t -> p c t", p=128)
            ys_src = ysw_d.rearrange("(c p) t -> p c t", p=128)
            xn_dst = xn2_d.rearrange("(c p) t -> p c t", p=128)
            sg_free = [None, None]
            y_free = [None, None]
            mg_free = [None, None]
            a_free = [None, None]
            b_free = [None, None]
            m_free = [None, None]
            m1_free = [None, None]
            xt_free = [None, None]
            xm_free = [None, None]
            xn_free = [None, None]
            pst_free = [None, None]
            xnT_free = [None, None]
            mst = {"cnt": 0, "mcnt": 0, "mg_tok": {}, "tb": {}}

            def gate_slot(s, js=range(8)):
                sp = s % 2
                if 0 in js:
                    mst["lsg"] = P.dma("sync", sg[sp], sg_src[:, :, s * 512:(s + 1) * 512], [sg_free[sp]])
                    mst["lyd"] = P.dma("sync", yds[sp], yd_src[:, :, s * 512:(s + 1) * 512], [y_free[sp]])
                    mst["lys"] = P.dma("sync", yss[sp], ys_src[:, :, s * 512:(s + 1) * 512], [y_free[sp]])
                    mst["mg_tok"][s] = []
                lsg, lyd, lys = mst["lsg"], mst["lyd"], mst["lys"]
                mg_tok = mst["mg_tok"][s]
                tb = None
                for j in js:
                    pb = mst["cnt"] % 2
                    mst["cnt"] += 1
                    ta = tb = None
                    for k in range(4):
                        ta = P.op("tensor", lambda e, k=k, pb=pb, j=j, sp=sp: e.matmul(psA[pb], lhsT=wbda[:, k, j * 128:(j + 1) * 128], rhs=yds[sp][:, k, :], start=(k == 0), stop=(k == 3)),
                                  [a_free[pb], lyd] if k == 0 else [])
                    for k in range(4):
                        tb = P.op("tensor", lambda e, k=k, pb=pb, j=j, sp=sp: e.matmul(psB[pb], lhsT=wbsw[:, k, j * 128:(j + 1) * 128], rhs=yss[sp][:, k, :], start=(k == 0), stop=(k == 3)),
                                  [b_free[pb], lys] if k == 0 else [])
                    d1 = P.op("vector", lambda e, pb=pb, j=j, sp=sp: e.tensor_tensor(out=m1[pb], in0=psA[pb], in1=sg[sp][:, j, :], op=ALU.mult), [ta, lsg, m1_free[pb]])
                    a_free[pb] = d1
                    d2 = P.op("vector", lambda e, pb=pb, j=j, sp=sp: e.tensor_tensor(out=m2[pb], in0=psB[pb], in1=sg[sp][:, 8 + j, :], op=ALU.mult), [tb, lsg])
                    b_free[pb] = d2
                    d3 = P.op("vector", lambda e, pb=pb, j=j, sp=sp: e.tensor_tensor(out=mg[sp][:, j, :], in0=m1[pb], in1=m2[pb], op=ALU.add), [d1, d2] + ([mg_free[sp]] if j == 0 else []))
                    m1_free[pb] = d3
                    mg_tok.append(d3)
                if 7 in js:
                    sg_free[sp] = mg_tok[-1]
                    y_free[sp] = tb

            def mix_slot(s, tts=range(4)):
                sp = s % 2
                mg_tok = mst["mg_tok"][s]
                last_mm = None
                for tt4 in tts:
                    tt = s * 4 + tt4
                    xb = tt % 2
                    lx = P.dma("sync", xt[xb], x_in[tt * 128:(tt + 1) * 128, :], [xt_free[xb]])
                    adds = []
                    for half in range(2):
                        pb = mst["mcnt"] % 2
                        mst["mcnt"] += 1
                        mm = None
                        for k in range(8):
                            mm = P.op("tensor", lambda e, k=k, pb=pb, tt4=tt4, half=half, sp=sp: e.matmul(
                                psM[pb], lhsT=mg[sp][:, k, tt4 * 128:(tt4 + 1) * 128], rhs=wmix[:, k, half * 512:(half + 1) * 512], start=(k == 0), stop=(k == 7)),
                                ([m_free[pb]] + mg_tok + lm) if k == 0 else [])
                        last_mm = mm
                        ad = P.op("vector", lambda e, pb=pb, xb=xb, half=half: e.tensor_tensor(out=xm[xb][:, half * 512:(half + 1) * 512], in0=psM[pb], in1=xt[xb][:, half * 512:(half + 1) * 512], op=ALU.add),
                                  [mm, lx, xm_free[xb]])
                        m_free[pb] = ad
                        adds.append(ad)
                    xt_free[xb] = adds[-1]
                    st = P.dma("sync", x_mid[tt * 128:(tt + 1) * 128, :], xm[xb], adds)
                    flush_pending()
                    P.wait_only("scalar", [xn_free[xb]])
                    fin, t4 = emit_rmsnorm_T(P, xm[xb], adds[-1], scr, ss[xb], xn[xb], pst[xb], identb, wn,
                                             lambda c, xb=xb: xnT[xb][:, c, :], eps_t, defer=True)
                    xm_free[xb] = st
                    P.wait_only("vector", [t4])
                    mst["pending"] = (fin, xb, tt)
                if 3 in tts:
                    mg_free[sp] = last_mm

            def flush_pending():
                if mst.get("pending") is None:
                    return
                fin, xb, tt = mst["pending"]
                mst["pending"] = None
                toks = fin([pst_free[xb]], [xnT_free[xb]])
                xn_free[xb] = toks[-1]
                pst_free[xb] = toks[-1]
                xnT_free[xb] = P.dma("sync", xn_dst[:, :, tt * 128:(tt + 1) * 128], xnT[xb], toks)

            gate_slot(0)
            for s in range(NS_):
                for q in range(4):
                    if s + 1 < NS_:
                        gate_slot(s + 1, range(2 * q, 2 * q + 2))
                    mix_slot(s, range(q, q + 1))
            flush_pending()
            prog_barrier(P, bar)

        def stage_ffn(l, half_i, x_mid, x_out, final):
            A.reset()
            TH = 2048
            NTH = 16
            NSH = 4
            t0 = half_i * TH
            big = A.alloc([128, 8 * (TH + 2)], BF16)
            xnx = big.rearrange("p (k t) -> p k t", t=TH + 2)
            wd1 = big[:, 0:22 * 512].rearrange("p (i c) -> p i c", c=512)
            wd0 = A.alloc([128, 22, 512], BF16)
            hT = A.alloc([128, 22, TH], BF16)
            wug = [A.alloc([128, 8, 256], BF16) for i in range(2)]
            wuv = [A.alloc([128, 8, 256], BF16) for i in range(2)]
            wk = A.alloc([128, 6, 512], F32)
            accg = [wk[:, i, :] for i in range(2)]
            accv = [wk[:, 2 + i, :] for i in range(2)]
            sgt = [wk[:, 4 + i, :] for i in range(2)]
            uc = {(w, p): A.alloc([128, 514], F32) for w in range(2) for p in range(2)}
            cw = A.alloc([128, 3, 44], F32)
            cb = A.alloc([128, 44], F32)
            xt = [A.alloc([128, 512], F32) for i in range(3)]
            xo = [A.alloc([128, D], F32) for i in range(2)]
            fw = A.alloc([128, D], F32) if final else None
            ss = [A.alloc([128, 4], F32) for i in range(2)]
            scr = wk[:, 0:2, :].rearrange("p a b -> p (a b)")
            psU = [bankf[0], bankf[1], bankf[2], bankf[3], bankf[5], bankf[6]]
            NU = 6
            psH = bankf[4]
            psD = bankf[5:8]
            if half_i == 0:
                lx0 = P.op("gpsimd", lambda e: e.memset(xnx[:, :, 0:2], 0.0))
                lx = [P.dma("sync", xnx[:, k, 2:TH + 2], xn2_d[k * 128:(k + 1) * 128, 0:TH]) for k in range(8)] + [lx0]
            else:
                lx = [P.dma("sync", xnx[:, k, :], xn2_d[k * 128:(k + 1) * 128, t0 - 2:t0 + TH]) for k in range(8)]
            lcw = [P.dma("sync", cw[:, j, :], cw_d[l, j].rearrange("(c p) -> p c", p=128), slow=True) for j in range(3)]
            lcb = P.dma("sync", cb, cb_d[l].rearrange("(c p) -> p c", p=128), slow=True)
            lfw = P.dma("sync", fw, fnw_d.partition_broadcast(128)) if final else None
            P.wait_only("vector", lcw + [lcb])
            P.wait_only("scalar", lcw + [lcb])
            lwd0 = []
            wu_free = [None, None]
            u_free = [None] * NU
            h_free = None
            pend = {"f": None}
            acc_free = {0: [None, None], 1: [None, None]}
            sgt_free = [None, None]
            uc_free = {k: [] for k in uc}
            ucnt = 0
            hT_tok = []
            last_up_mm = None
            for gi in range(11):
                i0, n = 2 * gi, 2
                wb = gi % 2
                lg = [P.dma("gpsimd", wug[wb][:, k, :], wup_d[l, k * 128:(k + 1) * 128, i0 * 128:(i0 + n) * 128], [wu_free[wb]]) for k in range(8)]
                lv = [P.dma("gpsimd", wuv[wb][:, k, :], wup_d[l, k * 128:(k + 1) * 128, DFF + i0 * 128:DFF + (i0 + n) * 128], [wu_free[wb]]) for k in range(8)]
                if gi == 1:
                    lwd0.extend([P.dma("gpsimd", wd0[:, i, :], wdn_d[l, i * 128:(i + 1) * 128, 0:512]) for i in range(22)])
                for ci in range(n):
                    i = i0 + ci
                    hm = None
                    for which, wt in ((0, wug[wb]), (1, wuv[wb])):
                        for k in range(8):
                            hm = P.op("tensor", lambda e, k=k, wt=wt, ci=ci, which=which: e.matmul(
                                psH[:, which * 2:which * 2 + 2], lhsT=wt[:, k, ci * 128:(ci + 1) * 128], rhs=xnx[:, k, 0:2], start=(k == 0), stop=(k == 7)),
                                ([h_free] + lg + lv + lx) if (k == 0 and which == 0) else [])
                    halo = {}
                    for which in range(2):
                        halo[(which, 0)] = P.op("vector", lambda e, which=which: e.tensor_copy(out=uc[(which, 0)][:, 0:2], in_=psH[:, which * 2:which * 2 + 2]),
                                                [hm] + uc_free[(which, 0)])
                    h_free = halo[(1, 0)]
                    for s in range(NSH):
                        p = s % 2
                        dlast = {}
                        for which, wt in ((0, wug[wb]), (1, wuv[wb])):
                            ub = ucnt % NU
                            ucnt += 1
                            ch = i if which == 0 else 22 + i
                            mm = None
                            for k in range(8):
                                mm = P.op("tensor", lambda e, k=k, wt=wt, ci=ci, ub=ub, s=s: e.matmul(
                                    psU[ub], lhsT=wt[:, k, ci * 128:(ci + 1) * 128], rhs=xnx[:, k, 2 + s * 512:2 + (s + 1) * 512], start=(k == 0), stop=(k == 7)),
                                    [u_free[ub]] if k == 0 else [])
                            last_up_mm = mm
                            ucb = uc[(which, p)]
                            acc = accg[p] if which == 0 else accv[p]
                            a0 = P.op("scalar", lambda e, ucb=ucb, ub=ub: e.activation(out=ucb[:, 2:514], in_=psU[ub], func=AF.Copy), [mm] + uc_free[(which, p)])
                            frees = []
                            if s < NSH - 1:
                                hcp = P.op("scalar", lambda e, ucb=ucb, which=which, p=p: e.activation(out=uc[(which, 1 - p)][:, 0:2], in_=ucb[:, 512:514], func=AF.Copy), [a0] + uc_free[(which, 1 - p)])
                                halo[(which, s + 1)] = hcp
                                frees.append(hcp)
                            d0 = P.op("scalar", lambda e, acc=acc, ub=ub, ch=ch: e.activation(out=acc, in_=psU[ub], func=AF.Identity, bias=cb[:, ch:ch + 1], scale=cw[:, 2, ch:ch + 1]),
                                      [mm, acc_free[which][p]])
                            u_free[ub] = d0
                            d1 = P.op("vector", lambda e, acc=acc, ucb=ucb, ch=ch: e.scalar_tensor_tensor(out=acc, in0=ucb[:, 1:513], scalar=cw[:, 1, ch:ch + 1], in1=acc, op0=ALU.mult, op1=ALU.add),
                                      [d0, a0, halo[(which, s)]])
                            d2 = P.op("vector", lambda e, acc=acc, ucb=ucb, ch=ch: e.scalar_tensor_tensor(out=acc, in0=ucb[:, 0:512], scalar=cw[:, 0, ch:ch + 1], in1=acc, op0=ALU.mult, op1=ALU.add), [d1])
                            frees.append(d2)
                            uc_free[(which, p)] = frees
                            dlast[which] = d2
                        def fin_glu(p=p, i=i, s=s, dg_=dlast[0], dv_=dlast[1]):
                            a2 = P.op("scalar", lambda e: e.activation(out=sgt[p], in_=accg[p], func=AF.Silu), [dg_, sgt_free[p]])
                            g1 = P.op("vector", lambda e: e.tensor_tensor(out=hT[:, i, s * 512:(s + 1) * 512], in0=sgt[p], in1=accv[p], op=ALU.mult), [a2, dv_])
                            acc_free[0][p] = a2
                            acc_free[1][p] = g1
                            sgt_free[p] = g1
                            hT_tok.append(g1)
                        if pend["f"] is not None:
                            pend["f"]()
                        pend["f"] = fin_glu
                wu_free[wb] = last_up_mm
            if pend["f"] is not None:
                pend["f"]()
                pend["f"] = None
            lwd1 = [P.dma("gpsimd", wd1[:, i, :], wdn_d[l, i * 128:(i + 1) * 128, 512:1024], [last_up_mm]) for i in range(22)]
            P.wait_only("tensor", hT_tok[-8:] + lwd0)
            d_free = [u_free[4], u_free[5], None]
            xt_free = [None] * 3
            xo_free = [None, None]
            dcnt = 0
            for tt in range(NTH):
                ob = tt % 2
                adds = []
                for half in range(2):
                    pb = dcnt % 3
                    dcnt += 1
                    wd = wd0 if half == 0 else wd1
                    lxt = P.dma("sync", xt[pb], x_mid[t0 + tt * 128:t0 + (tt + 1) * 128, half * 512:(half + 1) * 512], [xt_free[pb]])
                    mm = None
                    for i in range(22):
                        deps = []
                        if i == 0:
                            deps = [d_free[pb]] + (lwd1 if half == 1 else [])
                        mm = P.op("tensor", lambda e, i=i, pb=pb, tt=tt, wd=wd: e.matmul(psD[pb], lhsT=hT[:, i, tt * 128:(tt + 1) * 128], rhs=wd[:, i, :], start=(i == 0), stop=(i == 21)), deps)
                    ad = P.op("vector", lambda e, pb=pb, ob=ob, half=half: e.tensor_tensor(out=xo[ob][:, half * 512:(half + 1) * 512], in0=psD[pb], in1=xt[pb], op=ALU.add), [mm, lxt, xo_free[ob]])
                    d_free[pb] = ad
                    xt_free[pb] = ad
                    adds.append(ad)
                if final:
                    t1 = P.op("scalar", lambda e, ob=ob: e.activation(out=scr, in_=xo[ob], func=AF.Square, accum_out=ss[ob][:, 0:1]), adds)
                    t2 = P.op("scalar", lambda e, ob=ob: e.activation(out=ss[ob][:, 1:2], in_=ss[ob][:, 0:1], func=AF.Ln, bias=eps_t[:, 0:1], scale=1.0 / D), [t1])
                    t3 = P.op("scalar", lambda e, ob=ob: e.activation(out=ss[ob][:, 2:3], in_=ss[ob][:, 1:2], func=AF.Exp, scale=-0.5), [t2])
                    t4 = P.op("vector", lambda e, ob=ob: e.scalar_tensor_tensor(out=xo[ob], in0=xo[ob], scalar=ss[ob][:, 2:3], in1=fw, op0=ALU.mult, op1=ALU.mult), [t3, lfw])
                    adds = [t4]
                xo_free[ob] = P.dma("sync", x_out[t0 + tt * 128:t0 + (tt + 1) * 128, :], xo[ob], adds)
            prog_barrier(P, bar)

        x_cur = x_ext
        for l in range(depth):
            lam_init = 0.8 - 0.6 * math.exp(-0.3 * l)
            stage_proj(l, x_cur)
            stage_attn(l, lam_init)
            stage_mix(l, x_cur, xA)
            final = (l == depth - 1)
            x_next = out_d if final else xB
            for hi in range(2):
                stage_ffn(l, hi, xA, x_next, final)
            x_cur = x_next
        P.emit(None)
    return nc


FF_GROUPS = [(0, 4), (4, 4), (8, 4), (12, 4), (16, 4), (20, 2)]
_CACHE = {}


def kernel(**inputs):
    inp = {k: np.asarray(v) for k, v in inputs.items()}
    f32 = lambda a: np.ascontiguousarray(a, dtype=np.float32)
    dg, dd, dsw = make_tables()
    shared = {
        "w_in": f32(inp["w_in"]), "nw1": f32(inp["norm_mix_w"]),
        "lam4": f32(np.stack([inp["lambda_q1"], inp["lambda_k1"], inp["lambda_q2"], inp["lambda_k2"]], axis=1)),
        "subln": f32(inp["subln_w"]), "sinks": f32(inp["sinks"]),
        "w_br_da": f32(inp["w_br_da"]), "w_br_sw": f32(inp["w_br_sw"]), "w_mix": f32(inp["w_mix_out"]), "nw2": f32(inp["norm_ffn_w"]),
        "w_up": f32(inp["w_up"]), "conv_w": f32(inp["conv_w"]), "conv_b": f32(inp["conv_b"]), "w_down": f32(inp["w_down"]),
        "fnw": f32(inp["norm_final_w"]), "ident": np.eye(128, dtype=np.float32),
        "dg": dg, "dd": dd, "dsw": np.ascontiguousarray(dsw.reshape(128, 16, 128)), "btab": make_btab_full(),
    }
    x = f32(inp["x"])
    if "nc" not in _CACHE:
        _CACHE["nc"] = build_fused()
    nc = _CACHE["nc"]
    zero = {k: np.zeros_like(v) for k, v in shared.items()}
    in_maps = []
    for c in range(2 * B):
        if c % 2 == 0:
            m = dict(shared)
            m["x"] = np.ascontiguousarray(x[c // 2])
        else:
            m = dict(zero)
            m["x"] = np.zeros((TF, D), np.float32)
        in_maps.append(m)
    res = run_bass_kernel_spmd(nc, in_maps, core_ids=list(range(2 * B))).results
    out = np.stack([res[2 * b]["out"] for b in range(B)], axis=0)
    return np.ascontiguousarray(out, dtype=np.float32)
```

```python
import math
import numpy as np
import ml_dtypes
import concourse.bass as bass
import concourse.mybir as mybir
from concourse.bass_utils import run_bass_kernel_spmd

F32 = mybir.dt.float32
BF16 = mybir.dt.bfloat16
AF = mybir.ActivationFunctionType
ALU = mybir.AluOpType
BF = ml_dtypes.bfloat16

D = 1024
S = 4096
B = 4
DEPTH = 4
T = 2048
NT = T // 128
NS = T // 512
INC = 4352
DFF = 2816
EPS = 1e-6
NEG = -30000.0

ENGS = ("tensor", "vector", "scalar", "gpsimd", "sync")


class Prog:
    def __init__(self, nc, n_dma_sems=24):
        self.nc = nc
        self.streams = {e: [] for e in ENGS}
        self.count = {e: 0 for e in ENGS}
        self.n_dma = n_dma_sems
        self.dma_cnt = [0] * n_dma_sems
        self.pool = {"gpsimd": list(range(0, 8)), "sync": list(range(8, n_dma_sems - 4)), "scalar": list(range(n_dma_sems - 4, n_dma_sems))}
        self.dma_rr = {"gpsimd": 0, "sync": 0, "scalar": 0}
        self.all_dma = []

    def op(self, eng, fn, deps=()):
        self.count[eng] += 1
        tok = ("e", eng, self.count[eng])
        self.streams[eng].append(("op", fn, [d for d in deps if d is not None]))
        return tok

    def dma(self, queue, out, in_, deps=(), slow=False):
        pl = self.pool[queue]
        i = pl[self.dma_rr[queue] % len(pl)]
        self.dma_rr[queue] += 1
        prev = ("d", i, self.dma_cnt[i]) if self.dma_cnt[i] > 0 else None
        self.dma_cnt[i] += 1
        tok = ("d", i, self.dma_cnt[i])
        dl = [d for d in deps if d is not None]
        if prev is not None:
            dl.append(prev)
        self.streams[queue].append(("dma", (out, in_, i, slow), dl))
        self.all_dma.append(tok)
        return tok

    def wait_only(self, eng, deps):
        self.streams[eng].append(("wait", None, [d for d in deps if d is not None]))

    def emit(self, block):
        nc = self.nc
        import contextlib
        with contextlib.ExitStack() as es:
            esem = {e: es.enter_context(nc.semaphore("s_" + e)) for e in ENGS}
            dsem = [es.enter_context(nc.semaphore("d_%d" % i)) for i in range(self.n_dma)]
            blk = es.enter_context(nc.Block())

            def replay(name, eng):
                waited = {}

                def do_waits(deps):
                    for d in deps:
                        if d[0] == "e":
                            key = ("e", d[1])
                            val = d[2]
                            sem = esem[d[1]]
                        else:
                            key = ("d", d[1])
                            val = d[2] * 16
                            sem = dsem[d[1]]
                        if waited.get(key, 0) >= val:
                            continue
                        waited[key] = val
                        eng.wait_ge(sem, val)

                for kind, payload, deps in self.streams[name]:
                    do_waits(deps)
                    if kind == "op":
                        payload(eng).then_inc(esem[name], 1)
                    elif kind == "dma":
                        out, in_, i, slow = payload
                        if slow:
                            eng.dma_start(out=out, in_=in_, allow_slow_non_contiguous=True).then_inc(dsem[i], 16)
                        else:
                            eng.dma_start(out=out, in_=in_).then_inc(dsem[i], 16)

            final = list(self.all_dma)

            @blk.tensor
            def _(e):
                replay("tensor", e)

            @blk.vector
            def _(e):
                replay("vector", e)

            @blk.scalar
            def _(e):
                replay("scalar", e)

            @blk.gpsimd
            def _(e):
                replay("gpsimd", e)

            @blk.sync
            def _(e):
                replay("sync", e)
                for i in range(self.n_dma):
                    if self.dma_cnt[i] > 0:
                        e.wait_ge(dsem[i], self.dma_cnt[i] * 16)
                for en in ENGS:
                    if en != "sync" and self.count[en] > 0:
                        e.wait_ge(esem[en], self.count[en])


def _run(nc, in_maps):
    res = run_bass_kernel_spmd(nc, in_maps, core_ids=list(range(len(in_maps))))
    return res.results


def emit_rmsnorm_T(P, xt_ap, xt_tok, scr, ss, xn, pst, identb, wn, dst_fn, eps_t, n_feat=1024, defer=False, on_act=False):
    t1 = P.op("scalar", lambda e: e.activation(out=scr, in_=xt_ap, func=AF.Square, accum_out=ss[:, 0:1]), [xt_tok])
    t2 = P.op("scalar", lambda e: e.activation(out=ss[:, 1:2], in_=ss[:, 0:1], func=AF.Ln, bias=eps_t[:, 0:1], scale=1.0 / n_feat), [t1])
    t3 = P.op("scalar", lambda e: e.activation(out=ss[:, 2:3], in_=ss[:, 1:2], func=AF.Exp, scale=-0.5), [t2])
    if on_act:
        t4 = P.op("scalar", lambda e: e.activation(out=xn, in_=xt_ap, func=AF.Copy, scale=ss[:, 2:3]), [t3, xt_tok])
    else:
        t4 = P.op("vector", lambda e: e.tensor_scalar(out=xn, in0=xt_ap, scalar1=ss[:, 2:3], scalar2=None, op0=ALU.mult), [t3, xt_tok])

    def finish(extra_pe_deps=(), extra_ev_deps=()):
        toks = []
        tp = []
        for c in range(n_feat // 128):
            tp.append(P.op("tensor", lambda e, c=c: e.transpose(pst[:, c * 128:(c + 1) * 128], xn[:, c * 128:(c + 1) * 128], identb),
                           [t4] + (list(extra_pe_deps) if c == 0 else [])))
        for c in range(n_feat // 128):
            if on_act:
                toks.append(P.op("scalar", lambda e, c=c: e.activation(out=dst_fn(c), in_=pst[:, c * 128:(c + 1) * 128], func=AF.Copy, scale=wn[:, c:c + 1]),
                                 [tp[-1]] + (list(extra_ev_deps) if c == 0 else [])))
            else:
                toks.append(P.op("vector", lambda e, c=c: e.tensor_scalar(out=dst_fn(c), in0=pst[:, c * 128:(c + 1) * 128], scalar1=wn[:, c:c + 1], scalar2=None, op0=ALU.mult),
                                 [tp[-1]] + (list(extra_ev_deps) if c == 0 else [])))
        return toks

    if defer:
        return finish, t4
    return finish(), t4


def da_slopes():
    return [2.0 ** (-2 * (h + 1)) for h in range(4)]


def sw_slopes():
    return [2.0 ** (-(h + 1)) for h in range(8)]


def nkb(s):
    return 16 + 4 * (s + 1)


def make_tables():
    jj = np.arange(128, dtype=np.float64)[:, None]
    ii = np.arange(512, dtype=np.float64)[None, :]
    dg = np.zeros((128, 4, 512), np.float32)
    dd = np.zeros((128, 4, 4, 512), np.float32)
    for h, sl in enumerate(da_slopes()):
        dg[:, h, :] = np.exp(sl * (jj - 127 - ii))
        for d in range(4):
            rel = 128 * d + jj - ii
            dd[:, h, d, :] = np.where(rel <= 0, np.exp(sl * np.minimum(rel, 0)), 0.0)
    i2 = np.arange(128, dtype=np.float64)[None, :]
    dsw = np.zeros((128, 2, 2, 2, 2, 128), np.float32)
    for g in range(2):
        for hf in range(2):
            for j in range(2):
                hh = 4 * g + hf + 2 * j
                sl = sw_slopes()[hh]
                dist_p = 128 + i2 - jj
                dsw[:, g, hf, 0, j, :] = np.where(jj > i2, np.exp(-sl * dist_p), 0.0)
                dist_d = i2 - jj
                dsw[:, g, hf, 1, j, :] = np.where(jj <= i2, np.exp(-sl * np.maximum(dist_d, 0)), 0.0)
    return dg.astype(BF), dd.astype(BF), dsw.astype(BF)


def make_btab(half):
    cols = []
    for s in range(NS):
        for h, sl in enumerate(da_slopes()):
            for kb in range(nkb(s)):
                if kb < 16:
                    if half == 0:
                        v = NEG
                    else:
                        v = sl * (128 * kb - 2048 - 512 * s + 127)
                else:
                    m = kb - 16
                    if m < 4 * s:
                        v = sl * (128 * m - 512 * s + 127)
                    else:
                        v = 0.0
                cols.append(v)
    cols.append(NEG if half == 0 else 0.0)
    t = np.tile(np.asarray(cols, np.float32)[None, :], (128, 1))
    return np.ascontiguousarray(t)


TF = 4096
NTF = TF // 128
NSF = TF // 512


def nkbf(s):
    return 4 * (s + 1)


DEAD = -100.0


def kb_first(s, h):
    sl = da_slopes()[h]
    m = 0
    while m < 4 * s and sl * (128 * m - 512 * s + 127) < DEAD:
        m += 1
    return m


NBTF = sum(nkbf(s) - kb_first(s, h) for s in range(NSF) for h in range(4)) + 1


def make_btab_full():
    cols = []
    for s in range(NSF):
        for h, sl in enumerate(da_slopes()):
            for m in range(kb_first(s, h), nkbf(s)):
                cols.append(sl * (128 * m - 512 * s + 127) if m < 4 * s else 0.0)
    cols.append(NEG)
    return np.ascontiguousarray(np.tile(np.asarray(cols, np.float32)[None, :], (128, 1)))


class Arena:
    def __init__(self, base_ap, nbytes):
        self.base = base_ap
        self.nbytes = nbytes
        self.off = 0
        self.mark = 0

    def alloc(self, shape, dt):
        assert shape[0] == 128
        esz = 4 if dt == F32 else 2
        n = 1
        for v in shape[1:]:
            n *= v
        nb = (n * esz + 63) // 64 * 64
        assert self.off + nb <= self.nbytes, ("arena overflow", self.off, nb, self.nbytes)
        ap = self.base[:, self.off // 4:(self.off + nb) // 4]
        self.off += nb
        if dt != F32:
            ap = ap.bitcast(dt)
        ap = ap[:, 0:n]
        if len(shape) == 3:
            ap = ap.rearrange("p (a b) -> p a b", b=shape[2])
        elif len(shape) == 4:
            ap = ap.rearrange("p (a b c) -> p a b c", b=shape[2], c=shape[3])
        return ap

    def set_mark(self):
        self.mark = self.off

    def reset(self):
        self.off = self.mark


def prog_barrier(P, bar):
    deps = [("e", e, P.count[e]) for e in ENGS if P.count[e] > 0] + [("d", i, P.dma_cnt[i]) for i in range(P.n_dma) if P.dma_cnt[i] > 0]
    tok = P.dma("sync", bar[1:2, :], P.bar_src, deps)
    for e in ("tensor", "vector", "scalar", "gpsimd"):
        P.wait_only(e, [tok])
    return tok


FM_CHUNKS = [(j, "q") for j in range(0, 4)] + [(j, "k") for j in range(4, 8)] + \
            [(j, "q") for j in range(12, 16)] + [(16, "k")] + [(j, "g") for j in range(18, 34)]


def build_fused(depth=DEPTH, debug_out=None):
    nc = bass.Bass("TRN2", target_bir_lowering=False)
    dt_in = lambda name, shape, dt: nc.dram_tensor(name, shape, dt, kind="ExternalInput").ap()
    dt_sc = lambda name, shape, dt: nc.dram_tensor(name, shape, dt, kind="Internal").ap()
    x_ext = dt_in("x", [TF, D], F32)
    w_in_d = dt_in("w_in", [depth, D, INC], F32)
    nw1_d = dt_in("nw1", [depth, D], F32)
    lam4_d = dt_in("lam4", [depth, 4, 64], F32)
    subln_d = dt_in("subln", [depth, 128], F32)
    sinks_d = dt_in("sinks", [depth, 8], F32)
    wbda_d = dt_in("w_br_da", [depth, 512, D], F32)
    wbsw_d = dt_in("w_br_sw", [depth, 512, D], F32)
    wmix_d = dt_in("w_mix", [depth, D, D], F32)
    nw2_d = dt_in("nw2", [depth, D], F32)
    wup_d = dt_in("w_up", [depth, D, 2 * DFF], F32)
    cw_d = dt_in("conv_w", [depth, 3, 2 * DFF], F32)
    cb_d = dt_in("conv_b", [depth, 2 * DFF], F32)
    wdn_d = dt_in("w_down", [depth, DFF, D], F32)
    fnw_d = dt_in("fnw", [D], F32)
    ident_d = dt_in("ident", [128, 128], F32)
    dg_d = dt_in("dg", [128, 4, 512], BF16)
    dd_d = dt_in("dd", [128, 4, 4, 512], BF16)
    dsw_d = dt_in("dsw", [128, 16, 128], BF16)
    btab_d = dt_in("btab", [128, NBTF], F32)
    out_d = nc.dram_tensor("out", [TF, D], F32, kind="ExternalOutput").ap()
    xA = dt_sc("xA", [TF, D], F32)
    xB = dt_sc("xB", [TF, D], F32)
    fm_d = dt_sc("fm", [29 * 128, TF], BF16)
    vda_d = dt_sc("vda", [TF, 516], BF16)
    vsw_d = dt_sc("vsw", [TF, 130], BF16)
    yda_d = dt_sc("yda", [512, TF], BF16)
    ysw_d = dt_sc("ysw", [512, TF], BF16)
    xn2_d = dt_sc("xn2", [D, TF], BF16)
    bar = dt_sc("bar", [2, 16], F32)
    ARENA_BYTES = 206 * 1024
    import contextlib
    with contextlib.ExitStack() as es:
        arena_t = es.enter_context(nc.sbuf_tensor("arena", [128, ARENA_BYTES // 4], F32))
        bank = [es.enter_context(nc.psum_tensor("bank%d" % i, [128, 512], F32)) for i in range(8)]
        bankf = [b[:] for b in bank]
        bankb = [b[:].bitcast(BF16) for b in bank]
        A = Arena(arena_t[:], ARENA_BYTES)
        P = Prog(nc)
        identb = A.alloc([128, 128], BF16)
        eps_t = A.alloc([128, 16], F32)
        A.set_mark()
        P.bar_src = ident_d[0:1, 0:16]
        c_id = P.dma("gpsimd", identb, ident_d)
        c_eps = P.op("vector", lambda e: e.memset(eps_t[:, 0:1], EPS))
        P.wait_only("tensor", [c_id])
        P.wait_only("scalar", [c_eps])
        P.wait_only("gpsimd", [c_eps])

        def stage_proj(l, x_in):
            A.reset()
            NT_, NS_ = NTF, NSF
            wb = A.alloc([128, 8, INC], BF16)
            xnT = A.alloc([128, 8, TF], BF16)
            xt = [A.alloc([128, D], F32) for i in range(2)]
            scr = A.alloc([128, D], F32)
            xn = [A.alloc([128, D], BF16) for i in range(4)]
            ss = [A.alloc([128, 4], F32) for i in range(4)]
            wn = A.alloc([128, 8], F32)
            stg = [A.alloc([128, 512], BF16) for i in range(4)]
            vst = [A.alloc([128, 4 * 129], BF16) for i in range(2)]
            vsst = [A.alloc([128, 2 * 65], BF16) for i in range(2)]
            ps = bankf[0:6]
            pst = bankb[6:8]
            t_wn = P.dma("sync", wn, nw1_d[l].rearrange("(c p) -> p c", p=128), slow=True)
            t_ones = [P.op("vector", lambda e, i=i: e.memset(vst[i], 1.0)) for i in range(2)]
            t_ones2 = [P.op("vector", lambda e, i=i: e.memset(vsst[i], 1.0)) for i in range(2)]
            wtok = {}
            for q in range(4):
                for k in range(8):
                    c0 = q * 1088
                    wtok[(k, q)] = P.dma("gpsimd", wb[:, k, c0:c0 + 1088], w_in_d[l, k * 128:(k + 1) * 128, c0:c0 + 1088])

            def wdeps(c0, c1):
                qs = sorted(set([c0 // 1088, (c1 - 1) // 1088]))
                return [wtok[(k, q)] for k in range(8) for q in qs]
            P.wait_only("vector", [t_wn])
            xnT_tok = [None] * NT_
            st1 = {"xt_free": [None, None], "xn_free": [None, None], "pst_free": [None, None], "psi": 0, "si": 0,
                   "ps_free": [None] * 6, "stg_free": [None] * 4, "v_free": [None, None], "vs_free": [None, None]}

            st1["xn_free"] = [None] * 4
            st1["fins"] = {}

            def norm1_slot(s):
                for tt in range(4 * s, 4 * s + 4):
                    b = tt % 2
                    q = tt % 4
                    ld = P.dma("sync", xt[b], x_in[tt * 128:(tt + 1) * 128, :], [st1["xt_free"][b]])
                    P.wait_only("scalar", [st1["xn_free"][q]])
                    fin, t4 = emit_rmsnorm_T(P, xt[b], ld, scr, ss[q], xn[q], pst[b], identb, wn,
                                             lambda c, tt=tt: xnT[:, c, tt * 128:(tt + 1) * 128], eps_t, defer=True)
                    st1["fins"][tt] = fin
                    st1["xt_free"][b] = t4

            def finish_slot(s):
                for tt in range(4 * s, 4 * s + 4):
                    b = tt % 2
                    q = tt % 4
                    toks = st1["fins"].pop(tt)([st1["pst_free"][b]])
                    xnT_tok[tt] = toks
                    st1["xn_free"][q] = toks[-1]
                    st1["pst_free"][b] = toks[-1]

            def proj_slot(s):
                for ci, (j, kind) in enumerate(FM_CHUNKS):
                    if ci == 14 and s + 1 < NS_:
                        finish_slot(s + 1)
                    pb = st1["psi"] % 6
                    st1["psi"] += 1
                    deps = wdeps(j * 128, (j + 1) * 128) + [st1["ps_free"][pb]]
                    for tt in range(s * 4, s * 4 + 4):
                        deps += xnT_tok[tt]
                    mm = None
                    for k in range(8):
                        mm = P.op("tensor", lambda e, k=k, pb=pb, j=j, s=s: e.matmul(
                            ps[pb], lhsT=wb[:, k, j * 128:(j + 1) * 128], rhs=xnT[:, k, s * 512:(s + 1) * 512],
                            start=(k == 0), stop=(k == 7)), deps if k == 0 else [])
                    sb_i = st1["si"] % 4
                    st1["si"] += 1
                    if kind == "q":
                        ev = P.op("scalar", lambda e, pb=pb, sb_i=sb_i: e.activation(out=stg[sb_i], in_=ps[pb], func=AF.Copy, scale=0.125), [mm, st1["stg_free"][sb_i]])
                    elif kind == "k":
                        ev = P.op("vector", lambda e, pb=pb, sb_i=sb_i: e.tensor_copy(out=stg[sb_i], in_=ps[pb]), [mm, st1["stg_free"][sb_i]])
                    else:
                        ev = P.op("scalar", lambda e, pb=pb, sb_i=sb_i: e.activation(out=stg[sb_i], in_=ps[pb], func=AF.Sigmoid), [mm, st1["stg_free"][sb_i]])
                    st1["ps_free"][pb] = ev
                    st1["stg_free"][sb_i] = P.dma("sync", fm_d[ci * 128:(ci + 1) * 128, s * 512:(s + 1) * 512], stg[sb_i], [ev])
                for tt in range(4 * s, 4 * s + 4):
                    b = tt % 2
                    pb = st1["psi"] % 6
                    st1["psi"] += 1
                    deps = wdeps(1024, 1536) + [st1["ps_free"][pb]] + xnT_tok[tt]
                    mm = None
                    for k in range(8):
                        mm = P.op("tensor", lambda e, k=k, pb=pb, tt=tt: e.matmul(
                            ps[pb], lhsT=xnT[:, k, tt * 128:(tt + 1) * 128], rhs=wb[:, k, 1024:1536],
                            start=(k == 0), stop=(k == 7)), deps if k == 0 else [])
                    ev = P.op("vector", lambda e, pb=pb, b=b: e.tensor_copy(
                        out=vst[b].rearrange("p (h e) -> p h e", e=129)[:, :, 0:128],
                        in_=ps[pb].rearrange("p (h e) -> p h e", e=128)), [mm, st1["v_free"][b], t_ones[b]])
                    st1["ps_free"][pb] = ev
                    st1["v_free"][b] = P.dma("sync", vda_d[tt * 128:(tt + 1) * 128, :], vst[b], [ev])
                    pb = st1["psi"] % 6
                    st1["psi"] += 1
                    deps = wdeps(2176, 2304) + [st1["ps_free"][pb]] + xnT_tok[tt]
                    for k in range(8):
                        mm = P.op("tensor", lambda e, k=k, pb=pb, tt=tt: e.matmul(
                            ps[pb][:, 0:128], lhsT=xnT[:, k, tt * 128:(tt + 1) * 128], rhs=wb[:, k, 2176:2304],
                            start=(k == 0), stop=(k == 7)), deps if k == 0 else [])
                    ev = P.op("vector", lambda e, pb=pb, b=b: e.tensor_copy(
                        out=vsst[b].rearrange("p (h e) -> p h e", e=65)[:, :, 0:64],
                        in_=ps[pb][:, 0:128].rearrange("p (h e) -> p h e", e=64)), [mm, st1["vs_free"][b], t_ones2[b]])
                    st1["ps_free"][pb] = ev
                    st1["vs_free"][b] = P.dma("sync", vsw_d[tt * 128:(tt + 1) * 128, :], vsst[b], [ev])

            norm1_slot(0)
            finish_slot(0)
            for s in range(NS_):
                if s + 1 < NS_:
                    norm1_slot(s + 1)
                proj_slot(s)
            prog_barrier(P, bar)

        def stage_attn(l, lam_init):
            A.reset()
            NS_ = NSF
            dg = A.alloc([128, 4, 512], BF16)
            dd = A.alloc([128, 4, 4, 512], BF16)
            dsw = A.alloc([128, 16, 128], BF16)
            btab = A.alloc([128, NBTF], F32)
            c_ld = [P.dma("sync", dg, dg_d), P.dma("sync", dd, dd_d), P.dma("sync", dsw, dsw_d), P.dma("sync", btab, btab_d)]
            P.wait_only("vector", c_ld)
            P.wait_only("scalar", c_ld)
            kda = A.alloc([128, 4, TF], BF16)
            vda = A.alloc([128, 32, 516], BF16)
            ksw = A.alloc([128, 2, TF], BF16)
            vsw = A.alloc([128, 32, 130], BF16)
            qs = [A.alloc([128, 4, 512], BF16) for i in range(2)]
            qsw = [A.alloc([128, 4, 512], BF16) for i in range(2)]
            lamb = A.alloc([128, 4, 64], F32)
            lscr = A.alloc([128, 64], F32)
            lsum = A.alloc([128, 4], F32)
            neglam = A.alloc([128, 1], F32)
            slnw = A.alloc([128, 1], F32)
            esk = A.alloc([128, 8], F32)
            NPB = 6
            pbuf = [A.alloc([128, 512], BF16) for i in range(NPB)]
            accs = [A.alloc([128, 9, 129], F32) for i in range(2)]
            rc = [A.alloc([128, 8], F32) for i in range(2)]
            a_t = [A.alloc([128, 4, 128], F32) for i in range(2)]
            ssq = [A.alloc([128, 12], F32) for i in range(2)]
            junk = A.alloc([128, 8, 128], F32)
            y_t = [A.alloc([128, 4, 128], BF16) for i in range(2)]
            ysg = [A.alloc([128, 512], BF16) for i in range(2)]
            saccs = [A.alloc([128, 8, 65], F32) for i in range(2)]
            sden = [A.alloc([128, 8], F32) for i in range(2)]
            ysw_t = [A.alloc([128, 8, 64], BF16) for i in range(2)]
            yswg = [A.alloc([128, 4, 128], BF16) for i in range(2)]
            NSB = 4
            psS = bankf[0:4]
            psA = bankf[4:7]
            psT = [bankb[7], bankb[7]]
            ld = {}
            ld["lam"] = P.dma("sync", lamb.rearrange("p a b -> p (a b)"), lam4_d[l].rearrange("a b -> (a b)").partition_broadcast(128))
            ld["subln"] = P.dma("sync", slnw, subln_d[l].rearrange("(p o) -> p o", o=1))
            ld["sinks"] = P.dma("sync", esk, sinks_d[l].partition_broadcast(128))
            q_src = fm_d[0:512, :].rearrange("(h p) t -> p h t", p=128)
            qsw_src = fm_d[8 * 128:12 * 128, :].rearrange("(h p) t -> p h t", p=128)
            vda_src = vda_d.rearrange("(n p) e -> p n e", p=128)
            ld["kda"] = [P.dma("sync", kda[:, 0, :], fm_d[4 * 128:5 * 128, :])]
            lq_first = P.dma("sync", qs[0], q_src[:, :, 0:512])
            ld["vda"] = [P.dma("sync", vda[:, 0:8, :], vda_src[:, 0:8, :])]
            ld["kda"] += [P.dma("sync", kda[:, h, :], fm_d[(4 + h) * 128:(5 + h) * 128, :]) for h in range(1, 4)]
            ld["vda"] += [P.dma("sync", vda[:, q * 8:(q + 1) * 8, :], vda_src[:, q * 8:(q + 1) * 8, :]) for q in range(1, 4)]
            ld["ksw"] = [P.dma("sync", ksw[hf * 64:(hf + 1) * 64, g, :], fm_d[12 * 128 + g * 64:12 * 128 + (g + 1) * 64, :]) for hf in range(2) for g in range(2)]
            ld["vsw"] = P.dma("sync", vsw, vsw_d.rearrange("(n p) e -> p n e", p=128))
            tl = []
            for i in range(2):
                tl.append(P.op("vector", lambda e, i=i: e.scalar_tensor_tensor(
                    out=lscr, in0=lamb[:, 2 * i, :], scalar=1.0, in1=lamb[:, 2 * i + 1, :], op0=ALU.mult, op1=ALU.mult,
                    accum_out=lsum[:, i:i + 1]), [ld["lam"]] + tl))
            te = P.op("scalar", lambda e: e.activation(out=lsum[:, 2:4], in_=lsum[:, 0:2], func=AF.Exp), tl)
            tn = P.op("vector", lambda e: e.tensor_tensor(out=neglam, in0=lsum[:, 3:4], in1=lsum[:, 2:3], op=ALU.subtract), [te])
            tn = P.op("vector", lambda e: e.tensor_scalar(out=neglam, in0=neglam, scalar1=-float(lam_init), scalar2=None, op0=ALU.add), [tn])
            tsl = P.op("vector", lambda e: e.tensor_scalar(out=slnw, in0=slnw, scalar1=float(1.0 - lam_init), scalar2=None, op0=ALU.mult), [ld["subln"]])
            tes = P.op("scalar", lambda e: e.activation(out=esk, in_=esk, func=AF.Exp), [ld["sinks"]])
            state = {"it": 0, "s_free": [None] * NSB, "p_free": [None] * NPB, "acc_free": None, "grp": 0,
                     "pst_free": [None], "deferred": None, "ysg_free": [None, None], "q_free": [None, None]}

            def acc_ap(c, r):
                a = c * 4 + r
                return psA[a // 3][:, (a % 3) * 129:(a % 3) * 129 + 129]

            def finalize_A(par, last_pv):
                f1 = []
                for bk in range(3):
                    n = 3 if bk < 2 else 2
                    f1.append(P.op("vector", lambda e, bk=bk, n=n: e.tensor_copy(
                        out=accs[par][:, bk * 3:bk * 3 + n, :].rearrange("p a b -> p (a b)"), in_=psA[bk][:, 0:n * 129]), [last_pv] if bk == 0 else []))
                f2 = P.op("vector", lambda e: e.reciprocal(out=rc[par], in_=accs[par][:, 0:8, 128]), [f1[-1]])
                f3 = P.op("vector", lambda e: e.tensor_scalar(out=rc[par][:, 4:8], in0=rc[par][:, 4:8], scalar1=neglam[:, 0:1], scalar2=None, op0=ALU.mult), [f2, tn])
                f5 = []
                for r in range(4):
                    f4 = P.op("vector", lambda e, r=r: e.tensor_scalar(out=a_t[par][:, r, :], in0=accs[par][:, r, 0:128], scalar1=rc[par][:, r:r + 1], scalar2=None, op0=ALU.mult), [f3])
                    f5.append(P.op("vector", lambda e, r=r: e.scalar_tensor_tensor(
                        out=a_t[par][:, r, :], in0=accs[par][:, 4 + r, 0:128], scalar=rc[par][:, 4 + r:5 + r], in1=a_t[par][:, r, :],
                        op0=ALU.mult, op1=ALU.add), [f4]))
                return f1[-1], f5

            def finalize_B(par, f5, h, s):
                f6 = []
                for r in range(4):
                    f6.append(P.op("scalar", lambda e, r=r: e.activation(out=junk[:, par * 4 + r, :], in_=a_t[par][:, r, :], func=AF.Square, accum_out=ssq[par][:, r:r + 1]), [f5[r]]))
                f7 = P.op("scalar", lambda e: e.activation(out=ssq[par][:, 4:8], in_=ssq[par][:, 0:4], func=AF.Ln, bias=eps_t[:, 0:1], scale=1.0 / 128), [f6[-1]])
                f8 = P.op("scalar", lambda e: e.activation(out=ssq[par][:, 8:12], in_=ssq[par][:, 4:8], func=AF.Exp, scale=-0.5), [f7])
                f9 = []
                for r in range(4):
                    f9.append(P.op("vector", lambda e, r=r: e.tensor_scalar(out=y_t[par][:, r, :], in0=a_t[par][:, r, :], scalar1=ssq[par][:, 8 + r:9 + r], scalar2=None, op0=ALU.mult), [f8]))
                tp = None
                for r in range(4):
                    tp = P.op("tensor", lambda e, r=r: e.transpose(psT[par][:, r * 128:(r + 1) * 128], y_t[par][:, r, :], identb),
                              [f9[r]] + ([state["pst_free"][0]] if r == 0 else []))
                f10 = P.op("vector", lambda e: e.tensor_scalar(out=ysg[par], in0=psT[par][:, 0:512], scalar1=slnw[:, 0:1], scalar2=None, op0=ALU.mult), [tp, tsl, state["ysg_free"][par]])
                state["pst_free"][0] = f10
                state["ysg_free"][par] = P.dma("sync", yda_d[h * 128:(h + 1) * 128, s * 512:(s + 1) * 512], ysg[par], [f10])

            bcol = 0
            for s in range(NS_):
                qb = s % 2
                lq = lq_first if s == 0 else P.dma("sync", qs[qb], q_src[:, :, s * 512:(s + 1) * 512], [state["q_free"][qb]])
                last_qk_slot = None
                for h in range(4):
                    par = state["grp"] % 2
                    kb0 = kb_first(s, h)
                    n_it = (nkbf(s) - kb0) * 2
                    items = [(kb, c) for kb in range(kb0, nkbf(s)) for c in range(2)]
                    mul_tok = {}
                    qk_tok = {}

                    def emit_qk(i, h=h, s=s, qb=qb, lq=lq):
                        kb, c = items[i]
                        g_i = state["it"] + i
                        sbi = g_i % NSB
                        deps = [state["s_free"][sbi]]
                        if i == 0:
                            deps += [lq, ld["kda"][h]]
                        qk_tok[i] = P.op("tensor", lambda e: e.matmul(
                            psS[sbi], lhsT=kda[c * 64:(c + 1) * 64, h, kb * 128:(kb + 1) * 128],
                            rhs=qs[qb][c * 64:(c + 1) * 64, h, :], start=True, stop=True), deps)
                        return qk_tok[i]

                    def emit_soft(i, h=h, s=s, bcol_base=bcol, kb0=kb0):
                        kb, c = items[i]
                        g_i = state["it"] + i
                        sbi = g_i % NSB
                        pbi = g_i % NPB
                        col = bcol_base + kb - kb0
                        ex = P.op("scalar", lambda e: e.activation(out=pbuf[pbi], in_=psS[sbi], func=AF.Exp, bias=btab[:, col:col + 1], scale=1.0),
                                  [qk_tok[i], state["p_free"][pbi]])
                        state["s_free"][sbi] = ex
                        if kb >= 4 * s:
                            dtile = dd[:, h, kb - 4 * s, :]
                        else:
                            dtile = dg[:, h, :]
                        mul_tok[i] = P.op("vector", lambda e: e.tensor_tensor(out=pbuf[pbi], in0=pbuf[pbi], in1=dtile, op=ALU.mult), [ex])

                    def emit_pv(i, h=h, s=s, kb0=kb0):
                        kb, c = items[i]
                        g_i = state["it"] + i
                        pbi = g_i % NPB
                        pv = None
                        for r in range(4):
                            deps = [mul_tok[i]] if r == 0 else []
                            if i < 2 and r == 0:
                                deps += [state["acc_free"]]
                            if r == 0:
                                deps += [ld["vda"][kb // 8]]
                            first = (kb == kb0) and ((c * 4 + r) % 3 == 0)
                            pv = P.op("tensor", lambda e, r=r, first=first: e.matmul(
                                acc_ap(c, r), lhsT=pbuf[pbi][:, r * 128:(r + 1) * 128], rhs=vda[:, kb, h * 129:(h + 1) * 129],
                                start=first, stop=False, skip_group_check=True), deps)
                        state["p_free"][pbi] = pv
                        return pv

                    LOOK = 2
                    for i in range(min(LOOK, n_it)):
                        last_qk_slot = emit_qk(i)
                    last_pv = None
                    for i in range(n_it):
                        if i % 2 == 0:
                            for j in (i + LOOK, i + LOOK + 1):
                                if j < n_it:
                                    last_qk_slot = emit_qk(j)
                        emit_soft(i)
                        last_pv = emit_pv(i)
                        if i == 5 and state["deferred"] is not None:
                            finalize_B(*state["deferred"])
                            state["deferred"] = None
                    if state["deferred"] is not None:
                        finalize_B(*state["deferred"])
                        state["deferred"] = None
                    accf, f5 = finalize_A(par, last_pv)
                    state["acc_free"] = accf
                    state["deferred"] = (par, f5, h, s)
                    state["it"] += n_it
                    state["grp"] += 1
                    bcol += nkbf(s) - kb0
                state["q_free"][qb] = last_qk_slot
            if state["deferred"] is not None:
                finalize_B(*state["deferred"])
                state["deferred"] = None
            steps = [(n, g, hf) for n in range(4 * NS_) for g in range(2) for hf in range(2)]
            sw = {"acc_free": [[state["acc_free"]], [state["acc_free"], state["s_free"][3]]], "qsw_free": [None, None], "yswg_free": [None, None], "lqs": {}, "qk": {}, "mu": {}, "last_pv": None,
                  "pendA": None, "pendB": None, "ysw_t_free": [None, None]}
            it0 = state["it"]
            NSW = 3
            swacc = [[bankf[4], bankf[5]], [bankf[6], bankf[3]]]

            def sw_qk(idx):
                n, g, hf = steps[idx]
                sbi = idx % NSW
                s = n // 4
                qb = s % 2
                nl = n % 4
                if nl == 0 and g == 0 and hf == 0:
                    sw["lqs"][s] = P.dma("sync", qsw[qb], qsw_src[:, :, s * 512:(s + 1) * 512], [sw["qsw_free"][qb]])
                deps = [state["s_free"][sbi], sw["lqs"][s]]
                if idx == 0:
                    deps += ld["ksw"] + [ld["vsw"]]
                t = None
                for kind in range(2):
                    if n == 0 and kind == 0:
                        continue
                    kblk = n - 1 + kind
                    t = P.op("tensor", lambda e, g=g, hf=hf, kblk=kblk, sbi=sbi, nl=nl, qb=qb, kind=kind: e.matmul(
                        psS[sbi][:, kind * 256:(kind + 1) * 256], lhsT=ksw[hf * 64:(hf + 1) * 64, g, kblk * 128:(kblk + 1) * 128],
                        rhs=qsw[qb][hf * 64:(hf + 1) * 64, 2 * g:2 * g + 2, nl * 128:(nl + 1) * 128], start=True, stop=True), deps if t is None else [])
                sw["qk"][idx] = t
                if nl == 3 and g == 1 and hf == 1:
                    sw["qsw_free"][qb] = t

            def sw_soft(idx):
                n, g, hf = steps[idx]
                sbi = idx % NSW
                pbi = (it0 + idx) % NPB
                c0 = 256 if n == 0 else 0
                ex = P.op("scalar", lambda e: e.activation(out=pbuf[pbi][:, c0:512], in_=psS[sbi][:, c0:512], func=AF.Exp), [sw["qk"][idx], state["p_free"][pbi]])
                state["s_free"][sbi] = ex
                base = (g * 2 + hf) * 4
                dt_ap = dsw[:, base:base + 4, :].rearrange("p a b -> p (a b)")
                sw["mu"][idx] = P.op("vector", lambda e: e.tensor_tensor(out=pbuf[pbi][:, c0:512], in0=pbuf[pbi][:, c0:512], in1=dt_ap[:, c0:512], op=ALU.mult), [ex])

            def sw_pv(idx):
                n, g, hf = steps[idx]
                pbi = (it0 + idx) % NPB
                pv = None
                for kind in range(2):
                    if n == 0 and kind == 0:
                        continue
                    kblk = n - 1 + kind
                    for j in range(2):
                        gi = hf + 2 * j
                        first = (hf == 0 and pv is None)
                        deps = []
                        if pv is None:
                            deps = [sw["mu"][idx]]
                            if hf == 0:
                                deps += sw["acc_free"][n % 2]
                        accb = swacc[n % 2][g]
                        pv = P.op("tensor", lambda e, accb=accb, gi=gi, j=j, kblk=kblk, kind=kind, first=first: e.matmul(
                            accb[:, gi * 65:(gi + 1) * 65], lhsT=pbuf[pbi][:, kind * 256 + j * 128:kind * 256 + (j + 1) * 128], rhs=vsw[:, kblk, g * 65:(g + 1) * 65],
                            start=first, stop=False, skip_group_check=True), deps)
                state["p_free"][pbi] = pv
                sw["last_pv"] = pv

            def sw_final(n, last_pv_n):
                par = n % 2
                f1 = None
                for g in range(2):
                    f1 = P.op("vector", lambda e, g=g: e.tensor_copy(out=saccs[par][:, g * 4:(g + 1) * 4, :].rearrange("p a b -> p (a b)"), in_=swacc[par][g][:, 0:260]), [last_pv_n] if g == 0 else [])
                sw["acc_free"][par] = [f1]
                f2 = P.op("vector", lambda e: e.tensor_tensor(out=sden[par], in0=saccs[par][:, :, 64], in1=esk, op=ALU.add), [f1, tes])
                f3 = P.op("vector", lambda e: e.reciprocal(out=sden[par], in_=sden[par]), [f2])
                f4 = None
                for hh in range(8):
                    f4 = P.op("vector", lambda e, hh=hh: e.tensor_scalar(out=ysw_t[par][:, hh, :], in0=saccs[par][:, hh, 0:64], scalar1=sden[par][:, hh:hh + 1], scalar2=None, op0=ALU.mult),
                              [f3] + ([sw["ysw_t_free"][par]] if hh == 0 else []))
                def partB():
                    tp = None
                    for cc in range(4):
                        tp = P.op("tensor", lambda e, cc=cc: e.transpose(psT[par][:, cc * 128:(cc + 1) * 128], ysw_t[par][:, 2 * cc:2 * cc + 2, :].rearrange("p a b -> p (a b)"), identb),
                                  [f4] + ([state["pst_free"][0]] if cc == 0 else []))
                    f5 = P.op("vector", lambda e: e.tensor_copy(out=yswg[par], in_=psT[par][:, 0:512].rearrange("p (a b) -> p a b", b=128)), [tp, sw["yswg_free"][par]])
                    state["pst_free"][0] = f5
                    sw["ysw_t_free"][par] = f5
                    sw["yswg_free"][par] = P.dma("sync", ysw_d.rearrange("(c p) t -> p c t", p=128)[:, :, n * 128:(n + 1) * 128], yswg[par], [f5])
                sw["pendB"] = partB

            sw_qk(0)
            sw_qk(1)
            for idx in range(len(steps)):
                if idx + 2 < len(steps):
                    sw_qk(idx + 2)
                sw_soft(idx)
                n, g, hf = steps[idx]
                if g == 0 and hf == 0 and sw["pendA"] is not None:
                    sw["pendA"]()
                    sw["pendA"] = None
                sw_pv(idx)
                if g == 0 and hf == 0 and sw["pendB"] is not None:
                    sw["pendB"]()
                    sw["pendB"] = None
                if g == 1 and hf == 1:
                    sw["pendA"] = (lambda n=n, lp=sw["last_pv"]: sw_final(n, lp))
            if sw["pendA"] is not None:
                sw["pendA"]()
                sw["pendA"] = None
            if sw["pendB"] is not None:
                sw["pendB"]()
                sw["pendB"] = None
            state["it"] += len(steps)
            prog_barrier(P, bar)

        def stage_mix(l, x_in, x_mid):
            A.reset()
            NS_ = NSF
            wbda = A.alloc([128, 4, D], BF16)
            wbsw = A.alloc([128, 4, D], BF16)
            wmix = A.alloc([128, 8, D], BF16)
            yds = [A.alloc([128, 4, 512], BF16) for i in range(2)]
            yss = [A.alloc([128, 4, 512], BF16) for i in range(2)]
            sg = [A.alloc([128, 16, 512], BF16) for i in range(2)]
            mg = [A.alloc([128, 8, 512], BF16) for i in range(2)]
            m1 = [A.alloc([128, 512], F32) for i in range(2)]
            m2 = [A.alloc([128, 512], F32) for i in range(2)]
            xt = [A.alloc([128, D], F32) for i in range(2)]
            xm = [A.alloc([128, D], F32) for i in range(2)]
            scr = A.alloc([128, D], F32)
            xn = [A.alloc([128, D], BF16) for i in range(2)]
            ss = [A.alloc([128, 4], F32) for i in range(2)]
            xnT = [A.alloc([128, 8, 128], BF16) for i in range(2)]
            wn = A.alloc([128, 8], F32)
            psA = bankf[0:2]
            psB = bankf[2:4]
            psM = bankf[4:6]
            pst = bankb[6:8]
            t_wn = P.dma("sync", wn, nw2_d[l].rearrange("(c p) -> p c", p=128), slow=True)
            lw = []
            for k in range(4):
                lw.append(P.dma("gpsimd", wbda[:, k, :], wbda_d[l, k * 128:(k + 1) * 128, :]))
                lw.append(P.dma("gpsimd", wbsw[:, k, :], wbsw_d[l, k * 128:(k + 1) * 128, :]))
            lm = [P.dma("gpsimd", wmix[:, k, :], wmix_d[l, k * 128:(k + 1) * 128, :]) for k in range(8)]
            P.wait_only("tensor", lw)
            P.wait_only("vector", [t_wn])
            sg_src = fm_d[13 * 128:29 * 128, :].rearrange("(c p) t -> p c t", p=128)
            yd_src = yda_d.rearrange("(c p) t -> p c t", p=128)
            ys_src = ysw_d.rearrange("(c p) t -> p c t", p=128)
            xn_dst = xn2_d.rearrange("(c p) t -> p c t", p=128)
            sg_free = [None, None]
            y_free = [None, None]
            mg_free = [None, None]
            a_free = [None, None]
            b_free = [None, None]
            m_free = [None, None]
            m1_free = [None, None]
            xt_free = [None, None]
            xm_free = [None, None]
            xm_rd = [None, None]
            xn_free = [None, None]
            pst_free = [None, None]
            xnT_free = [None, None]
            mst = {"cnt": 0, "mcnt": 0, "mg_tok": {}, "tb": {}}

            def gate_slot(s, js=range(8)):
                sp = s % 2
                if 0 in js:
                    mst["lsg"] = P.dma("sync", sg[sp], sg_src[:, :, s * 512:(s + 1) * 512], [sg_free[sp]])
                    mst["lyd"] = P.dma("sync", yds[sp], yd_src[:, :, s * 512:(s + 1) * 512], [y_free[sp]])
                    mst["lys"] = P.dma("sync", yss[sp], ys_src[:, :, s * 512:(s + 1) * 512], [y_free[sp]])
                    mst["mg_tok"][s] = []
                lsg, lyd, lys = mst["lsg"], mst["lyd"], mst["lys"]
                mg_tok = mst["mg_tok"][s]
                tb = None
                for j in js:
                    pb = mst["cnt"] % 2
                    mst["cnt"] += 1
                    ta = tb = None
                    for k in range(4):
                        ta = P.op("tensor", lambda e, k=k, pb=pb, j=j, sp=sp: e.matmul(psA[pb], lhsT=wbda[:, k, j * 128:(j + 1) * 128], rhs=yds[sp][:, k, :], start=(k == 0), stop=(k == 3)),
                                  [a_free[pb], lyd] if k == 0 else [])
                    for k in range(4):
                        tb = P.op("tensor", lambda e, k=k, pb=pb, j=j, sp=sp: e.matmul(psB[pb], lhsT=wbsw[:, k, j * 128:(j + 1) * 128], rhs=yss[sp][:, k, :], start=(k == 0), stop=(k == 3)),
                                  [b_free[pb], lys] if k == 0 else [])
                    d1 = P.op("vector", lambda e, pb=pb, j=j, sp=sp: e.tensor_tensor(out=m1[pb], in0=psA[pb], in1=sg[sp][:, j, :], op=ALU.mult), [ta, lsg, m1_free[pb]])
                    a_free[pb] = d1
                    d2 = P.op("vector", lambda e, pb=pb, j=j, sp=sp: e.tensor_tensor(out=m2[pb], in0=psB[pb], in1=sg[sp][:, 8 + j, :], op=ALU.mult), [tb, lsg])
                    b_free[pb] = d2
                    d3 = P.op("vector", lambda e, pb=pb, j=j, sp=sp: e.tensor_tensor(out=mg[sp][:, j, :], in0=m1[pb], in1=m2[pb], op=ALU.add), [d1, d2] + ([mg_free[sp]] if j == 0 else []))
                    m1_free[pb] = d3
                    mg_tok.append(d3)
                if 7 in js:
                    sg_free[sp] = mg_tok[-1]
                    y_free[sp] = tb

            def mix_slot(s, tts=range(4)):
                sp = s % 2
                mg_tok = mst["mg_tok"][s]
                last_mm = None
                for tt4 in tts:
                    tt = s * 4 + tt4
                    xb = tt % 2
                    lx = P.dma("sync", xt[xb], x_in[tt * 128:(tt + 1) * 128, :], [xt_free[xb]])
                    adds = []
                    for half in range(2):
                        pb = mst["mcnt"] % 2
                        mst["mcnt"] += 1
                        mm = None
                        for k in range(8):
                            mm = P.op("tensor", lambda e, k=k, pb=pb, tt4=tt4, half=half, sp=sp: e.matmul(
                                psM[pb], lhsT=mg[sp][:, k, tt4 * 128:(tt4 + 1) * 128], rhs=wmix[:, k, half * 512:(half + 1) * 512], start=(k == 0), stop=(k == 7)),
                                ([m_free[pb]] + mg_tok + lm) if k == 0 else [])
                        last_mm = mm
                        ad = P.op("vector", lambda e, pb=pb, xb=xb, half=half: e.tensor_tensor(out=xm[xb][:, half * 512:(half + 1) * 512], in0=psM[pb], in1=xt[xb][:, half * 512:(half + 1) * 512], op=ALU.add),
                                  [mm, lx, xm_free[xb], xm_rd[xb]])
                        m_free[pb] = ad
                        adds.append(ad)
                    xt_free[xb] = adds[-1]
                    st = P.dma("sync", x_mid[tt * 128:(tt + 1) * 128, :], xm[xb], adds)
                    flush_pending()
                    P.wait_only("scalar", [xn_free[xb]])
                    fin, t4 = emit_rmsnorm_T(P, xm[xb], adds[-1], scr, ss[xb], xn[xb], pst[xb], identb, wn,
                                             lambda c, xb=xb: xnT[xb][:, c, :], eps_t, defer=True, on_act=True)
                    xm_free[xb] = st
                    xm_rd[xb] = t4
                    mst["pending"] = (fin, xb, tt)
                if 3 in tts:
                    mg_free[sp] = last_mm

            def flush_pending():
                if mst.get("pending") is None:
                    return
                fin, xb, tt = mst["pending"]
                mst["pending"] = None
                toks = fin([pst_free[xb]], [xnT_free[xb]])
                xn_free[xb] = toks[-1]
                pst_free[xb] = toks[-1]
                xnT_free[xb] = P.dma("sync", xn_dst[:, :, tt * 128:(tt + 1) * 128], xnT[xb], toks)

            gate_slot(0)
            for s in range(NS_):
                for q in range(4):
                    if s + 1 < NS_:
                        gate_slot(s + 1, range(2 * q, 2 * q + 2))
                    mix_slot(s, range(q, q + 1))
            flush_pending()
            prog_barrier(P, bar)

        def stage_ffn(l, half_i, x_mid, x_out, final):
            A.reset()
            TH = 2048
            NTH = 16
            NSH = 4
            t0 = half_i * TH
            big = A.alloc([128, 8 * (TH + 2)], BF16)
            xnx = big.rearrange("p (k t) -> p k t", t=TH + 2)
            wd1 = big[:, 0:22 * 512].rearrange("p (i c) -> p i c", c=512)
            wd0 = A.alloc([128, 22, 512], BF16)
            hT = A.alloc([128, 22, TH], BF16)
            wug = [A.alloc([128, 8, 256], BF16) for i in range(2)]
            wuv = [A.alloc([128, 8, 256], BF16) for i in range(2)]
            wk = A.alloc([128, 6, 512], F32)
            accg = [wk[:, i, :] for i in range(2)]
            accv = [wk[:, 2 + i, :] for i in range(2)]
            sgt = [wk[:, 4 + i, :] for i in range(2)]
            uc = {(w, p): A.alloc([128, 514], F32) for w in range(2) for p in range(2)}
            cw = A.alloc([128, 3, 44], F32)
            cb = A.alloc([128, 44], F32)
            xt = [A.alloc([128, 512], F32) for i in range(3)]
            xo = [A.alloc([128, D], F32) for i in range(2)]
            fw = A.alloc([128, D], F32) if final else None
            ss = [A.alloc([128, 4], F32) for i in range(2)]
            scr = wk[:, 0:2, :].rearrange("p a b -> p (a b)")
            psU = [bankf[0], bankf[1], bankf[2], bankf[3], bankf[5], bankf[6]]
            NU = 6
            psH = bankf[4]
            psD = bankf[5:8]
            if half_i == 0:
                lx0 = P.op("gpsimd", lambda e: e.memset(xnx[:, :, 0:2], 0.0))
                lx = [P.dma("sync", xnx[:, k, 2:TH + 2], xn2_d[k * 128:(k + 1) * 128, 0:TH]) for k in range(8)] + [lx0]
            else:
                lx = [P.dma("sync", xnx[:, k, :], xn2_d[k * 128:(k + 1) * 128, t0 - 2:t0 + TH]) for k in range(8)]
            lcw = [P.dma("sync", cw[:, j, :], cw_d[l, j].rearrange("(c p) -> p c", p=128), slow=True) for j in range(3)]
            lcb = P.dma("sync", cb, cb_d[l].rearrange("(c p) -> p c", p=128), slow=True)
            lfw = P.dma("sync", fw, fnw_d.partition_broadcast(128)) if final else None
            P.wait_only("vector", lcw + [lcb])
            P.wait_only("scalar", lcw + [lcb])
            lwd0 = []
            wu_free = [None, None]
            u_free = [None] * NU
            h_free = None
            pend = {"f": None}
            acc_free = {0: [None, None], 1: [None, None]}
            sgt_free = [None, None]
            uc_free = {k: [] for k in uc}
            ucnt = 0
            hT_tok = []
            last_up_mm = None
            for gi in range(11):
                i0, n = 2 * gi, 2
                wb = gi % 2
                lg = [P.dma("gpsimd", wug[wb][:, k, :], wup_d[l, k * 128:(k + 1) * 128, i0 * 128:(i0 + n) * 128], [wu_free[wb]]) for k in range(8)]
                lv = [P.dma("gpsimd", wuv[wb][:, k, :], wup_d[l, k * 128:(k + 1) * 128, DFF + i0 * 128:DFF + (i0 + n) * 128], [wu_free[wb]]) for k in range(8)]
                if gi == 1:
                    lwd0.extend([P.dma("gpsimd", wd0[:, i, :], wdn_d[l, i * 128:(i + 1) * 128, 0:512]) for i in range(22)])
                for ci in range(n):
                    i = i0 + ci
                    hm = None
                    for which, wt in ((0, wug[wb]), (1, wuv[wb])):
                        for k in range(8):
                            hm = P.op("tensor", lambda e, k=k, wt=wt, ci=ci, which=which: e.matmul(
                                psH[:, which * 2:which * 2 + 2], lhsT=wt[:, k, ci * 128:(ci + 1) * 128], rhs=xnx[:, k, 0:2], start=(k == 0), stop=(k == 7)),
                                ([h_free] + lg + lv + lx) if (k == 0 and which == 0) else [])
                    halo = {}
                    for which in range(2):
                        halo[(which, 0)] = P.op("vector", lambda e, which=which: e.tensor_copy(out=uc[(which, 0)][:, 0:2], in_=psH[:, which * 2:which * 2 + 2]),
                                                [hm] + uc_free[(which, 0)])
                    h_free = halo[(1, 0)]
                    for s in range(NSH):
                        p = s % 2
                        dlast = {}
                        for which, wt in ((0, wug[wb]), (1, wuv[wb])):
                            ub = ucnt % NU
                            ucnt += 1
                            ch = i if which == 0 else 22 + i
                            mm = None
                            for k in range(8):
                                mm = P.op("tensor", lambda e, k=k, wt=wt, ci=ci, ub=ub, s=s: e.matmul(
                                    psU[ub], lhsT=wt[:, k, ci * 128:(ci + 1) * 128], rhs=xnx[:, k, 2 + s * 512:2 + (s + 1) * 512], start=(k == 0), stop=(k == 7)),
                                    [u_free[ub]] if k == 0 else [])
                            last_up_mm = mm
                            ucb = uc[(which, p)]
                            acc = accg[p] if which == 0 else accv[p]
                            a0 = P.op("scalar", lambda e, ucb=ucb, ub=ub: e.activation(out=ucb[:, 2:514], in_=psU[ub], func=AF.Copy), [mm] + uc_free[(which, p)])
                            frees = []
                            if s < NSH - 1:
                                hcp = P.op("scalar", lambda e, ucb=ucb, which=which, p=p: e.activation(out=uc[(which, 1 - p)][:, 0:2], in_=ucb[:, 512:514], func=AF.Copy), [a0] + uc_free[(which, 1 - p)])
                                halo[(which, s + 1)] = hcp
                                frees.append(hcp)
                            d0 = P.op("scalar", lambda e, acc=acc, ub=ub, ch=ch: e.activation(out=acc, in_=psU[ub], func=AF.Identity, bias=cb[:, ch:ch + 1], scale=cw[:, 2, ch:ch + 1]),
                                      [mm, acc_free[which][p]])
                            u_free[ub] = d0
                            d1 = P.op("vector", lambda e, acc=acc, ucb=ucb, ch=ch: e.scalar_tensor_tensor(out=acc, in0=ucb[:, 1:513], scalar=cw[:, 1, ch:ch + 1], in1=acc, op0=ALU.mult, op1=ALU.add),
                                      [d0, a0, halo[(which, s)]])
                            d2 = P.op("vector", lambda e, acc=acc, ucb=ucb, ch=ch: e.scalar_tensor_tensor(out=acc, in0=ucb[:, 0:512], scalar=cw[:, 0, ch:ch + 1], in1=acc, op0=ALU.mult, op1=ALU.add), [d1])
                            frees.append(d2)
                            uc_free[(which, p)] = frees
                            dlast[which] = d2
                        def fin_glu(p=p, i=i, s=s, dg_=dlast[0], dv_=dlast[1]):
                            a2 = P.op("scalar", lambda e: e.activation(out=sgt[p], in_=accg[p], func=AF.Silu), [dg_, sgt_free[p]])
                            g1 = P.op("vector", lambda e: e.tensor_tensor(out=hT[:, i, s * 512:(s + 1) * 512], in0=sgt[p], in1=accv[p], op=ALU.mult), [a2, dv_])
                            acc_free[0][p] = a2
                            acc_free[1][p] = g1
                            sgt_free[p] = g1
                            hT_tok.append(g1)
                        if pend["f"] is not None:
                            pend["f"]()
                        pend["f"] = fin_glu
                wu_free[wb] = last_up_mm
            if pend["f"] is not None:
                pend["f"]()
                pend["f"] = None
            lwd1 = [P.dma("gpsimd", wd1[:, i, :], wdn_d[l, i * 128:(i + 1) * 128, 512:1024], [last_up_mm]) for i in range(22)]
            P.wait_only("tensor", hT_tok[-8:] + lwd0)
            d_free = [u_free[4], u_free[5], None]
            xt_free = [None] * 3
            xo_free = [None, None]
            dcnt = 0
            for tt in range(NTH):
                ob = tt % 2
                adds = []
                for half in range(2):
                    pb = dcnt % 3
                    dcnt += 1
                    wd = wd0 if half == 0 else wd1
                    lxt = P.dma("sync", xt[pb], x_mid[t0 + tt * 128:t0 + (tt + 1) * 128, half * 512:(half + 1) * 512], [xt_free[pb]])
                    mm = None
                    for i in range(22):
                        deps = []
                        if i == 0:
                            deps = [d_free[pb]] + (lwd1 if half == 1 else [])
                        mm = P.op("tensor", lambda e, i=i, pb=pb, tt=tt, wd=wd: e.matmul(psD[pb], lhsT=hT[:, i, tt * 128:(tt + 1) * 128], rhs=wd[:, i, :], start=(i == 0), stop=(i == 21)), deps)
                    ad = P.op("vector", lambda e, pb=pb, ob=ob, half=half: e.tensor_tensor(out=xo[ob][:, half * 512:(half + 1) * 512], in0=psD[pb], in1=xt[pb], op=ALU.add), [mm, lxt, xo_free[ob]])
                    d_free[pb] = ad
                    xt_free[pb] = ad
                    adds.append(ad)
                if final:
                    t1 = P.op("scalar", lambda e, ob=ob: e.activation(out=scr, in_=xo[ob], func=AF.Square, accum_out=ss[ob][:, 0:1]), adds)
                    t2 = P.op("scalar", lambda e, ob=ob: e.activation(out=ss[ob][:, 1:2], in_=ss[ob][:, 0:1], func=AF.Ln, bias=eps_t[:, 0:1], scale=1.0 / D), [t1])
                    t3 = P.op("scalar", lambda e, ob=ob: e.activation(out=ss[ob][:, 2:3], in_=ss[ob][:, 1:2], func=AF.Exp, scale=-0.5), [t2])
                    t4 = P.op("vector", lambda e, ob=ob: e.scalar_tensor_tensor(out=xo[ob], in0=xo[ob], scalar=ss[ob][:, 2:3], in1=fw, op0=ALU.mult, op1=ALU.mult), [t3, lfw])
                    adds = [t4]
                xo_free[ob] = P.dma("sync", x_out[t0 + tt * 128:t0 + (tt + 1) * 128, :], xo[ob], adds)
            prog_barrier(P, bar)

        x_cur = x_ext
        for l in range(depth):
            lam_init = 0.8 - 0.6 * math.exp(-0.3 * l)
            stage_proj(l, x_cur)
            stage_attn(l, lam_init)
            stage_mix(l, x_cur, xA)
            final = (l == depth - 1)
            x_next = out_d if final else xB
            for hi in range(2):
                stage_ffn(l, hi, xA, x_next, final)
            x_cur = x_next
        P.emit(None)
    return nc


FF_GROUPS = [(0, 4), (4, 4), (8, 4), (12, 4), (16, 4), (20, 2)]
_CACHE = {}


def kernel(**inputs):
    inp = {k: np.asarray(v) for k, v in inputs.items()}
    f32 = lambda a: np.ascontiguousarray(a, dtype=np.float32)
    dg, dd, dsw = make_tables()
    shared = {
        "w_in": f32(inp["w_in"]), "nw1": f32(inp["norm_mix_w"]),
        "lam4": f32(np.stack([inp["lambda_q1"], inp["lambda_k1"], inp["lambda_q2"], inp["lambda_k2"]], axis=1)),
        "subln": f32(inp["subln_w"]), "sinks": f32(inp["sinks"]),
        "w_br_da": f32(inp["w_br_da"]), "w_br_sw": f32(inp["w_br_sw"]), "w_mix": f32(inp["w_mix_out"]), "nw2": f32(inp["norm_ffn_w"]),
        "w_up": f32(inp["w_up"]), "conv_w": f32(inp["conv_w"]), "conv_b": f32(inp["conv_b"]), "w_down": f32(inp["w_down"]),
        "fnw": f32(inp["norm_final_w"]), "ident": np.eye(128, dtype=np.float32),
        "dg": dg, "dd": dd, "dsw": np.ascontiguousarray(dsw.reshape(128, 16, 128)), "btab": make_btab_full(),
    }
    x = f32(inp["x"])
    if "nc" not in _CACHE:
        _CACHE["nc"] = build_fused()
    nc = _CACHE["nc"]
    zero = {k: np.zeros_like(v) for k, v in shared.items()}
    in_maps = []
    for c in range(2 * B):
        if c % 2 == 0:
            m = dict(shared)
            m["x"] = np.ascontiguousarray(x[c // 2])
        else:
            m = dict(zero)
            m["x"] = np.zeros((TF, D), np.float32)
        in_maps.append(m)
    res = run_bass_kernel_spmd(nc, in_maps, core_ids=list(range(2 * B))).results
    out = np.stack([res[2 * b]["out"] for b in range(B)], axis=0)
    return np.ascontiguousarray(out, dtype=np.float32)
```

```python
import math
import numpy as np
import ml_dtypes
import concourse.bass as bass
import concourse.mybir as mybir
from concourse.bass_utils import run_bass_kernel_spmd

F32 = mybir.dt.float32
BF16 = mybir.dt.bfloat16
AF = mybir.ActivationFunctionType
ALU = mybir.AluOpType
BF = ml_dtypes.bfloat16

D = 1024
S = 4096
B = 4
DEPTH = 4
T = 2048
NT = T // 128
NS = T // 512
INC = 4352
DFF = 2816
EPS = 1e-6
NEG = -30000.0

ENGS = ("tensor", "vector", "scalar", "gpsimd", "sync")


class Prog:
    def __init__(self, nc, n_dma_sems=24):
        self.nc = nc
        self.streams = {e: [] for e in ENGS}
        self.count = {e: 0 for e in ENGS}
        self.n_dma = n_dma_sems
        self.dma_cnt = [0] * n_dma_sems
        self.pool = {"gpsimd": list(range(0, 8)), "sync": list(range(8, n_dma_sems - 4)), "scalar": list(range(n_dma_sems - 4, n_dma_sems))}
        self.dma_rr = {"gpsimd": 0, "sync": 0, "scalar": 0}
        self.all_dma = []

    def op(self, eng, fn, deps=()):
        self.count[eng] += 1
        tok = ("e", eng, self.count[eng])
        self.streams[eng].append(("op", fn, [d for d in deps if d is not None]))
        return tok

    def dma(self, queue, out, in_, deps=(), slow=False):
        pl = self.pool[queue]
        i = pl[self.dma_rr[queue] % len(pl)]
        self.dma_rr[queue] += 1
        prev = ("d", i, self.dma_cnt[i]) if self.dma_cnt[i] > 0 else None
        self.dma_cnt[i] += 1
        tok = ("d", i, self.dma_cnt[i])
        dl = [d for d in deps if d is not None]
        if prev is not None:
            dl.append(prev)
        self.streams[queue].append(("dma", (out, in_, i, slow), dl))
        self.all_dma.append(tok)
        return tok

    def wait_only(self, eng, deps):
        self.streams[eng].append(("wait", None, [d for d in deps if d is not None]))

    def emit(self, block):
        nc = self.nc
        import contextlib
        with contextlib.ExitStack() as es:
            esem = {e: es.enter_context(nc.semaphore("s_" + e)) for e in ENGS}
            dsem = [es.enter_context(nc.semaphore("d_%d" % i)) for i in range(self.n_dma)]
            blk = es.enter_context(nc.Block())

            def replay(name, eng):
                waited = {}

                def do_waits(deps):
                    for d in deps:
                        if d[0] == "e":
                            key = ("e", d[1])
                            val = d[2]
                            sem = esem[d[1]]
                        else:
                            key = ("d", d[1])
                            val = d[2] * 16
                            sem = dsem[d[1]]
                        if waited.get(key, 0) >= val:
                            continue
                        waited[key] = val
                        eng.wait_ge(sem, val)

                for kind, payload, deps in self.streams[name]:
                    do_waits(deps)
                    if kind == "op":
                        payload(eng).then_inc(esem[name], 1)
                    elif kind == "dma":
                        out, in_, i, slow = payload
                        if slow:
                            eng.dma_start(out=out, in_=in_, allow_slow_non_contiguous=True).then_inc(dsem[i], 16)
                        else:
                            eng.dma_start(out=out, in_=in_).then_inc(dsem[i], 16)

            final = list(self.all_dma)

            @blk.tensor
            def _(e):
                replay("tensor", e)

            @blk.vector
            def _(e):
                replay("vector", e)

            @blk.scalar
            def _(e):
                replay("scalar", e)

            @blk.gpsimd
            def _(e):
                replay("gpsimd", e)

            @blk.sync
            def _(e):
                replay("sync", e)
                for i in range(self.n_dma):
                    if self.dma_cnt[i] > 0:
                        e.wait_ge(dsem[i], self.dma_cnt[i] * 16)
                for en in ENGS:
                    if en != "sync" and self.count[en] > 0:
                        e.wait_ge(esem[en], self.count[en])


def _run(nc, in_maps):
    res = run_bass_kernel_spmd(nc, in_maps, core_ids=list(range(len(in_maps))))
    return res.results


def emit_rmsnorm_T(P, xt_ap, xt_tok, scr, ss, xn, pst, identb, wn, dst_fn, eps_t, n_feat=1024, defer=False, on_act=False):
    t1 = P.op("scalar", lambda e: e.activation(out=scr, in_=xt_ap, func=AF.Square, accum_out=ss[:, 0:1]), [xt_tok])
    t2 = P.op("scalar", lambda e: e.activation(out=ss[:, 1:2], in_=ss[:, 0:1], func=AF.Ln, bias=eps_t[:, 0:1], scale=1.0 / n_feat), [t1])
    t3 = P.op("scalar", lambda e: e.activation(out=ss[:, 2:3], in_=ss[:, 1:2], func=AF.Exp, scale=-0.5), [t2])
    if on_act:
        t4 = P.op("scalar", lambda e: e.activation(out=xn, in_=xt_ap, func=AF.Copy, scale=ss[:, 2:3]), [t3, xt_tok])
    else:
        t4 = P.op("vector", lambda e: e.tensor_scalar(out=xn, in0=xt_ap, scalar1=ss[:, 2:3], scalar2=None, op0=ALU.mult), [t3, xt_tok])

    def finish(extra_pe_deps=(), extra_ev_deps=()):
        toks = []
        tp = []
        for c in range(n_feat // 128):
            tp.append(P.op("tensor", lambda e, c=c: e.transpose(pst[:, c * 128:(c + 1) * 128], xn[:, c * 128:(c + 1) * 128], identb),
                           [t4] + (list(extra_pe_deps) if c == 0 else [])))
        for c in range(n_feat // 128):
            if on_act:
                toks.append(P.op("scalar", lambda e, c=c: e.activation(out=dst_fn(c), in_=pst[:, c * 128:(c + 1) * 128], func=AF.Copy, scale=wn[:, c:c + 1]),
                                 [tp[-1]] + (list(extra_ev_deps) if c == 0 else [])))
            else:
                toks.append(P.op("vector", lambda e, c=c: e.tensor_scalar(out=dst_fn(c), in0=pst[:, c * 128:(c + 1) * 128], scalar1=wn[:, c:c + 1], scalar2=None, op0=ALU.mult),
                                 [tp[-1]] + (list(extra_ev_deps) if c == 0 else [])))
        return toks

    if defer:
        return finish, t4
    return finish(), t4


def da_slopes():
    return [2.0 ** (-2 * (h + 1)) for h in range(4)]


def sw_slopes():
    return [2.0 ** (-(h + 1)) for h in range(8)]


def nkb(s):
    return 16 + 4 * (s + 1)


def make_tables():
    jj = np.arange(128, dtype=np.float64)[:, None]
    ii = np.arange(512, dtype=np.float64)[None, :]
    dg = np.zeros((128, 4, 512), np.float32)
    dd = np.zeros((128, 4, 4, 512), np.float32)
    for h, sl in enumerate(da_slopes()):
        dg[:, h, :] = np.exp(sl * (jj - 127 - ii))
        for d in range(4):
            rel = 128 * d + jj - ii
            if h == 0:
                dd[:, h, d, :] = np.where(rel <= 0, np.exp(sl * np.minimum(rel, 0)), 0.0)
            else:
                dd[:, h, d, :] = np.where(rel <= 0, np.exp(sl * (128 * d + jj)) * np.ones_like(ii), 0.0)
    i2 = np.arange(128, dtype=np.float64)[None, :]
    dsw = np.zeros((128, 2, 2, 2, 2, 128), np.float32)
    for g in range(2):
        for hf in range(2):
            for j in range(2):
                hh = 4 * g + hf + 2 * j
                sl = sw_slopes()[hh]
                dist_p = 128 + i2 - jj
                dsw[:, g, hf, 0, j, :] = np.where(jj > i2, np.exp(-sl * dist_p), 0.0)
                dist_d = i2 - jj
                dsw[:, g, hf, 1, j, :] = np.where(jj <= i2, np.exp(-sl * np.maximum(dist_d, 0)), 0.0)
    return dg.astype(BF), dd.astype(BF), dsw.astype(BF)


def make_btab(half):
    cols = []
    for s in range(NS):
        for h, sl in enumerate(da_slopes()):
            for kb in range(nkb(s)):
                if kb < 16:
                    if half == 0:
                        v = NEG
                    else:
                        v = sl * (128 * kb - 2048 - 512 * s + 127)
                else:
                    m = kb - 16
                    if m < 4 * s:
                        v = sl * (128 * m - 512 * s + 127)
                    else:
                        v = 0.0
                cols.append(v)
    cols.append(NEG if half == 0 else 0.0)
    t = np.tile(np.asarray(cols, np.float32)[None, :], (128, 1))
    return np.ascontiguousarray(t)


TF = 4096
NTF = TF // 128
NSF = TF // 512


def nkbf(s):
    return 4 * (s + 1)


DEAD = -100.0


def kb_first(s, h):
    sl = da_slopes()[h]
    m = 0
    while m < 4 * s and sl * (128 * m - 512 * s + 127) < DEAD:
        m += 1
    return m


NBTF = sum(nkbf(s) - kb_first(s, h) for s in range(NSF) for h in range(4)) + 1


def make_btab_full():
    cols = []
    jj = np.arange(128, dtype=np.float64)
    for s in range(NSF):
        for h, sl in enumerate(da_slopes()):
            for m in range(kb_first(s, h), nkbf(s)):
                if m >= 4 * s:
                    cols.append(np.zeros(128))
                elif h == 0:
                    cols.append(np.full(128, sl * (128 * m - 512 * s + 127)))
                else:
                    cols.append(sl * (128 * m - 512 * s + jj))
    cols.append(np.full(128, NEG))
    return np.ascontiguousarray(np.stack(cols, axis=1).astype(np.float32))


class Arena:
    def __init__(self, base_ap, nbytes):
        self.base = base_ap
        self.nbytes = nbytes
        self.off = 0
        self.mark = 0

    def alloc(self, shape, dt):
        assert shape[0] == 128
        esz = 4 if dt == F32 else 2
        n = 1
        for v in shape[1:]:
            n *= v
        nb = (n * esz + 63) // 64 * 64
        assert self.off + nb <= self.nbytes, ("arena overflow", self.off, nb, self.nbytes)
        ap = self.base[:, self.off // 4:(self.off + nb) // 4]
        self.off += nb
        if dt != F32:
            ap = ap.bitcast(dt)
        ap = ap[:, 0:n]
        if len(shape) == 3:
            ap = ap.rearrange("p (a b) -> p a b", b=shape[2])
        elif len(shape) == 4:
            ap = ap.rearrange("p (a b c) -> p a b c", b=shape[2], c=shape[3])
        return ap

    def set_mark(self):
        self.mark = self.off

    def reset(self):
        self.off = self.mark


def prog_barrier(P, bar):
    deps = [("e", e, P.count[e]) for e in ENGS if P.count[e] > 0] + [("d", i, P.dma_cnt[i]) for i in range(P.n_dma) if P.dma_cnt[i] > 0]
    tok = P.dma("sync", bar[1:2, :], P.bar_src, deps)
    for e in ("tensor", "vector", "scalar", "gpsimd"):
        P.wait_only(e, [tok])
    return tok


FM_CHUNKS = [(j, "q") for j in range(0, 4)] + [(j, "k") for j in range(4, 8)] + \
            [(j, "q") for j in range(12, 16)] + [(16, "k")] + [(j, "g") for j in range(18, 34)]


def build_fused(depth=DEPTH, debug_out=None):
    nc = bass.Bass("TRN2", target_bir_lowering=False)
    dt_in = lambda name, shape, dt: nc.dram_tensor(name, shape, dt, kind="ExternalInput").ap()
    dt_sc = lambda name, shape, dt: nc.dram_tensor(name, shape, dt, kind="Internal").ap()
    x_ext = dt_in("x", [TF, D], F32)
    w_in_d = dt_in("w_in", [depth, D, INC], F32)
    nw1_d = dt_in("nw1", [depth, D], F32)
    lam4_d = dt_in("lam4", [depth, 4, 64], F32)
    subln_d = dt_in("subln", [depth, 128], F32)
    sinks_d = dt_in("sinks", [depth, 8], F32)
    wbda_d = dt_in("w_br_da", [depth, 512, D], F32)
    wbsw_d = dt_in("w_br_sw", [depth, 512, D], F32)
    wmix_d = dt_in("w_mix", [depth, D, D], F32)
    nw2_d = dt_in("nw2", [depth, D], F32)
    wup_d = dt_in("w_up", [depth, D, 2 * DFF], F32)
    cw_d = dt_in("conv_w", [depth, 3, 2 * DFF], F32)
    cb_d = dt_in("conv_b", [depth, 2 * DFF], F32)
    wdn_d = dt_in("w_down", [depth, DFF, D], F32)
    fnw_d = dt_in("fnw", [D], F32)
    ident_d = dt_in("ident", [128, 128], F32)
    dg_d = dt_in("dg", [128, 4, 512], BF16)
    dd_d = dt_in("dd", [128, 4, 4, 512], BF16)
    dsw_d = dt_in("dsw", [128, 16, 128], BF16)
    btab_d = dt_in("btab", [128, NBTF], F32)
    out_d = nc.dram_tensor("out", [TF, D], F32, kind="ExternalOutput").ap()
    xA = dt_sc("xA", [TF, D], F32)
    xB = dt_sc("xB", [TF, D], F32)
    fm_d = dt_sc("fm", [29 * 128, TF], BF16)
    vda_d = dt_sc("vda", [TF, 516], BF16)
    vsw_d = dt_sc("vsw", [TF, 130], BF16)
    yda_d = dt_sc("yda", [512, TF], BF16)
    ysw_d = dt_sc("ysw", [512, TF], BF16)
    xn2_d = dt_sc("xn2", [D, TF], BF16)
    bar = dt_sc("bar", [2, 16], F32)
    ARENA_BYTES = 206 * 1024
    import contextlib
    with contextlib.ExitStack() as es:
        arena_t = es.enter_context(nc.sbuf_tensor("arena", [128, ARENA_BYTES // 4], F32))
        bank = [es.enter_context(nc.psum_tensor("bank%d" % i, [128, 512], F32)) for i in range(8)]
        bankf = [b[:] for b in bank]
        bankb = [b[:].bitcast(BF16) for b in bank]
        A = Arena(arena_t[:], ARENA_BYTES)
        P = Prog(nc)
        identb = A.alloc([128, 128], BF16)
        eps_t = A.alloc([128, 16], F32)
        A.set_mark()
        P.bar_src = ident_d[0:1, 0:16]
        c_id = P.dma("gpsimd", identb, ident_d)
        c_eps = P.op("vector", lambda e: e.memset(eps_t[:, 0:1], EPS))
        P.wait_only("tensor", [c_id])
        P.wait_only("scalar", [c_eps])
        P.wait_only("gpsimd", [c_eps])

        def stage_proj(l, x_in):
            A.reset()
            NT_, NS_ = NTF, NSF
            wb = A.alloc([128, 8, INC], BF16)
            xnT = A.alloc([128, 8, TF], BF16)
            xt = [A.alloc([128, D], F32) for i in range(2)]
            scr = A.alloc([128, D], F32)
            xn = [A.alloc([128, D], BF16) for i in range(4)]
            ss = [A.alloc([128, 4], F32) for i in range(4)]
            wn = A.alloc([128, 8], F32)
            stg = [A.alloc([128, 512], BF16) for i in range(4)]
            vst = [A.alloc([128, 4 * 129], BF16) for i in range(2)]
            vsst = [A.alloc([128, 2 * 65], BF16) for i in range(2)]
            ps = bankf[0:6]
            pst = bankb[6:8]
            t_wn = P.dma("sync", wn, nw1_d[l].rearrange("(c p) -> p c", p=128), slow=True)
            t_ones = [P.op("vector", lambda e, i=i: e.memset(vst[i], 1.0)) for i in range(2)]
            t_ones2 = [P.op("vector", lambda e, i=i: e.memset(vsst[i], 1.0)) for i in range(2)]
            wtok = {}
            for q in range(4):
                for k in range(8):
                    c0 = q * 1088
                    wtok[(k, q)] = P.dma("gpsimd", wb[:, k, c0:c0 + 1088], w_in_d[l, k * 128:(k + 1) * 128, c0:c0 + 1088])

            def wdeps(c0, c1):
                qs = sorted(set([c0 // 1088, (c1 - 1) // 1088]))
                return [wtok[(k, q)] for k in range(8) for q in qs]
            P.wait_only("vector", [t_wn])
            xnT_tok = [None] * NT_
            st1 = {"xt_free": [None, None], "xn_free": [None, None], "pst_free": [None, None], "psi": 0, "si": 0,
                   "ps_free": [None] * 6, "stg_free": [None] * 4, "v_free": [None, None], "vs_free": [None, None]}

            st1["xn_free"] = [None] * 4
            st1["fins"] = {}

            def norm1_slot(s):
                for tt in range(4 * s, 4 * s + 4):
                    b = tt % 2
                    q = tt % 4
                    ld = P.dma("sync", xt[b], x_in[tt * 128:(tt + 1) * 128, :], [st1["xt_free"][b]])
                    P.wait_only("scalar", [st1["xn_free"][q]])
                    fin, t4 = emit_rmsnorm_T(P, xt[b], ld, scr, ss[q], xn[q], pst[b], identb, wn,
                                             lambda c, tt=tt: xnT[:, c, tt * 128:(tt + 1) * 128], eps_t, defer=True)
                    st1["fins"][tt] = fin
                    st1["xt_free"][b] = t4

            def finish_slot(s):
                for tt in range(4 * s, 4 * s + 4):
                    b = tt % 2
                    q = tt % 4
                    toks = st1["fins"].pop(tt)([st1["pst_free"][b]])
                    xnT_tok[tt] = toks
                    st1["xn_free"][q] = toks[-1]
                    st1["pst_free"][b] = toks[-1]

            def proj_slot(s):
                for ci, (j, kind) in enumerate(FM_CHUNKS):
                    if ci == 14 and s + 1 < NS_:
                        finish_slot(s + 1)
                    pb = st1["psi"] % 6
                    st1["psi"] += 1
                    deps = wdeps(j * 128, (j + 1) * 128) + [st1["ps_free"][pb]]
                    for tt in range(s * 4, s * 4 + 4):
                        deps += xnT_tok[tt]
                    mm = None
                    for k in range(8):
                        mm = P.op("tensor", lambda e, k=k, pb=pb, j=j, s=s: e.matmul(
                            ps[pb], lhsT=wb[:, k, j * 128:(j + 1) * 128], rhs=xnT[:, k, s * 512:(s + 1) * 512],
                            start=(k == 0), stop=(k == 7)), deps if k == 0 else [])
                    sb_i = st1["si"] % 4
                    st1["si"] += 1
                    if kind == "q":
                        ev = P.op("scalar", lambda e, pb=pb, sb_i=sb_i: e.activation(out=stg[sb_i], in_=ps[pb], func=AF.Copy, scale=0.125), [mm, st1["stg_free"][sb_i]])
                    elif kind == "k":
                        ev = P.op("vector", lambda e, pb=pb, sb_i=sb_i: e.tensor_copy(out=stg[sb_i], in_=ps[pb]), [mm, st1["stg_free"][sb_i]])
                    else:
                        ev = P.op("scalar", lambda e, pb=pb, sb_i=sb_i: e.activation(out=stg[sb_i], in_=ps[pb], func=AF.Sigmoid), [mm, st1["stg_free"][sb_i]])
                    st1["ps_free"][pb] = ev
                    st1["stg_free"][sb_i] = P.dma("sync", fm_d[ci * 128:(ci + 1) * 128, s * 512:(s + 1) * 512], stg[sb_i], [ev])
                for tt in range(4 * s, 4 * s + 4):
                    b = tt % 2
                    pb = st1["psi"] % 6
                    st1["psi"] += 1
                    deps = wdeps(1024, 1536) + [st1["ps_free"][pb]] + xnT_tok[tt]
                    mm = None
                    for k in range(8):
                        mm = P.op("tensor", lambda e, k=k, pb=pb, tt=tt: e.matmul(
                            ps[pb], lhsT=xnT[:, k, tt * 128:(tt + 1) * 128], rhs=wb[:, k, 1024:1536],
                            start=(k == 0), stop=(k == 7)), deps if k == 0 else [])
                    ev = P.op("vector", lambda e, pb=pb, b=b: e.tensor_copy(
                        out=vst[b].rearrange("p (h e) -> p h e", e=129)[:, :, 0:128],
                        in_=ps[pb].rearrange("p (h e) -> p h e", e=128)), [mm, st1["v_free"][b], t_ones[b]])
                    st1["ps_free"][pb] = ev
                    st1["v_free"][b] = P.dma("sync", vda_d[tt * 128:(tt + 1) * 128, :], vst[b], [ev])
                    pb = st1["psi"] % 6
                    st1["psi"] += 1
                    deps = wdeps(2176, 2304) + [st1["ps_free"][pb]] + xnT_tok[tt]
                    for k in range(8):
                        mm = P.op("tensor", lambda e, k=k, pb=pb, tt=tt: e.matmul(
                            ps[pb][:, 0:128], lhsT=xnT[:, k, tt * 128:(tt + 1) * 128], rhs=wb[:, k, 2176:2304],
                            start=(k == 0), stop=(k == 7)), deps if k == 0 else [])
                    ev = P.op("vector", lambda e, pb=pb, b=b: e.tensor_copy(
                        out=vsst[b].rearrange("p (h e) -> p h e", e=65)[:, :, 0:64],
                        in_=ps[pb][:, 0:128].rearrange("p (h e) -> p h e", e=64)), [mm, st1["vs_free"][b], t_ones2[b]])
                    st1["ps_free"][pb] = ev
                    st1["vs_free"][b] = P.dma("sync", vsw_d[tt * 128:(tt + 1) * 128, :], vsst[b], [ev])

            norm1_slot(0)
            finish_slot(0)
            for s in range(NS_):
                if s + 1 < NS_:
                    norm1_slot(s + 1)
                proj_slot(s)
            prog_barrier(P, bar)

        def stage_attn(l, lam_init):
            A.reset()
            NS_ = NSF
            dg = A.alloc([128, 4, 512], BF16)
            dd = A.alloc([128, 4, 4, 512], BF16)
            dsw = A.alloc([128, 16, 128], BF16)
            btab = A.alloc([128, NBTF], F32)
            c_ld = [P.dma("sync", dg, dg_d), P.dma("sync", dd, dd_d), P.dma("sync", dsw, dsw_d), P.dma("sync", btab, btab_d)]
            P.wait_only("vector", c_ld)
            P.wait_only("scalar", c_ld)
            kda = A.alloc([128, 4, TF], BF16)
            vda = A.alloc([128, 32, 516], BF16)
            ksw = A.alloc([128, 2, TF], BF16)
            vsw = A.alloc([128, 32, 130], BF16)
            qs = [A.alloc([128, 4, 512], BF16) for i in range(2)]
            qsw = [A.alloc([128, 4, 512], BF16) for i in range(2)]
            lamb = A.alloc([128, 4, 64], F32)
            lscr = A.alloc([128, 64], F32)
            lsum = A.alloc([128, 4], F32)
            neglam = A.alloc([128, 1], F32)
            slnw = A.alloc([128, 1], F32)
            esk = A.alloc([128, 8], F32)
            NPB = 6
            pbuf = [A.alloc([128, 512], BF16) for i in range(NPB)]
            accs = [A.alloc([128, 9, 129], F32) for i in range(2)]
            rc = [A.alloc([128, 8], F32) for i in range(2)]
            a_t = [A.alloc([128, 4, 128], F32) for i in range(2)]
            ssq = [A.alloc([128, 12], F32) for i in range(2)]
            junk = A.alloc([128, 8, 128], F32)
            y_t = [A.alloc([128, 4, 128], BF16) for i in range(2)]
            ysg = [A.alloc([128, 512], BF16) for i in range(2)]
            saccs = [A.alloc([128, 8, 65], F32) for i in range(2)]
            sden = [A.alloc([128, 8], F32) for i in range(2)]
            ysw_t = [A.alloc([128, 8, 64], BF16) for i in range(2)]
            yswg = [A.alloc([128, 4, 128], BF16) for i in range(2)]
            NSB = 4
            psS = bankf[0:4]
            psA = bankf[4:7]
            psT = [bankb[7], bankb[7]]
            ld = {}
            ld["lam"] = P.dma("sync", lamb.rearrange("p a b -> p (a b)"), lam4_d[l].rearrange("a b -> (a b)").partition_broadcast(128))
            ld["subln"] = P.dma("sync", slnw, subln_d[l].rearrange("(p o) -> p o", o=1))
            ld["sinks"] = P.dma("sync", esk, sinks_d[l].partition_broadcast(128))
            q_src = fm_d[0:512, :].rearrange("(h p) t -> p h t", p=128)
            qsw_src = fm_d[8 * 128:12 * 128, :].rearrange("(h p) t -> p h t", p=128)
            vda_src = vda_d.rearrange("(n p) e -> p n e", p=128)
            ld["kda"] = [P.dma("sync", kda[:, 0, :], fm_d[4 * 128:5 * 128, :])]
            lq_first = P.dma("sync", qs[0], q_src[:, :, 0:512])
            ld["vda"] = [P.dma("sync", vda[:, 0:8, :], vda_src[:, 0:8, :])]
            ld["kda"] += [P.dma("sync", kda[:, h, :], fm_d[(4 + h) * 128:(5 + h) * 128, :]) for h in range(1, 4)]
            ld["vda"] += [P.dma("sync", vda[:, q * 8:(q + 1) * 8, :], vda_src[:, q * 8:(q + 1) * 8, :]) for q in range(1, 4)]
            ld["ksw"] = [P.dma("sync", ksw[hf * 64:(hf + 1) * 64, g, :], fm_d[12 * 128 + g * 64:12 * 128 + (g + 1) * 64, :]) for hf in range(2) for g in range(2)]
            ld["vsw"] = P.dma("sync", vsw, vsw_d.rearrange("(n p) e -> p n e", p=128))
            tl = []
            for i in range(2):
                tl.append(P.op("vector", lambda e, i=i: e.scalar_tensor_tensor(
                    out=lscr, in0=lamb[:, 2 * i, :], scalar=1.0, in1=lamb[:, 2 * i + 1, :], op0=ALU.mult, op1=ALU.mult,
                    accum_out=lsum[:, i:i + 1]), [ld["lam"]] + tl))
            te = P.op("scalar", lambda e: e.activation(out=lsum[:, 2:4], in_=lsum[:, 0:2], func=AF.Exp), tl)
            tn = P.op("vector", lambda e: e.tensor_tensor(out=neglam, in0=lsum[:, 3:4], in1=lsum[:, 2:3], op=ALU.subtract), [te])
            tn = P.op("vector", lambda e: e.tensor_scalar(out=neglam, in0=neglam, scalar1=-float(lam_init), scalar2=None, op0=ALU.add), [tn])
            tsl = P.op("vector", lambda e: e.tensor_scalar(out=slnw, in0=slnw, scalar1=float(1.0 - lam_init), scalar2=None, op0=ALU.mult), [ld["subln"]])
            tes = P.op("scalar", lambda e: e.activation(out=esk, in_=esk, func=AF.Exp), [ld["sinks"]])
            state = {"it": 0, "s_free": [None] * NSB, "p_free": [None] * NPB, "acc_free": None, "grp": 0,
                     "pst_free": [None], "deferred": None, "ysg_free": [None, None], "q_free": [None, None]}

            def acc_ap(c, r):
                a = c * 4 + r
                return psA[a // 3][:, (a % 3) * 129:(a % 3) * 129 + 129]

            def finalize_A(par, last_pv):
                f1 = []
                for bk in range(3):
                    n = 3 if bk < 2 else 2
                    f1.append(P.op("vector", lambda e, bk=bk, n=n: e.tensor_copy(
                        out=accs[par][:, bk * 3:bk * 3 + n, :].rearrange("p a b -> p (a b)"), in_=psA[bk][:, 0:n * 129]), [last_pv] if bk == 0 else []))
                f2 = P.op("vector", lambda e: e.reciprocal(out=rc[par], in_=accs[par][:, 0:8, 128]), [f1[-1]])
                f3 = P.op("vector", lambda e: e.tensor_scalar(out=rc[par][:, 4:8], in0=rc[par][:, 4:8], scalar1=neglam[:, 0:1], scalar2=None, op0=ALU.mult), [f2, tn])
                f5 = []
                for r in range(4):
                    f4 = P.op("vector", lambda e, r=r: e.tensor_scalar(out=a_t[par][:, r, :], in0=accs[par][:, r, 0:128], scalar1=rc[par][:, r:r + 1], scalar2=None, op0=ALU.mult), [f3])
                    f5.append(P.op("vector", lambda e, r=r: e.scalar_tensor_tensor(
                        out=a_t[par][:, r, :], in0=accs[par][:, 4 + r, 0:128], scalar=rc[par][:, 4 + r:5 + r], in1=a_t[par][:, r, :],
                        op0=ALU.mult, op1=ALU.add), [f4]))
                return f1[-1], f5

            def finalize_B(par, f5, h, s):
                f6 = []
                for r in range(4):
                    f6.append(P.op("scalar", lambda e, r=r: e.activation(out=junk[:, par * 4 + r, :], in_=a_t[par][:, r, :], func=AF.Square, accum_out=ssq[par][:, r:r + 1]), [f5[r]]))
                f7 = P.op("scalar", lambda e: e.activation(out=ssq[par][:, 4:8], in_=ssq[par][:, 0:4], func=AF.Ln, bias=eps_t[:, 0:1], scale=1.0 / 128), [f6[-1]])
                f8 = P.op("scalar", lambda e: e.activation(out=ssq[par][:, 8:12], in_=ssq[par][:, 4:8], func=AF.Exp, scale=-0.5), [f7])
                f9 = []
                for r in range(4):
                    f9.append(P.op("vector", lambda e, r=r: e.tensor_scalar(out=y_t[par][:, r, :], in0=a_t[par][:, r, :], scalar1=ssq[par][:, 8 + r:9 + r], scalar2=None, op0=ALU.mult), [f8]))
                tp = None
                for r in range(4):
                    tp = P.op("tensor", lambda e, r=r: e.transpose(psT[par][:, r * 128:(r + 1) * 128], y_t[par][:, r, :], identb),
                              [f9[r]] + ([state["pst_free"][0]] if r == 0 else []))
                f10 = P.op("vector", lambda e: e.tensor_scalar(out=ysg[par], in0=psT[par][:, 0:512], scalar1=slnw[:, 0:1], scalar2=None, op0=ALU.mult), [tp, tsl, state["ysg_free"][par]])
                state["pst_free"][0] = f10
                state["ysg_free"][par] = P.dma("sync", yda_d[h * 128:(h + 1) * 128, s * 512:(s + 1) * 512], ysg[par], [f10])

            bcol = 0
            for s in range(NS_):
                qb = s % 2
                lq = lq_first if s == 0 else P.dma("sync", qs[qb], q_src[:, :, s * 512:(s + 1) * 512], [state["q_free"][qb]])
                last_qk_slot = None
                for h in range(4):
                    par = state["grp"] % 2
                    kb0 = kb_first(s, h)
                    n_it = (nkbf(s) - kb0) * 2
                    items = [(kb, c) for kb in range(kb0, nkbf(s)) for c in range(2)]
                    mul_tok = {}
                    qk_tok = {}

                    def emit_qk(i, h=h, s=s, qb=qb, lq=lq):
                        kb, c = items[i]
                        g_i = state["it"] + i
                        sbi = g_i % NSB
                        deps = [state["s_free"][sbi]]
                        if i == 0:
                            deps += [lq, ld["kda"][h]]
                        qk_tok[i] = P.op("tensor", lambda e: e.matmul(
                            psS[sbi], lhsT=kda[c * 64:(c + 1) * 64, h, kb * 128:(kb + 1) * 128],
                            rhs=qs[qb][c * 64:(c + 1) * 64, h, :], start=True, stop=True), deps)
                        return qk_tok[i]

                    def emit_soft(i, h=h, s=s, bcol_base=bcol, kb0=kb0):
                        kb, c = items[i]
                        g_i = state["it"] + i
                        sbi = g_i % NSB
                        pbi = g_i % NPB
                        col = bcol_base + kb - kb0
                        ex = P.op("scalar", lambda e: e.activation(out=pbuf[pbi], in_=psS[sbi], func=AF.Exp, bias=btab[:, col:col + 1], scale=1.0),
                                  [qk_tok[i], state["p_free"][pbi]])
                        state["s_free"][sbi] = ex
                        if kb >= 4 * s:
                            dtile = dd[:, h, kb - 4 * s, :]
                        else:
                            dtile = dg[:, h, :]
                        if kb < 4 * s and h > 0:
                            mul_tok[i] = ex
                        else:
                            mul_tok[i] = P.op("vector", lambda e: e.tensor_tensor(out=pbuf[pbi], in0=pbuf[pbi], in1=dtile, op=ALU.mult), [ex])

                    def emit_pv(i, h=h, s=s, kb0=kb0):
                        kb, c = items[i]
                        g_i = state["it"] + i
                        pbi = g_i % NPB
                        pv = None
                        for r in range(4):
                            deps = [mul_tok[i]] if r == 0 else []
                            if i < 2 and r == 0:
                                deps += [state["acc_free"]]
                            if r == 0:
                                deps += [ld["vda"][kb // 8]]
                            first = (kb == kb0) and ((c * 4 + r) % 3 == 0)
                            pv = P.op("tensor", lambda e, r=r, first=first: e.matmul(
                                acc_ap(c, r), lhsT=pbuf[pbi][:, r * 128:(r + 1) * 128], rhs=vda[:, kb, h * 129:(h + 1) * 129],
                                start=first, stop=False, skip_group_check=True), deps)
                        state["p_free"][pbi] = pv
                        return pv

                    LOOK = 2
                    for i in range(min(LOOK, n_it)):
                        last_qk_slot = emit_qk(i)
                    last_pv = None
                    for i in range(n_it):
                        if i % 2 == 0:
                            for j in (i + LOOK, i + LOOK + 1):
                                if j < n_it:
                                    last_qk_slot = emit_qk(j)
                        emit_soft(i)
                        last_pv = emit_pv(i)
                        if i == 5 and state["deferred"] is not None:
                            finalize_B(*state["deferred"])
                            state["deferred"] = None
                    if state["deferred"] is not None:
                        finalize_B(*state["deferred"])
                        state["deferred"] = None
                    accf, f5 = finalize_A(par, last_pv)
                    state["acc_free"] = accf
                    state["deferred"] = (par, f5, h, s)
                    state["it"] += n_it
                    state["grp"] += 1
                    bcol += nkbf(s) - kb0
                state["q_free"][qb] = last_qk_slot
            if state["deferred"] is not None:
                finalize_B(*state["deferred"])
                state["deferred"] = None
            steps = [(n, g, hf) for n in range(4 * NS_) for g in range(2) for hf in range(2)]
            sw = {"acc_free": [[state["acc_free"]], [state["acc_free"], state["s_free"][3]]], "qsw_free": [None, None], "yswg_free": [None, None], "lqs": {}, "qk": {}, "mu": {}, "last_pv": None,
                  "pendA": None, "pendB": None, "ysw_t_free": [None, None]}
            it0 = state["it"]
            NSW = 3
            swacc = [[bankf[4], bankf[5]], [bankf[6], bankf[3]]]

            def sw_qk(idx):
                n, g, hf = steps[idx]
                sbi = idx % NSW
                s = n // 4
                qb = s % 2
                nl = n % 4
                if nl == 0 and g == 0 and hf == 0:
                    sw["lqs"][s] = P.dma("sync", qsw[qb], qsw_src[:, :, s * 512:(s + 1) * 512], [sw["qsw_free"][qb]])
                deps = [state["s_free"][sbi], sw["lqs"][s]]
                if idx == 0:
                    deps += ld["ksw"] + [ld["vsw"]]
                t = None
                for kind in range(2):
                    if n == 0 and kind == 0:
                        continue
                    kblk = n - 1 + kind
                    t = P.op("tensor", lambda e, g=g, hf=hf, kblk=kblk, sbi=sbi, nl=nl, qb=qb, kind=kind: e.matmul(
                        psS[sbi][:, kind * 256:(kind + 1) * 256], lhsT=ksw[hf * 64:(hf + 1) * 64, g, kblk * 128:(kblk + 1) * 128],
                        rhs=qsw[qb][hf * 64:(hf + 1) * 64, 2 * g:2 * g + 2, nl * 128:(nl + 1) * 128], start=True, stop=True), deps if t is None else [])
                sw["qk"][idx] = t
                if nl == 3 and g == 1 and hf == 1:
                    sw["qsw_free"][qb] = t

            def sw_soft(idx):
                n, g, hf = steps[idx]
                sbi = idx % NSW
                pbi = (it0 + idx) % NPB
                c0 = 256 if n == 0 else 0
                ex = P.op("scalar", lambda e: e.activation(out=pbuf[pbi][:, c0:512], in_=psS[sbi][:, c0:512], func=AF.Exp), [sw["qk"][idx], state["p_free"][pbi]])
                state["s_free"][sbi] = ex
                base = (g * 2 + hf) * 4
                dt_ap = dsw[:, base:base + 4, :].rearrange("p a b -> p (a b)")
                sw["mu"][idx] = P.op("vector", lambda e: e.tensor_tensor(out=pbuf[pbi][:, c0:512], in0=pbuf[pbi][:, c0:512], in1=dt_ap[:, c0:512], op=ALU.mult), [ex])

            def sw_pv(idx):
                n, g, hf = steps[idx]
                pbi = (it0 + idx) % NPB
                pv = None
                for kind in range(2):
                    if n == 0 and kind == 0:
                        continue
                    kblk = n - 1 + kind
                    for j in range(2):
                        gi = hf + 2 * j
                        first = (hf == 0 and pv is None)
                        deps = []
                        if pv is None:
                            deps = [sw["mu"][idx]]
                            if hf == 0:
                                deps += sw["acc_free"][n % 2]
                        accb = swacc[n % 2][g]
                        pv = P.op("tensor", lambda e, accb=accb, gi=gi, j=j, kblk=kblk, kind=kind, first=first: e.matmul(
                            accb[:, gi * 65:(gi + 1) * 65], lhsT=pbuf[pbi][:, kind * 256 + j * 128:kind * 256 + (j + 1) * 128], rhs=vsw[:, kblk, g * 65:(g + 1) * 65],
                            start=first, stop=False, skip_group_check=True), deps)
                state["p_free"][pbi] = pv
                sw["last_pv"] = pv

            def sw_final(n, last_pv_n):
                par = n % 2
                f1 = None
                for g in range(2):
                    f1 = P.op("vector", lambda e, g=g: e.tensor_copy(out=saccs[par][:, g * 4:(g + 1) * 4, :].rearrange("p a b -> p (a b)"), in_=swacc[par][g][:, 0:260]), [last_pv_n] if g == 0 else [])
                sw["acc_free"][par] = [f1]
                f2 = P.op("vector", lambda e: e.tensor_tensor(out=sden[par], in0=saccs[par][:, :, 64], in1=esk, op=ALU.add), [f1, tes])
                f3 = P.op("vector", lambda e: e.reciprocal(out=sden[par], in_=sden[par]), [f2])
                f4 = None
                for hh in range(8):
                    f4 = P.op("vector", lambda e, hh=hh: e.tensor_scalar(out=ysw_t[par][:, hh, :], in0=saccs[par][:, hh, 0:64], scalar1=sden[par][:, hh:hh + 1], scalar2=None, op0=ALU.mult),
                              [f3] + ([sw["ysw_t_free"][par]] if hh == 0 else []))
                def partB():
                    tp = None
                    for cc in range(4):
                        tp = P.op("tensor", lambda e, cc=cc: e.transpose(psT[par][:, cc * 128:(cc + 1) * 128], ysw_t[par][:, 2 * cc:2 * cc + 2, :].rearrange("p a b -> p (a b)"), identb),
                                  [f4] + ([state["pst_free"][0]] if cc == 0 else []))
                    f5 = P.op("vector", lambda e: e.tensor_copy(out=yswg[par], in_=psT[par][:, 0:512].rearrange("p (a b) -> p a b", b=128)), [tp, sw["yswg_free"][par]])
                    state["pst_free"][0] = f5
                    sw["ysw_t_free"][par] = f5
                    sw["yswg_free"][par] = P.dma("sync", ysw_d.rearrange("(c p) t -> p c t", p=128)[:, :, n * 128:(n + 1) * 128], yswg[par], [f5])
                sw["pendB"] = partB

            sw_qk(0)
            sw_qk(1)
            for idx in range(len(steps)):
                if idx + 2 < len(steps):
                    sw_qk(idx + 2)
                sw_soft(idx)
                n, g, hf = steps[idx]
                if g == 0 and hf == 0 and sw["pendA"] is not None:
                    sw["pendA"]()
                    sw["pendA"] = None
                sw_pv(idx)
                if g == 0 and hf == 0 and sw["pendB"] is not None:
                    sw["pendB"]()
                    sw["pendB"] = None
                if g == 1 and hf == 1:
                    sw["pendA"] = (lambda n=n, lp=sw["last_pv"]: sw_final(n, lp))
            if sw["pendA"] is not None:
                sw["pendA"]()
                sw["pendA"] = None
            if sw["pendB"] is not None:
                sw["pendB"]()
                sw["pendB"] = None
            state["it"] += len(steps)
            prog_barrier(P, bar)

        def stage_mix(l, x_in, x_mid):
            A.reset()
            NS_ = NSF
            wbda = A.alloc([128, 4, D], BF16)
            wbsw = A.alloc([128, 4, D], BF16)
            wmix = A.alloc([128, 8, D], BF16)
            yds = [A.alloc([128, 4, 512], BF16) for i in range(2)]
            yss = [A.alloc([128, 4, 512], BF16) for i in range(2)]
            sg = [A.alloc([128, 16, 512], BF16) for i in range(2)]
            mg = [A.alloc([128, 8, 512], BF16) for i in range(2)]
            m1 = [A.alloc([128, 512], F32) for i in range(2)]
            m2 = [A.alloc([128, 512], F32) for i in range(2)]
            xt = [A.alloc([128, D], F32) for i in range(2)]
            xm = [A.alloc([128, D], F32) for i in range(2)]
            scr = A.alloc([128, D], F32)
            xn = [A.alloc([128, D], BF16) for i in range(2)]
            ss = [A.alloc([128, 4], F32) for i in range(2)]
            xnT = [A.alloc([128, 8, 128], BF16) for i in range(2)]
            wn = A.alloc([128, 8], F32)
            psA = bankf[0:2]
            psB = bankf[2:4]
            psM = bankf[4:6]
            pst = bankb[6:8]
            t_wn = P.dma("sync", wn, nw2_d[l].rearrange("(c p) -> p c", p=128), slow=True)
            lw = []
            for k in range(4):
                lw.append(P.dma("gpsimd", wbda[:, k, :], wbda_d[l, k * 128:(k + 1) * 128, :]))
                lw.append(P.dma("gpsimd", wbsw[:, k, :], wbsw_d[l, k * 128:(k + 1) * 128, :]))
            lm = [P.dma("gpsimd", wmix[:, k, :], wmix_d[l, k * 128:(k + 1) * 128, :]) for k in range(8)]
            P.wait_only("tensor", lw)
            P.wait_only("vector", [t_wn])
            sg_src = fm_d[13 * 128:29 * 128, :].rearrange("(c p) t -> p c t", p=128)
            yd_src = yda_d.rearrange("(c p) t -> p c t", p=128)
            ys_src = ysw_d.rearrange("(c p) t -> p c t", p=128)
            xn_dst = xn2_d.rearrange("(c p) t -> p c t", p=128)
            sg_free = [None, None]
            y_free = [None, None]
            mg_free = [None, None]
            a_free = [None, None]
            b_free = [None, None]
            m_free = [None, None]
            m1_free = [None, None]
            xt_free = [None, None]
            xm_free = [None, None]
            xm_rd = [None, None]
            xn_free = [None, None]
            pst_free = [None, None]
            xnT_free = [None, None]
            mst = {"cnt": 0, "mcnt": 0, "mg_tok": {}, "tb": {}}

            def gate_slot(s, js=range(8)):
                sp = s % 2
                if 0 in js:
                    mst["lsg"] = P.dma("sync", sg[sp], sg_src[:, :, s * 512:(s + 1) * 512], [sg_free[sp]])
                    mst["lyd"] = P.dma("sync", yds[sp], yd_src[:, :, s * 512:(s + 1) * 512], [y_free[sp]])
                    mst["lys"] = P.dma("sync", yss[sp], ys_src[:, :, s * 512:(s + 1) * 512], [y_free[sp]])
                    mst["mg_tok"][s] = []
                lsg, lyd, lys = mst["lsg"], mst["lyd"], mst["lys"]
                mg_tok = mst["mg_tok"][s]
                tb = None
                for j in js:
                    pb = mst["cnt"] % 2
                    mst["cnt"] += 1
                    ta = tb = None
                    for k in range(4):
                        ta = P.op("tensor", lambda e, k=k, pb=pb, j=j, sp=sp: e.matmul(psA[pb], lhsT=wbda[:, k, j * 128:(j + 1) * 128], rhs=yds[sp][:, k, :], start=(k == 0), stop=(k == 3)),
                                  [a_free[pb], lyd] if k == 0 else [])
                    for k in range(4):
                        tb = P.op("tensor", lambda e, k=k, pb=pb, j=j, sp=sp: e.matmul(psB[pb], lhsT=wbsw[:, k, j * 128:(j + 1) * 128], rhs=yss[sp][:, k, :], start=(k == 0), stop=(k == 3)),
                                  [b_free[pb], lys] if k == 0 else [])
                    d1 = P.op("vector", lambda e, pb=pb, j=j, sp=sp: e.tensor_tensor(out=m1[pb], in0=psA[pb], in1=sg[sp][:, j, :], op=ALU.mult), [ta, lsg, m1_free[pb]])
                    a_free[pb] = d1
                    d2 = P.op("vector", lambda e, pb=pb, j=j, sp=sp: e.tensor_tensor(out=m2[pb], in0=psB[pb], in1=sg[sp][:, 8 + j, :], op=ALU.mult), [tb, lsg])
                    b_free[pb] = d2
                    d3 = P.op("vector", lambda e, pb=pb, j=j, sp=sp: e.tensor_tensor(out=mg[sp][:, j, :], in0=m1[pb], in1=m2[pb], op=ALU.add), [d1, d2] + ([mg_free[sp]] if j == 0 else []))
                    m1_free[pb] = d3
                    mg_tok.append(d3)
                if 7 in js:
                    sg_free[sp] = mg_tok[-1]
                    y_free[sp] = tb

            def mix_slot(s, tts=range(4)):
                sp = s % 2
                mg_tok = mst["mg_tok"][s]
                last_mm = None
                for tt4 in tts:
                    tt = s * 4 + tt4
                    xb = tt % 2
                    lx = P.dma("sync", xt[xb], x_in[tt * 128:(tt + 1) * 128, :], [xt_free[xb]])
                    adds = []
                    for half in range(2):
                        pb = mst["mcnt"] % 2
                        mst["mcnt"] += 1
                        mm = None
                        for k in range(8):
                            mm = P.op("tensor", lambda e, k=k, pb=pb, tt4=tt4, half=half, sp=sp: e.matmul(
                                psM[pb], lhsT=mg[sp][:, k, tt4 * 128:(tt4 + 1) * 128], rhs=wmix[:, k, half * 512:(half + 1) * 512], start=(k == 0), stop=(k == 7)),
                                ([m_free[pb]] + mg_tok + lm) if k == 0 else [])
                        last_mm = mm
                        ad = P.op("vector", lambda e, pb=pb, xb=xb, half=half: e.tensor_tensor(out=xm[xb][:, half * 512:(half + 1) * 512], in0=psM[pb], in1=xt[xb][:, half * 512:(half + 1) * 512], op=ALU.add),
                                  [mm, lx, xm_free[xb], xm_rd[xb]])
                        m_free[pb] = ad
                        adds.append(ad)
                    xt_free[xb] = adds[-1]
                    st = P.dma("sync", x_mid[tt * 128:(tt + 1) * 128, :], xm[xb], adds)
                    flush_pending()
                    P.wait_only("scalar", [xn_free[xb]])
                    fin, t4 = emit_rmsnorm_T(P, xm[xb], adds[-1], scr, ss[xb], xn[xb], pst[xb], identb, wn,
                                             lambda c, xb=xb: xnT[xb][:, c, :], eps_t, defer=True, on_act=True)
                    xm_free[xb] = st
                    xm_rd[xb] = t4
                    mst["pending"] = (fin, xb, tt)
                if 3 in tts:
                    mg_free[sp] = last_mm

            def flush_pending():
                if mst.get("pending") is None:
                    return
                fin, xb, tt = mst["pending"]
                mst["pending"] = None
                toks = fin([pst_free[xb]], [xnT_free[xb]])
                xn_free[xb] = toks[-1]
                pst_free[xb] = toks[-1]
                xnT_free[xb] = P.dma("sync", xn_dst[:, :, tt * 128:(tt + 1) * 128], xnT[xb], toks)

            gate_slot(0)
            for s in range(NS_):
                for q in range(4):
                    if s + 1 < NS_:
                        gate_slot(s + 1, range(2 * q, 2 * q + 2))
                    mix_slot(s, range(q, q + 1))
            flush_pending()
            prog_barrier(P, bar)

        def stage_ffn(l, half_i, x_mid, x_out, final):
            A.reset()
            TH = 2048
            NTH = 16
            NSH = 4
            t0 = half_i * TH
            big = A.alloc([128, 8 * (TH + 2)], BF16)
            xnx = big.rearrange("p (k t) -> p k t", t=TH + 2)
            wd1 = big[:, 0:22 * 512].rearrange("p (i c) -> p i c", c=512)
            wd0 = A.alloc([128, 22, 512], BF16)
            hT = A.alloc([128, 22, TH], BF16)
            wug = [A.alloc([128, 8, 256], BF16) for i in range(2)]
            wuv = [A.alloc([128, 8, 256], BF16) for i in range(2)]
            wk = A.alloc([128, 6, 512], F32)
            accg = [wk[:, i, :] for i in range(2)]
            accv = [wk[:, 2 + i, :] for i in range(2)]
            sgt = [wk[:, 4 + i, :] for i in range(2)]
            uc = {(w, p): A.alloc([128, 514], F32) for w in range(2) for p in range(2)}
            cw = A.alloc([128, 3, 44], F32)
            cb = A.alloc([128, 44], F32)
            xt = [A.alloc([128, 512], F32) for i in range(3)]
            xo = [A.alloc([128, D], F32) for i in range(2)]
            fw = A.alloc([128, D], F32) if final else None
            ss = [A.alloc([128, 4], F32) for i in range(2)]
            scr = wk[:, 0:2, :].rearrange("p a b -> p (a b)")
            psU = [bankf[0], bankf[1], bankf[2], bankf[3], bankf[5], bankf[6]]
            NU = 6
            psH = bankf[4]
            psD = bankf[5:8]
            if half_i == 0:
                lx0 = P.op("gpsimd", lambda e: e.memset(xnx[:, :, 0:2], 0.0))
                lx = [P.dma("sync", xnx[:, k, 2:TH + 2], xn2_d[k * 128:(k + 1) * 128, 0:TH]) for k in range(8)] + [lx0]
            else:
                lx = [P.dma("sync", xnx[:, k, :], xn2_d[k * 128:(k + 1) * 128, t0 - 2:t0 + TH]) for k in range(8)]
            lcw = [P.dma("sync", cw[:, j, :], cw_d[l, j].rearrange("(c p) -> p c", p=128), slow=True) for j in range(3)]
            lcb = P.dma("sync", cb, cb_d[l].rearrange("(c p) -> p c", p=128), slow=True)
            lfw = P.dma("sync", fw, fnw_d.partition_broadcast(128)) if final else None
            P.wait_only("vector", lcw + [lcb])
            P.wait_only("scalar", lcw + [lcb])
            lwd0 = []
            wu_free = [None, None]
            u_free = [None] * NU
            h_free = None
            pend = {"f": None}
            acc_free = {0: [None, None], 1: [None, None]}
            sgt_free = [None, None]
            uc_free = {k: [] for k in uc}
            ucnt = 0
            hT_tok = []
            last_up_mm = None
            for gi in range(11):
                i0, n = 2 * gi, 2
                wb = gi % 2
                lg = [P.dma("gpsimd", wug[wb][:, k, :], wup_d[l, k * 128:(k + 1) * 128, i0 * 128:(i0 + n) * 128], [wu_free[wb]]) for k in range(8)]
                lv = [P.dma("gpsimd", wuv[wb][:, k, :], wup_d[l, k * 128:(k + 1) * 128, DFF + i0 * 128:DFF + (i0 + n) * 128], [wu_free[wb]]) for k in range(8)]
                if gi == 1:
                    lwd0.extend([P.dma("gpsimd", wd0[:, i, :], wdn_d[l, i * 128:(i + 1) * 128, 0:512]) for i in range(22)])
                for ci in range(n):
                    i = i0 + ci
                    hm = None
                    for which, wt in ((0, wug[wb]), (1, wuv[wb])):
                        for k in range(8):
                            hm = P.op("tensor", lambda e, k=k, wt=wt, ci=ci, which=which: e.matmul(
                                psH[:, which * 2:which * 2 + 2], lhsT=wt[:, k, ci * 128:(ci + 1) * 128], rhs=xnx[:, k, 0:2], start=(k == 0), stop=(k == 7)),
                                ([h_free] + lg + lv + lx) if (k == 0 and which == 0) else [])
                    halo = {}
                    for which in range(2):
                        halo[(which, 0)] = P.op("vector", lambda e, which=which: e.tensor_copy(out=uc[(which, 0)][:, 0:2], in_=psH[:, which * 2:which * 2 + 2]),
                                                [hm] + uc_free[(which, 0)])
                    h_free = halo[(1, 0)]
                    for s in range(NSH):
                        p = s % 2
                        dlast = {}
                        for which, wt in ((0, wug[wb]), (1, wuv[wb])):
                            ub = ucnt % NU
                            ucnt += 1
                            ch = i if which == 0 else 22 + i
                            mm = None
                            for k in range(8):
                                mm = P.op("tensor", lambda e, k=k, wt=wt, ci=ci, ub=ub, s=s: e.matmul(
                                    psU[ub], lhsT=wt[:, k, ci * 128:(ci + 1) * 128], rhs=xnx[:, k, 2 + s * 512:2 + (s + 1) * 512], start=(k == 0), stop=(k == 7)),
                                    [u_free[ub]] if k == 0 else [])
                            last_up_mm = mm
                            ucb = uc[(which, p)]
                            acc = accg[p] if which == 0 else accv[p]
                            a0 = P.op("scalar", lambda e, ucb=ucb, ub=ub: e.activation(out=ucb[:, 2:514], in_=psU[ub], func=AF.Copy), [mm] + uc_free[(which, p)])
                            frees = []
                            if s < NSH - 1:
                                hcp = P.op("scalar", lambda e, ucb=ucb, which=which, p=p: e.activation(out=uc[(which, 1 - p)][:, 0:2], in_=ucb[:, 512:514], func=AF.Copy), [a0] + uc_free[(which, 1 - p)])
                                halo[(which, s + 1)] = hcp
                                frees.append(hcp)
                            d0 = P.op("scalar", lambda e, acc=acc, ub=ub, ch=ch: e.activation(out=acc, in_=psU[ub], func=AF.Identity, bias=cb[:, ch:ch + 1], scale=cw[:, 2, ch:ch + 1]),
                                      [mm, acc_free[which][p]])
                            u_free[ub] = d0
                            d1 = P.op("vector", lambda e, acc=acc, ucb=ucb, ch=ch: e.scalar_tensor_tensor(out=acc, in0=ucb[:, 1:513], scalar=cw[:, 1, ch:ch + 1], in1=acc, op0=ALU.mult, op1=ALU.add),
                                      [d0, a0, halo[(which, s)]])
                            d2 = P.op("vector", lambda e, acc=acc, ucb=ucb, ch=ch: e.scalar_tensor_tensor(out=acc, in0=ucb[:, 0:512], scalar=cw[:, 0, ch:ch + 1], in1=acc, op0=ALU.mult, op1=ALU.add), [d1])
                            frees.append(d2)
                            uc_free[(which, p)] = frees
                            dlast[which] = d2
                        def fin_glu(p=p, i=i, s=s, dg_=dlast[0], dv_=dlast[1]):
                            a2 = P.op("scalar", lambda e: e.activation(out=sgt[p], in_=accg[p], func=AF.Silu), [dg_, sgt_free[p]])
                            g1 = P.op("vector", lambda e: e.tensor_tensor(out=hT[:, i, s * 512:(s + 1) * 512], in0=sgt[p], in1=accv[p], op=ALU.mult), [a2, dv_])
                            acc_free[0][p] = a2
                            acc_free[1][p] = g1
                            sgt_free[p] = g1
                            hT_tok.append(g1)
                        if pend["f"] is not None:
                            pend["f"]()
                        pend["f"] = fin_glu
                wu_free[wb] = last_up_mm
            if pend["f"] is not None:
                pend["f"]()
                pend["f"] = None
            lwd1 = [P.dma("gpsimd", wd1[:, i, :], wdn_d[l, i * 128:(i + 1) * 128, 512:1024], [last_up_mm]) for i in range(22)]
            P.wait_only("tensor", hT_tok[-8:] + lwd0)
            d_free = [u_free[4], u_free[5], None]
            xt_free = [None] * 3
            xo_free = [None, None]
            dcnt = 0
            for tt in range(NTH):
                ob = tt % 2
                adds = []
                for half in range(2):
                    pb = dcnt % 3
                    dcnt += 1
                    wd = wd0 if half == 0 else wd1
                    lxt = P.dma("sync", xt[pb], x_mid[t0 + tt * 128:t0 + (tt + 1) * 128, half * 512:(half + 1) * 512], [xt_free[pb]])
                    mm = None
                    for i in range(22):
                        deps = []
                        if i == 0:
                            deps = [d_free[pb]] + (lwd1 if half == 1 else [])
                        mm = P.op("tensor", lambda e, i=i, pb=pb, tt=tt, wd=wd: e.matmul(psD[pb], lhsT=hT[:, i, tt * 128:(tt + 1) * 128], rhs=wd[:, i, :], start=(i == 0), stop=(i == 21)), deps)
                    ad = P.op("vector", lambda e, pb=pb, ob=ob, half=half: e.tensor_tensor(out=xo[ob][:, half * 512:(half + 1) * 512], in0=psD[pb], in1=xt[pb], op=ALU.add), [mm, lxt, xo_free[ob]])
                    d_free[pb] = ad
                    xt_free[pb] = ad
                    adds.append(ad)
                if final:
                    t1 = P.op("scalar", lambda e, ob=ob: e.activation(out=scr, in_=xo[ob], func=AF.Square, accum_out=ss[ob][:, 0:1]), adds)
                    t2 = P.op("scalar", lambda e, ob=ob: e.activation(out=ss[ob][:, 1:2], in_=ss[ob][:, 0:1], func=AF.Ln, bias=eps_t[:, 0:1], scale=1.0 / D), [t1])
                    t3 = P.op("scalar", lambda e, ob=ob: e.activation(out=ss[ob][:, 2:3], in_=ss[ob][:, 1:2], func=AF.Exp, scale=-0.5), [t2])
                    t4 = P.op("vector", lambda e, ob=ob: e.scalar_tensor_tensor(out=xo[ob], in0=xo[ob], scalar=ss[ob][:, 2:3], in1=fw, op0=ALU.mult, op1=ALU.mult), [t3, lfw])
                    adds = [t4]
                xo_free[ob] = P.dma("sync", x_out[t0 + tt * 128:t0 + (tt + 1) * 128, :], xo[ob], adds)
            prog_barrier(P, bar)

        x_cur = x_ext
        for l in range(depth):
            lam_init = 0.8 - 0.6 * math.exp(-0.3 * l)
            stage_proj(l, x_cur)
            stage_attn(l, lam_init)
            stage_mix(l, x_cur, xA)
            final = (l == depth - 1)
            x_next = out_d if final else xB
            for hi in range(2):
                stage_ffn(l, hi, xA, x_next, final)
            x_cur = x_next
        P.emit(None)
    return nc


FF_GROUPS = [(0, 4), (4, 4), (8, 4), (12, 4), (16, 4), (20, 2)]
_CACHE = {}


def kernel(**inputs):
    inp = {k: np.asarray(v) for k, v in inputs.items()}
    f32 = lambda a: np.ascontiguousarray(a, dtype=np.float32)
    dg, dd, dsw = make_tables()
    shared = {
        "w_in": f32(inp["w_in"]), "nw1": f32(inp["norm_mix_w"]),
        "lam4": f32(np.stack([inp["lambda_q1"], inp["lambda_k1"], inp["lambda_q2"], inp["lambda_k2"]], axis=1)),
        "subln": f32(inp["subln_w"]), "sinks": f32(inp["sinks"]),
        "w_br_da": f32(inp["w_br_da"]), "w_br_sw": f32(inp["w_br_sw"]), "w_mix": f32(inp["w_mix_out"]), "nw2": f32(inp["norm_ffn_w"]),
        "w_up": f32(inp["w_up"]), "conv_w": f32(inp["conv_w"]), "conv_b": f32(inp["conv_b"]), "w_down": f32(inp["w_down"]),
        "fnw": f32(inp["norm_final_w"]), "ident": np.eye(128, dtype=np.float32),
        "dg": dg, "dd": dd, "dsw": np.ascontiguousarray(dsw.reshape(128, 16, 128)), "btab": make_btab_full(),
    }
    x = f32(inp["x"])
    if "nc" not in _CACHE:
        _CACHE["nc"] = build_fused()
    nc = _CACHE["nc"]
    zero = {k: np.zeros_like(v) for k, v in shared.items()}
    in_maps = []
    for c in range(2 * B):
        if c % 2 == 0:
            m = dict(shared)
            m["x"] = np.ascontiguousarray(x[c // 2])
        else:
            m = dict(zero)
            m["x"] = np.zeros((TF, D), np.float32)
        in_maps.append(m)
    res = run_bass_kernel_spmd(nc, in_maps, core_ids=list(range(2 * B))).results
    out = np.stack([res[2 * b]["out"] for b in range(B)], axis=0)
    return np.ascontiguousarray(out, dtype=np.float32)
```

```python
import math
import numpy as np
import ml_dtypes
import concourse.bass as bass
import concourse.mybir as mybir
from concourse.bass_utils import run_bass_kernel_spmd

F32 = mybir.dt.float32
BF16 = mybir.dt.bfloat16
AF = mybir.ActivationFunctionType
ALU = mybir.AluOpType
BF = ml_dtypes.bfloat16

D = 1024
S = 4096
B = 4
DEPTH = 4
T = 2048
NT = T // 128
NS = T // 512
INC = 4352
DFF = 2816
EPS = 1e-6
NEG = -30000.0

ENGS = ("tensor", "vector", "scalar", "gpsimd", "sync")


class Prog:
    def __init__(self, nc, n_dma_sems=24):
        self.nc = nc
        self.streams = {e: [] for e in ENGS}
        self.count = {e: 0 for e in ENGS}
        self.n_dma = n_dma_sems
        self.dma_cnt = [0] * n_dma_sems
        self.pool = {"gpsimd": list(range(0, 8)), "sync": list(range(8, n_dma_sems - 4)), "scalar": list(range(n_dma_sems - 4, n_dma_sems))}
        self.dma_rr = {"gpsimd": 0, "sync": 0, "scalar": 0}
        self.all_dma = []

    def op(self, eng, fn, deps=()):
        self.count[eng] += 1
        tok = ("e", eng, self.count[eng])
        self.streams[eng].append(("op", fn, [d for d in deps if d is not None]))
        return tok

    def dma(self, queue, out, in_, deps=(), slow=False):
        pl = self.pool[queue]
        i = pl[self.dma_rr[queue] % len(pl)]
        self.dma_rr[queue] += 1
        prev = ("d", i, self.dma_cnt[i]) if self.dma_cnt[i] > 0 else None
        self.dma_cnt[i] += 1
        tok = ("d", i, self.dma_cnt[i])
        dl = [d for d in deps if d is not None]
        if prev is not None:
            dl.append(prev)
        self.streams[queue].append(("dma", (out, in_, i, slow), dl))
        self.all_dma.append(tok)
        return tok

    def wait_only(self, eng, deps):
        self.streams[eng].append(("wait", None, [d for d in deps if d is not None]))

    def emit(self, block):
        nc = self.nc
        import contextlib
        with contextlib.ExitStack() as es:
            esem = {e: es.enter_context(nc.semaphore("s_" + e)) for e in ENGS}
            dsem = [es.enter_context(nc.semaphore("d_%d" % i)) for i in range(self.n_dma)]
            blk = es.enter_context(nc.Block())

            def replay(name, eng):
                waited = {}

                def do_waits(deps):
                    for d in deps:
                        if d[0] == "e":
                            key = ("e", d[1])
                            val = d[2]
                            sem = esem[d[1]]
                        else:
                            key = ("d", d[1])
                            val = d[2] * 16
                            sem = dsem[d[1]]
                        if waited.get(key, 0) >= val:
                            continue
                        waited[key] = val
                        eng.wait_ge(sem, val)

                for kind, payload, deps in self.streams[name]:
                    do_waits(deps)
                    if kind == "op":
                        payload(eng).then_inc(esem[name], 1)
                    elif kind == "dma":
                        out, in_, i, slow = payload
                        if slow:
                            eng.dma_start(out=out, in_=in_, allow_slow_non_contiguous=True).then_inc(dsem[i], 16)
                        else:
                            eng.dma_start(out=out, in_=in_).then_inc(dsem[i], 16)

            final = list(self.all_dma)

            @blk.tensor
            def _(e):
                replay("tensor", e)

            @blk.vector
            def _(e):
                replay("vector", e)

            @blk.scalar
            def _(e):
                replay("scalar", e)

            @blk.gpsimd
            def _(e):
                replay("gpsimd", e)

            @blk.sync
            def _(e):
                replay("sync", e)
                for i in range(self.n_dma):
                    if self.dma_cnt[i] > 0:
                        e.wait_ge(dsem[i], self.dma_cnt[i] * 16)
                for en in ENGS:
                    if en != "sync" and self.count[en] > 0:
                        e.wait_ge(esem[en], self.count[en])


def _run(nc, in_maps):
    res = run_bass_kernel_spmd(nc, in_maps, core_ids=list(range(len(in_maps))))
    return res.results


def emit_rmsnorm_T(P, xt_ap, xt_tok, scr, ss, xn, pst, identb, wn, dst_fn, eps_t, n_feat=1024, defer=False, on_act=False):
    t1 = P.op("scalar", lambda e: e.activation(out=scr, in_=xt_ap, func=AF.Square, accum_out=ss[:, 0:1]), [xt_tok])
    t2 = P.op("scalar", lambda e: e.activation(out=ss[:, 1:2], in_=ss[:, 0:1], func=AF.Ln, bias=eps_t[:, 0:1], scale=1.0 / n_feat), [t1])
    t3 = P.op("scalar", lambda e: e.activation(out=ss[:, 2:3], in_=ss[:, 1:2], func=AF.Exp, scale=-0.5), [t2])
    if on_act:
        t4 = P.op("scalar", lambda e: e.activation(out=xn, in_=xt_ap, func=AF.Copy, scale=ss[:, 2:3]), [t3, xt_tok])
    else:
        t4 = P.op("vector", lambda e: e.tensor_scalar(out=xn, in0=xt_ap, scalar1=ss[:, 2:3], scalar2=None, op0=ALU.mult), [t3, xt_tok])

    def finish(extra_pe_deps=(), extra_ev_deps=()):
        toks = []
        tp = []
        for c in range(n_feat // 128):
            tp.append(P.op("tensor", lambda e, c=c: e.transpose(pst[:, c * 128:(c + 1) * 128], xn[:, c * 128:(c + 1) * 128], identb),
                           [t4] + (list(extra_pe_deps) if c == 0 else [])))
        for c in range(n_feat // 128):
            if on_act:
                toks.append(P.op("scalar", lambda e, c=c: e.activation(out=dst_fn(c), in_=pst[:, c * 128:(c + 1) * 128], func=AF.Copy, scale=wn[:, c:c + 1]),
                                 [tp[-1]] + (list(extra_ev_deps) if c == 0 else [])))
            else:
                toks.append(P.op("vector", lambda e, c=c: e.tensor_scalar(out=dst_fn(c), in0=pst[:, c * 128:(c + 1) * 128], scalar1=wn[:, c:c + 1], scalar2=None, op0=ALU.mult),
                                 [tp[-1]] + (list(extra_ev_deps) if c == 0 else [])))
        return toks

    if defer:
        return finish, t4
    return finish(), t4


def da_slopes():
    return [2.0 ** (-2 * (h + 1)) for h in range(4)]


def sw_slopes():
    return [2.0 ** (-(h + 1)) for h in range(8)]


def nkb(s):
    return 16 + 4 * (s + 1)


def make_tables():
    jj = np.arange(128, dtype=np.float64)[:, None]
    ii = np.arange(512, dtype=np.float64)[None, :]
    dg = np.zeros((128, 4, 512), np.float32)
    dd = np.zeros((128, 4, 4, 512), np.float32)
    for h, sl in enumerate(da_slopes()):
        dg[:, h, :] = np.exp(sl * (jj - 127 - ii))
        for d in range(4):
            rel = 128 * d + jj - ii
            if h == 0:
                dd[:, h, d, :] = np.where(rel <= 0, np.exp(sl * np.minimum(rel, 0)), 0.0)
            else:
                dd[:, h, d, :] = np.where(rel <= 0, np.exp(sl * (128 * d + jj)) * np.ones_like(ii), 0.0)
    i2 = np.arange(128, dtype=np.float64)[None, :]
    dsw = np.zeros((128, 2, 2, 2, 2, 128), np.float32)
    for g in range(2):
        for hf in range(2):
            for j in range(2):
                hh = 4 * g + hf + 2 * j
                sl = sw_slopes()[hh]
                dist_p = 128 + i2 - jj
                dsw[:, g, hf, 0, j, :] = np.where(jj > i2, np.exp(-sl * dist_p), 0.0)
                dist_d = i2 - jj
                dsw[:, g, hf, 1, j, :] = np.where(jj <= i2, np.exp(-sl * np.maximum(dist_d, 0)), 0.0)
    return dg.astype(BF), dd.astype(BF), dsw.astype(BF)


def make_btab(half):
    cols = []
    for s in range(NS):
        for h, sl in enumerate(da_slopes()):
            for kb in range(nkb(s)):
                if kb < 16:
                    if half == 0:
                        v = NEG
                    else:
                        v = sl * (128 * kb - 2048 - 512 * s + 127)
                else:
                    m = kb - 16
                    if m < 4 * s:
                        v = sl * (128 * m - 512 * s + 127)
                    else:
                        v = 0.0
                cols.append(v)
    cols.append(NEG if half == 0 else 0.0)
    t = np.tile(np.asarray(cols, np.float32)[None, :], (128, 1))
    return np.ascontiguousarray(t)


TF = 4096
NTF = TF // 128
NSF = TF // 512


def nkbf(s):
    return 4 * (s + 1)


DEAD = -100.0


def kb_first(s, h):
    sl = da_slopes()[h]
    m = 0
    while m < 4 * s and sl * (128 * m - 512 * s + 127) < DEAD:
        m += 1
    return m


NBTF = sum(nkbf(s) - kb_first(s, h) for s in range(NSF) for h in range(4)) + 1


def make_btab_full():
    cols = []
    jj = np.arange(128, dtype=np.float64)
    for s in range(NSF):
        for h, sl in enumerate(da_slopes()):
            for m in range(kb_first(s, h), nkbf(s)):
                if m >= 4 * s:
                    cols.append(np.zeros(128))
                elif h == 0:
                    cols.append(np.full(128, sl * (128 * m - 512 * s + 127)))
                else:
                    cols.append(sl * (128 * m - 512 * s + jj))
    cols.append(np.full(128, NEG))
    return np.ascontiguousarray(np.stack(cols, axis=1).astype(np.float32))


class Arena:
    def __init__(self, base_ap, nbytes):
        self.base = base_ap
        self.nbytes = nbytes
        self.off = 0
        self.mark = 0

    def alloc(self, shape, dt):
        assert shape[0] == 128
        esz = 4 if dt == F32 else 2
        n = 1
        for v in shape[1:]:
            n *= v
        nb = (n * esz + 63) // 64 * 64
        assert self.off + nb <= self.nbytes, ("arena overflow", self.off, nb, self.nbytes)
        ap = self.base[:, self.off // 4:(self.off + nb) // 4]
        self.off += nb
        if dt != F32:
            ap = ap.bitcast(dt)
        ap = ap[:, 0:n]
        if len(shape) == 3:
            ap = ap.rearrange("p (a b) -> p a b", b=shape[2])
        elif len(shape) == 4:
            ap = ap.rearrange("p (a b c) -> p a b c", b=shape[2], c=shape[3])
        return ap

    def set_mark(self):
        self.mark = self.off

    def reset(self):
        self.off = self.mark


def prog_barrier(P, bar):
    deps = [("e", e, P.count[e]) for e in ENGS if P.count[e] > 0] + [("d", i, P.dma_cnt[i]) for i in range(P.n_dma) if P.dma_cnt[i] > 0]
    tok = P.dma("sync", bar[1:2, :], P.bar_src, deps)
    for e in ("tensor", "vector", "scalar", "gpsimd"):
        P.wait_only(e, [tok])
    return tok


FM_CHUNKS = [(j, "q") for j in range(0, 4)] + [(j, "k") for j in range(4, 8)] + \
            [(j, "q") for j in range(12, 16)] + [(16, "k")] + [(j, "g") for j in range(18, 34)]


def build_fused(depth=DEPTH, debug_out=None):
    nc = bass.Bass("TRN2", target_bir_lowering=False)
    dt_in = lambda name, shape, dt: nc.dram_tensor(name, shape, dt, kind="ExternalInput").ap()
    dt_sc = lambda name, shape, dt: nc.dram_tensor(name, shape, dt, kind="Internal").ap()
    x_ext = dt_in("x", [TF, D], F32)
    w_in_d = dt_in("w_in", [depth, D, INC], F32)
    nw1_d = dt_in("nw1", [depth, D], F32)
    lam4_d = dt_in("lam4", [depth, 4, 64], F32)
    subln_d = dt_in("subln", [depth, 128], F32)
    sinks_d = dt_in("sinks", [depth, 8], F32)
    wbda_d = dt_in("w_br_da", [depth, 512, D], F32)
    wbsw_d = dt_in("w_br_sw", [depth, 512, D], F32)
    wmix_d = dt_in("w_mix", [depth, D, D], F32)
    nw2_d = dt_in("nw2", [depth, D], F32)
    wup_d = dt_in("w_up", [depth, D, 2 * DFF], F32)
    cw_d = dt_in("conv_w", [depth, 3, 2 * DFF], F32)
    cb_d = dt_in("conv_b", [depth, 2 * DFF], F32)
    wdn_d = dt_in("w_down", [depth, DFF, D], F32)
    fnw_d = dt_in("fnw", [D], F32)
    ident_d = dt_in("ident", [128, 128], F32)
    dg_d = dt_in("dg", [128, 4, 512], BF16)
    dd_d = dt_in("dd", [128, 4, 4, 512], BF16)
    dsw_d = dt_in("dsw", [128, 16, 128], BF16)
    btab_d = dt_in("btab", [128, NBTF], F32)
    out_d = nc.dram_tensor("out", [TF, D], F32, kind="ExternalOutput").ap()
    xA = dt_sc("xA", [TF, D], F32)
    xB = dt_sc("xB", [TF, D], F32)
    fm_d = dt_sc("fm", [29 * 128, TF], BF16)
    vda_d = dt_sc("vda", [TF, 516], BF16)
    vsw_d = dt_sc("vsw", [TF, 130], BF16)
    yda_d = dt_sc("yda", [512, TF], BF16)
    ysw_d = dt_sc("ysw", [512, TF], BF16)
    xn2_d = dt_sc("xn2", [D, TF], BF16)
    bar = dt_sc("bar", [2, 16], F32)
    ARENA_BYTES = 206 * 1024
    import contextlib
    with contextlib.ExitStack() as es:
        arena_t = es.enter_context(nc.sbuf_tensor("arena", [128, ARENA_BYTES // 4], F32))
        bank = [es.enter_context(nc.psum_tensor("bank%d" % i, [128, 512], F32)) for i in range(8)]
        bankf = [b[:] for b in bank]
        bankb = [b[:].bitcast(BF16) for b in bank]
        A = Arena(arena_t[:], ARENA_BYTES)
        P = Prog(nc)
        identb = A.alloc([128, 128], BF16)
        eps_t = A.alloc([128, 16], F32)
        A.set_mark()
        P.bar_src = ident_d[0:1, 0:16]
        c_id = P.dma("gpsimd", identb, ident_d)
        c_eps = P.op("vector", lambda e: e.memset(eps_t[:, 0:1], EPS))
        P.wait_only("tensor", [c_id])
        P.wait_only("scalar", [c_eps])
        P.wait_only("gpsimd", [c_eps])

        def stage_proj(l, x_in):
            A.reset()
            NT_, NS_ = NTF, NSF
            wb = A.alloc([128, 8, INC], BF16)
            xnT = A.alloc([128, 8, TF], BF16)
            xt = [A.alloc([128, D], F32) for i in range(2)]
            scr = A.alloc([128, D], F32)
            xn = [A.alloc([128, D], BF16) for i in range(4)]
            ss = [A.alloc([128, 4], F32) for i in range(4)]
            wn = A.alloc([128, 8], F32)
            stg = [A.alloc([128, 512], BF16) for i in range(4)]
            vst = [A.alloc([128, 4 * 129], BF16) for i in range(2)]
            vsst = [A.alloc([128, 2 * 65], BF16) for i in range(2)]
            ps = bankf[0:6]
            pst = bankb[6:8]
            t_wn = P.dma("sync", wn, nw1_d[l].rearrange("(c p) -> p c", p=128), slow=True)
            t_ones = [P.op("vector", lambda e, i=i: e.memset(vst[i], 1.0)) for i in range(2)]
            t_ones2 = [P.op("vector", lambda e, i=i: e.memset(vsst[i], 1.0)) for i in range(2)]
            wtok = {}
            for q in range(4):
                for k in range(8):
                    c0 = q * 1088
                    wtok[(k, q)] = P.dma("gpsimd", wb[:, k, c0:c0 + 1088], w_in_d[l, k * 128:(k + 1) * 128, c0:c0 + 1088])

            def wdeps(c0, c1):
                qs = sorted(set([c0 // 1088, (c1 - 1) // 1088]))
                return [wtok[(k, q)] for k in range(8) for q in qs]
            P.wait_only("vector", [t_wn])
            xnT_tok = [None] * NT_
            st1 = {"xt_free": [None, None], "xn_free": [None, None], "pst_free": [None, None], "psi": 0, "si": 0,
                   "ps_free": [None] * 6, "stg_free": [None] * 4, "v_free": [None, None], "vs_free": [None, None]}

            st1["xn_free"] = [None] * 4
            st1["fins"] = {}

            def norm1_slot(s):
                for tt in range(4 * s, 4 * s + 4):
                    b = tt % 2
                    q = tt % 4
                    ld = P.dma("sync", xt[b], x_in[tt * 128:(tt + 1) * 128, :], [st1["xt_free"][b]])
                    P.wait_only("scalar", [st1["xn_free"][q]])
                    fin, t4 = emit_rmsnorm_T(P, xt[b], ld, scr, ss[q], xn[q], pst[b], identb, wn,
                                             lambda c, tt=tt: xnT[:, c, tt * 128:(tt + 1) * 128], eps_t, defer=True)
                    st1["fins"][tt] = fin
                    st1["xt_free"][b] = t4

            def finish_slot(s):
                for tt in range(4 * s, 4 * s + 4):
                    b = tt % 2
                    q = tt % 4
                    toks = st1["fins"].pop(tt)([st1["pst_free"][b]])
                    xnT_tok[tt] = toks
                    st1["xn_free"][q] = toks[-1]
                    st1["pst_free"][b] = toks[-1]

            def proj_slot(s):
                for ci, (j, kind) in enumerate(FM_CHUNKS):
                    if ci == 14 and s + 1 < NS_:
                        finish_slot(s + 1)
                    pb = st1["psi"] % 6
                    st1["psi"] += 1
                    deps = wdeps(j * 128, (j + 1) * 128) + [st1["ps_free"][pb]]
                    for tt in range(s * 4, s * 4 + 4):
                        deps += xnT_tok[tt]
                    mm = None
                    for k in range(8):
                        mm = P.op("tensor", lambda e, k=k, pb=pb, j=j, s=s: e.matmul(
                            ps[pb], lhsT=wb[:, k, j * 128:(j + 1) * 128], rhs=xnT[:, k, s * 512:(s + 1) * 512],
                            start=(k == 0), stop=(k == 7)), deps if k == 0 else [])
                    sb_i = st1["si"] % 4
                    st1["si"] += 1
                    if kind == "q":
                        ev = P.op("scalar", lambda e, pb=pb, sb_i=sb_i: e.activation(out=stg[sb_i], in_=ps[pb], func=AF.Copy, scale=0.125), [mm, st1["stg_free"][sb_i]])
                    elif kind == "k":
                        ev = P.op("vector", lambda e, pb=pb, sb_i=sb_i: e.tensor_copy(out=stg[sb_i], in_=ps[pb]), [mm, st1["stg_free"][sb_i]])
                    else:
                        ev = P.op("scalar", lambda e, pb=pb, sb_i=sb_i: e.activation(out=stg[sb_i], in_=ps[pb], func=AF.Sigmoid), [mm, st1["stg_free"][sb_i]])
                    st1["ps_free"][pb] = ev
                    st1["stg_free"][sb_i] = P.dma("sync", fm_d[ci * 128:(ci + 1) * 128, s * 512:(s + 1) * 512], stg[sb_i], [ev])
                for tt in range(4 * s, 4 * s + 4):
                    b = tt % 2
                    pb = st1["psi"] % 6
                    st1["psi"] += 1
                    deps = wdeps(1024, 1536) + [st1["ps_free"][pb]] + xnT_tok[tt]
                    mm = None
                    for k in range(8):
                        mm = P.op("tensor", lambda e, k=k, pb=pb, tt=tt: e.matmul(
                            ps[pb], lhsT=xnT[:, k, tt * 128:(tt + 1) * 128], rhs=wb[:, k, 1024:1536],
                            start=(k == 0), stop=(k == 7)), deps if k == 0 else [])
                    ev = P.op("vector", lambda e, pb=pb, b=b: e.tensor_copy(
                        out=vst[b].rearrange("p (h e) -> p h e", e=129)[:, :, 0:128],
                        in_=ps[pb].rearrange("p (h e) -> p h e", e=128)), [mm, st1["v_free"][b], t_ones[b]])
                    st1["ps_free"][pb] = ev
                    st1["v_free"][b] = P.dma("sync", vda_d[tt * 128:(tt + 1) * 128, :], vst[b], [ev])
                    pb = st1["psi"] % 6
                    st1["psi"] += 1
                    deps = wdeps(2176, 2304) + [st1["ps_free"][pb]] + xnT_tok[tt]
                    for k in range(8):
                        mm = P.op("tensor", lambda e, k=k, pb=pb, tt=tt: e.matmul(
                            ps[pb][:, 0:128], lhsT=xnT[:, k, tt * 128:(tt + 1) * 128], rhs=wb[:, k, 2176:2304],
                            start=(k == 0), stop=(k == 7)), deps if k == 0 else [])
                    ev = P.op("vector", lambda e, pb=pb, b=b: e.tensor_copy(
                        out=vsst[b].rearrange("p (h e) -> p h e", e=65)[:, :, 0:64],
                        in_=ps[pb][:, 0:128].rearrange("p (h e) -> p h e", e=64)), [mm, st1["vs_free"][b], t_ones2[b]])
                    st1["ps_free"][pb] = ev
                    st1["vs_free"][b] = P.dma("sync", vsw_d[tt * 128:(tt + 1) * 128, :], vsst[b], [ev])

            norm1_slot(0)
            finish_slot(0)
            for s in range(NS_):
                if s + 1 < NS_:
                    norm1_slot(s + 1)
                proj_slot(s)
            prog_barrier(P, bar)

        def stage_attn(l, lam_init):
            A.reset()
            NS_ = NSF
            dg = A.alloc([128, 4, 512], BF16)
            dd = A.alloc([128, 4, 4, 512], BF16)
            dsw = A.alloc([128, 16, 128], BF16)
            btab = A.alloc([128, NBTF], F32)
            c_ld = [P.dma("sync", dg, dg_d), P.dma("sync", dd, dd_d), P.dma("sync", dsw, dsw_d), P.dma("sync", btab, btab_d)]
            P.wait_only("vector", c_ld)
            P.wait_only("scalar", c_ld)
            kda = A.alloc([128, 4, TF], BF16)
            vda = A.alloc([128, 32, 516], BF16)
            ksw = A.alloc([128, 2, TF], BF16)
            vsw = A.alloc([128, 32, 130], BF16)
            qs = [A.alloc([128, 4, 512], BF16) for i in range(2)]
            qsw = [A.alloc([128, 4, 512], BF16) for i in range(2)]
            lamb = A.alloc([128, 4, 64], F32)
            lscr = A.alloc([128, 64], F32)
            lsum = A.alloc([128, 4], F32)
            neglam = A.alloc([128, 1], F32)
            slnw = A.alloc([128, 1], F32)
            esk = A.alloc([128, 8], F32)
            NPB = 6
            pbuf = [A.alloc([128, 512], BF16) for i in range(NPB)]
            accs = [A.alloc([128, 9, 129], F32) for i in range(2)]
            rc = [A.alloc([128, 8], F32) for i in range(2)]
            a_t = [A.alloc([128, 4, 128], F32) for i in range(2)]
            ssq = [A.alloc([128, 12], F32) for i in range(2)]
            junk = A.alloc([128, 8, 128], F32)
            y_t = [A.alloc([128, 4, 128], BF16) for i in range(2)]
            ysg = [A.alloc([128, 512], BF16) for i in range(2)]
            saccs = [A.alloc([128, 8, 65], F32) for i in range(2)]
            sden = [A.alloc([128, 8], F32) for i in range(2)]
            ysw_t = [A.alloc([128, 8, 64], BF16) for i in range(2)]
            yswg = [A.alloc([128, 4, 128], BF16) for i in range(2)]
            NSB = 4
            psS = bankf[0:4]
            psA = bankf[4:7]
            psT = [bankb[7], bankb[7]]
            ld = {}
            ld["lam"] = P.dma("sync", lamb.rearrange("p a b -> p (a b)"), lam4_d[l].rearrange("a b -> (a b)").partition_broadcast(128))
            ld["subln"] = P.dma("sync", slnw, subln_d[l].rearrange("(p o) -> p o", o=1))
            ld["sinks"] = P.dma("sync", esk, sinks_d[l].partition_broadcast(128))
            q_src = fm_d[0:512, :].rearrange("(h p) t -> p h t", p=128)
            qsw_src = fm_d[8 * 128:12 * 128, :].rearrange("(h p) t -> p h t", p=128)
            vda_src = vda_d.rearrange("(n p) e -> p n e", p=128)
            ld["kda"] = [P.dma("sync", kda[:, 0, :], fm_d[4 * 128:5 * 128, :])]
            lq_first = P.dma("sync", qs[0], q_src[:, :, 0:512])
            ld["vda"] = [P.dma("sync", vda[:, 0:8, :], vda_src[:, 0:8, :])]
            ld["kda"] += [P.dma("sync", kda[:, h, :], fm_d[(4 + h) * 128:(5 + h) * 128, :]) for h in range(1, 4)]
            ld["vda"] += [P.dma("sync", vda[:, q * 8:(q + 1) * 8, :], vda_src[:, q * 8:(q + 1) * 8, :]) for q in range(1, 4)]
            ld["ksw"] = [P.dma("sync", ksw[hf * 64:(hf + 1) * 64, g, :], fm_d[12 * 128 + g * 64:12 * 128 + (g + 1) * 64, :]) for hf in range(2) for g in range(2)]
            ld["vsw"] = P.dma("sync", vsw, vsw_d.rearrange("(n p) e -> p n e", p=128))
            tl = []
            for i in range(2):
                tl.append(P.op("vector", lambda e, i=i: e.scalar_tensor_tensor(
                    out=lscr, in0=lamb[:, 2 * i, :], scalar=1.0, in1=lamb[:, 2 * i + 1, :], op0=ALU.mult, op1=ALU.mult,
                    accum_out=lsum[:, i:i + 1]), [ld["lam"]] + tl))
            te = P.op("scalar", lambda e: e.activation(out=lsum[:, 2:4], in_=lsum[:, 0:2], func=AF.Exp), tl)
            tn = P.op("vector", lambda e: e.tensor_tensor(out=neglam, in0=lsum[:, 3:4], in1=lsum[:, 2:3], op=ALU.subtract), [te])
            tn = P.op("vector", lambda e: e.tensor_scalar(out=neglam, in0=neglam, scalar1=-float(lam_init), scalar2=None, op0=ALU.add), [tn])
            tsl = P.op("vector", lambda e: e.tensor_scalar(out=slnw, in0=slnw, scalar1=float(1.0 - lam_init), scalar2=None, op0=ALU.mult), [ld["subln"]])
            tes = P.op("scalar", lambda e: e.activation(out=esk, in_=esk, func=AF.Exp), [ld["sinks"]])
            state = {"it": 0, "s_free": [None] * NSB, "p_free": [None] * NPB, "acc_free": None, "grp": 0,
                     "pst_free": [None], "deferred": None, "ysg_free": [None, None], "q_free": [None, None]}

            def acc_ap(c, r):
                a = c * 4 + r
                return psA[a // 3][:, (a % 3) * 129:(a % 3) * 129 + 129]

            def finalize_A(par, last_pv):
                f1 = []
                for bk in range(3):
                    n = 3 if bk < 2 else 2
                    f1.append(P.op("vector", lambda e, bk=bk, n=n: e.tensor_copy(
                        out=accs[par][:, bk * 3:bk * 3 + n, :].rearrange("p a b -> p (a b)"), in_=psA[bk][:, 0:n * 129]), [last_pv] if bk == 0 else []))
                f2 = P.op("vector", lambda e: e.reciprocal(out=rc[par], in_=accs[par][:, 0:8, 128]), [f1[-1]])
                f3 = P.op("vector", lambda e: e.tensor_scalar(out=rc[par][:, 4:8], in0=rc[par][:, 4:8], scalar1=neglam[:, 0:1], scalar2=None, op0=ALU.mult), [f2, tn])
                f5 = []
                for r in range(4):
                    f4 = P.op("vector", lambda e, r=r: e.tensor_scalar(out=a_t[par][:, r, :], in0=accs[par][:, r, 0:128], scalar1=rc[par][:, r:r + 1], scalar2=None, op0=ALU.mult), [f3])
                    f5.append(P.op("vector", lambda e, r=r: e.scalar_tensor_tensor(
                        out=a_t[par][:, r, :], in0=accs[par][:, 4 + r, 0:128], scalar=rc[par][:, 4 + r:5 + r], in1=a_t[par][:, r, :],
                        op0=ALU.mult, op1=ALU.add), [f4]))
                return f1[-1], f5

            def finalize_B(par, f5, h, s):
                f6 = []
                for r in range(4):
                    f6.append(P.op("vector", lambda e, r=r: e.scalar_tensor_tensor(out=junk[:, par * 4 + r, :], in0=a_t[par][:, r, :], scalar=1.0, in1=a_t[par][:, r, :],
                                                                             op0=ALU.mult, op1=ALU.mult, accum_out=ssq[par][:, r:r + 1]), [f5[r]]))
                f7 = P.op("scalar", lambda e: e.activation(out=ssq[par][:, 4:8], in_=ssq[par][:, 0:4], func=AF.Ln, bias=eps_t[:, 0:1], scale=1.0 / 128), [f6[-1]])
                f8 = P.op("scalar", lambda e: e.activation(out=ssq[par][:, 8:12], in_=ssq[par][:, 4:8], func=AF.Exp, scale=-0.5), [f7])
                f9 = []
                for r in range(4):
                    f9.append(P.op("vector", lambda e, r=r: e.tensor_scalar(out=y_t[par][:, r, :], in0=a_t[par][:, r, :], scalar1=ssq[par][:, 8 + r:9 + r], scalar2=None, op0=ALU.mult), [f8]))
                tp = None
                for r in range(4):
                    tp = P.op("tensor", lambda e, r=r: e.transpose(psT[par][:, r * 128:(r + 1) * 128], y_t[par][:, r, :], identb),
                              [f9[r]] + ([state["pst_free"][0]] if r == 0 else []))
                f10 = P.op("vector", lambda e: e.tensor_scalar(out=ysg[par], in0=psT[par][:, 0:512], scalar1=slnw[:, 0:1], scalar2=None, op0=ALU.mult), [tp, tsl, state["ysg_free"][par]])
                state["pst_free"][0] = f10
                state["ysg_free"][par] = P.dma("sync", yda_d[h * 128:(h + 1) * 128, s * 512:(s + 1) * 512], ysg[par], [f10])

            bcol = 0
            for s in range(NS_):
                qb = s % 2
                lq = lq_first if s == 0 else P.dma("sync", qs[qb], q_src[:, :, s * 512:(s + 1) * 512], [state["q_free"][qb]])
                last_qk_slot = None
                for h in range(4):
                    par = state["grp"] % 2
                    kb0 = kb_first(s, h)
                    n_it = (nkbf(s) - kb0) * 2
                    items = [(kb, c) for kb in range(kb0, nkbf(s)) for c in range(2)]
                    mul_tok = {}
                    qk_tok = {}

                    def emit_qk(i, h=h, s=s, qb=qb, lq=lq):
                        kb, c = items[i]
                        g_i = state["it"] + i
                        sbi = g_i % NSB
                        deps = [state["s_free"][sbi]]
                        if i == 0:
                            deps += [lq, ld["kda"][h]]
                        qk_tok[i] = P.op("tensor", lambda e: e.matmul(
                            psS[sbi], lhsT=kda[c * 64:(c + 1) * 64, h, kb * 128:(kb + 1) * 128],
                            rhs=qs[qb][c * 64:(c + 1) * 64, h, :], start=True, stop=True), deps)
                        return qk_tok[i]

                    def emit_soft(i, h=h, s=s, bcol_base=bcol, kb0=kb0):
                        kb, c = items[i]
                        g_i = state["it"] + i
                        sbi = g_i % NSB
                        pbi = g_i % NPB
                        col = bcol_base + kb - kb0
                        ex = P.op("scalar", lambda e: e.activation(out=pbuf[pbi], in_=psS[sbi], func=AF.Exp, bias=btab[:, col:col + 1], scale=1.0),
                                  [qk_tok[i], state["p_free"][pbi]])
                        state["s_free"][sbi] = ex
                        if kb >= 4 * s:
                            dtile = dd[:, h, kb - 4 * s, :]
                        else:
                            dtile = dg[:, h, :]
                        if kb < 4 * s and h > 0:
                            mul_tok[i] = ex
                        else:
                            mul_tok[i] = P.op("vector", lambda e: e.tensor_tensor(out=pbuf[pbi], in0=pbuf[pbi], in1=dtile, op=ALU.mult), [ex])

                    def emit_pv(i, h=h, s=s, kb0=kb0):
                        kb, c = items[i]
                        g_i = state["it"] + i
                        pbi = g_i % NPB
                        pv = None
                        for r in range(4):
                            deps = [mul_tok[i]] if r == 0 else []
                            if i < 2 and r == 0:
                                deps += [state["acc_free"]]
                            if r == 0:
                                deps += [ld["vda"][kb // 8]]
                            first = (kb == kb0) and ((c * 4 + r) % 3 == 0)
                            pv = P.op("tensor", lambda e, r=r, first=first: e.matmul(
                                acc_ap(c, r), lhsT=pbuf[pbi][:, r * 128:(r + 1) * 128], rhs=vda[:, kb, h * 129:(h + 1) * 129],
                                start=first, stop=False, skip_group_check=True), deps)
                        state["p_free"][pbi] = pv
                        return pv

                    LOOK = 2
                    for i in range(min(LOOK, n_it)):
                        last_qk_slot = emit_qk(i)
                    last_pv = None
                    for i in range(n_it):
                        if i % 2 == 0:
                            for j in (i + LOOK, i + LOOK + 1):
                                if j < n_it:
                                    last_qk_slot = emit_qk(j)
                        emit_soft(i)
                        last_pv = emit_pv(i)
                        if i == 5 and state["deferred"] is not None:
                            finalize_B(*state["deferred"])
                            state["deferred"] = None
                    if state["deferred"] is not None:
                        finalize_B(*state["deferred"])
                        state["deferred"] = None
                    accf, f5 = finalize_A(par, last_pv)
                    state["acc_free"] = accf
                    state["deferred"] = (par, f5, h, s)
                    state["it"] += n_it
                    state["grp"] += 1
                    bcol += nkbf(s) - kb0
                state["q_free"][qb] = last_qk_slot
            if state["deferred"] is not None:
                finalize_B(*state["deferred"])
                state["deferred"] = None
            steps = [(n, g, hf) for n in range(4 * NS_) for g in range(2) for hf in range(2)]
            sw = {"acc_free": [[state["acc_free"]], [state["acc_free"], state["s_free"][3]]], "qsw_free": [None, None], "yswg_free": [None, None], "lqs": {}, "qk": {}, "mu": {}, "last_pv": None,
                  "pendA": None, "pendB": None, "ysw_t_free": [None, None]}
            it0 = state["it"]
            NSW = 3
            swacc = [[bankf[4], bankf[5]], [bankf[6], bankf[3]]]

            def sw_qk(idx):
                n, g, hf = steps[idx]
                sbi = idx % NSW
                s = n // 4
                qb = s % 2
                nl = n % 4
                if nl == 0 and g == 0 and hf == 0:
                    sw["lqs"][s] = P.dma("sync", qsw[qb], qsw_src[:, :, s * 512:(s + 1) * 512], [sw["qsw_free"][qb]])
                deps = [state["s_free"][sbi], sw["lqs"][s]]
                if idx == 0:
                    deps += ld["ksw"] + [ld["vsw"]]
                t = None
                for kind in range(2):
                    if n == 0 and kind == 0:
                        continue
                    kblk = n - 1 + kind
                    t = P.op("tensor", lambda e, g=g, hf=hf, kblk=kblk, sbi=sbi, nl=nl, qb=qb, kind=kind: e.matmul(
                        psS[sbi][:, kind * 256:(kind + 1) * 256], lhsT=ksw[hf * 64:(hf + 1) * 64, g, kblk * 128:(kblk + 1) * 128],
                        rhs=qsw[qb][hf * 64:(hf + 1) * 64, 2 * g:2 * g + 2, nl * 128:(nl + 1) * 128], start=True, stop=True), deps if t is None else [])
                sw["qk"][idx] = t
                if nl == 3 and g == 1 and hf == 1:
                    sw["qsw_free"][qb] = t

            def sw_soft(idx):
                n, g, hf = steps[idx]
                sbi = idx % NSW
                pbi = (it0 + idx) % NPB
                c0 = 256 if n == 0 else 0
                ex = P.op("scalar", lambda e: e.activation(out=pbuf[pbi][:, c0:512], in_=psS[sbi][:, c0:512], func=AF.Exp), [sw["qk"][idx], state["p_free"][pbi]])
                state["s_free"][sbi] = ex
                base = (g * 2 + hf) * 4
                dt_ap = dsw[:, base:base + 4, :].rearrange("p a b -> p (a b)")
                sw["mu"][idx] = P.op("vector", lambda e: e.tensor_tensor(out=pbuf[pbi][:, c0:512], in0=pbuf[pbi][:, c0:512], in1=dt_ap[:, c0:512], op=ALU.mult), [ex])

            def sw_pv(idx):
                n, g, hf = steps[idx]
                pbi = (it0 + idx) % NPB
                pv = None
                for kind in range(2):
                    if n == 0 and kind == 0:
                        continue
                    kblk = n - 1 + kind
                    for j in range(2):
                        gi = hf + 2 * j
                        first = (hf == 0 and pv is None)
                        deps = []
                        if pv is None:
                            deps = [sw["mu"][idx]]
                            if hf == 0:
                                deps += sw["acc_free"][n % 2]
                        accb = swacc[n % 2][g]
                        pv = P.op("tensor", lambda e, accb=accb, gi=gi, j=j, kblk=kblk, kind=kind, first=first: e.matmul(
                            accb[:, gi * 65:(gi + 1) * 65], lhsT=pbuf[pbi][:, kind * 256 + j * 128:kind * 256 + (j + 1) * 128], rhs=vsw[:, kblk, g * 65:(g + 1) * 65],
                            start=first, stop=False, skip_group_check=True), deps)
                state["p_free"][pbi] = pv
                sw["last_pv"] = pv

            def sw_final(n, last_pv_n):
                par = n % 2
                f1 = None
                for g in range(2):
                    f1 = P.op("vector", lambda e, g=g: e.tensor_copy(out=saccs[par][:, g * 4:(g + 1) * 4, :].rearrange("p a b -> p (a b)"), in_=swacc[par][g][:, 0:260]), [last_pv_n] if g == 0 else [])
                sw["acc_free"][par] = [f1]
                f2 = P.op("vector", lambda e: e.tensor_tensor(out=sden[par], in0=saccs[par][:, :, 64], in1=esk, op=ALU.add), [f1, tes])
                f3 = P.op("vector", lambda e: e.reciprocal(out=sden[par], in_=sden[par]), [f2])
                f4 = None
                for hh in range(8):
                    f4 = P.op("vector", lambda e, hh=hh: e.tensor_scalar(out=ysw_t[par][:, hh, :], in0=saccs[par][:, hh, 0:64], scalar1=sden[par][:, hh:hh + 1], scalar2=None, op0=ALU.mult),
                              [f3] + ([sw["ysw_t_free"][par]] if hh == 0 else []))
                def partB():
                    tp = None
                    for cc in range(4):
                        tp = P.op("tensor", lambda e, cc=cc: e.transpose(psT[par][:, cc * 128:(cc + 1) * 128], ysw_t[par][:, 2 * cc:2 * cc + 2, :].rearrange("p a b -> p (a b)"), identb),
                                  [f4] + ([state["pst_free"][0]] if cc == 0 else []))
                    f5 = P.op("vector", lambda e: e.tensor_copy(out=yswg[par], in_=psT[par][:, 0:512].rearrange("p (a b) -> p a b", b=128)), [tp, sw["yswg_free"][par]])
                    state["pst_free"][0] = f5
                    sw["ysw_t_free"][par] = f5
                    sw["yswg_free"][par] = P.dma("sync", ysw_d.rearrange("(c p) t -> p c t", p=128)[:, :, n * 128:(n + 1) * 128], yswg[par], [f5])
                sw["pendB"] = partB

            sw_qk(0)
            sw_qk(1)
            for idx in range(len(steps)):
                if idx + 2 < len(steps):
                    sw_qk(idx + 2)
                sw_soft(idx)
                n, g, hf = steps[idx]
                if g == 0 and hf == 0 and sw["pendA"] is not None:
                    sw["pendA"]()
                    sw["pendA"] = None
                sw_pv(idx)
                if g == 0 and hf == 0 and sw["pendB"] is not None:
                    sw["pendB"]()
                    sw["pendB"] = None
                if g == 1 and hf == 1:
                    sw["pendA"] = (lambda n=n, lp=sw["last_pv"]: sw_final(n, lp))
            if sw["pendA"] is not None:
                sw["pendA"]()
                sw["pendA"] = None
            if sw["pendB"] is not None:
                sw["pendB"]()
                sw["pendB"] = None
            state["it"] += len(steps)
            prog_barrier(P, bar)

        def stage_mix(l, x_in, x_mid):
            A.reset()
            NS_ = NSF
            wbda = A.alloc([128, 4, D], BF16)
            wbsw = A.alloc([128, 4, D], BF16)
            wmix = A.alloc([128, 8, D], BF16)
            yds = [A.alloc([128, 4, 512], BF16) for i in range(2)]
            yss = [A.alloc([128, 4, 512], BF16) for i in range(2)]
            sg = [A.alloc([128, 16, 512], BF16) for i in range(2)]
            mg = [A.alloc([128, 8, 512], BF16) for i in range(2)]
            m1 = [A.alloc([128, 512], F32) for i in range(2)]
            m2 = [A.alloc([128, 512], F32) for i in range(2)]
            xt = [A.alloc([128, D], F32) for i in range(2)]
            xm = [A.alloc([128, D], F32) for i in range(2)]
            scr = A.alloc([128, D], F32)
            xn = [A.alloc([128, D], BF16) for i in range(2)]
            ss = [A.alloc([128, 4], F32) for i in range(2)]
            xnT = [A.alloc([128, 8, 128], BF16) for i in range(2)]
            wn = A.alloc([128, 8], F32)
            psA = bankf[0:2]
            psB = bankf[2:4]
            psM = bankf[4:6]
            pst = bankb[6:8]
            t_wn = P.dma("sync", wn, nw2_d[l].rearrange("(c p) -> p c", p=128), slow=True)
            lw = []
            for k in range(4):
                lw.append(P.dma("gpsimd", wbda[:, k, :], wbda_d[l, k * 128:(k + 1) * 128, :]))
                lw.append(P.dma("gpsimd", wbsw[:, k, :], wbsw_d[l, k * 128:(k + 1) * 128, :]))
            lm = [P.dma("gpsimd", wmix[:, k, :], wmix_d[l, k * 128:(k + 1) * 128, :]) for k in range(8)]
            P.wait_only("tensor", lw)
            P.wait_only("vector", [t_wn])
            sg_src = fm_d[13 * 128:29 * 128, :].rearrange("(c p) t -> p c t", p=128)
            yd_src = yda_d.rearrange("(c p) t -> p c t", p=128)
            ys_src = ysw_d.rearrange("(c p) t -> p c t", p=128)
            xn_dst = xn2_d.rearrange("(c p) t -> p c t", p=128)
            sg_free = [None, None]
            y_free = [None, None]
            mg_free = [None, None]
            a_free = [None, None]
            b_free = [None, None]
            m_free = [None, None]
            m1_free = [None, None]
            xt_free = [None, None]
            xm_free = [None, None]
            xm_rd = [None, None]
            xn_free = [None, None]
            pst_free = [None, None]
            xnT_free = [None, None]
            mst = {"cnt": 0, "mcnt": 0, "mg_tok": {}, "tb": {}}

            def gate_slot(s, js=range(8)):
                sp = s % 2
                if 0 in js:
                    mst["lsg"] = P.dma("sync", sg[sp], sg_src[:, :, s * 512:(s + 1) * 512], [sg_free[sp]])
                    mst["lyd"] = P.dma("sync", yds[sp], yd_src[:, :, s * 512:(s + 1) * 512], [y_free[sp]])
                    mst["lys"] = P.dma("sync", yss[sp], ys_src[:, :, s * 512:(s + 1) * 512], [y_free[sp]])
                    mst["mg_tok"][s] = []
                lsg, lyd, lys = mst["lsg"], mst["lyd"], mst["lys"]
                mg_tok = mst["mg_tok"][s]
                tb = None
                for j in js:
                    pb = mst["cnt"] % 2
                    mst["cnt"] += 1
                    ta = tb = None
                    for k in range(4):
                        ta = P.op("tensor", lambda e, k=k, pb=pb, j=j, sp=sp: e.matmul(psA[pb], lhsT=wbda[:, k, j * 128:(j + 1) * 128], rhs=yds[sp][:, k, :], start=(k == 0), stop=(k == 3)),
                                  [a_free[pb], lyd] if k == 0 else [])
                    for k in range(4):
                        tb = P.op("tensor", lambda e, k=k, pb=pb, j=j, sp=sp: e.matmul(psB[pb], lhsT=wbsw[:, k, j * 128:(j + 1) * 128], rhs=yss[sp][:, k, :], start=(k == 0), stop=(k == 3)),
                                  [b_free[pb], lys] if k == 0 else [])
                    d1 = P.op("vector", lambda e, pb=pb, j=j, sp=sp: e.tensor_tensor(out=m1[pb], in0=psA[pb], in1=sg[sp][:, j, :], op=ALU.mult), [ta, lsg, m1_free[pb]])
                    a_free[pb] = d1
                    d2 = P.op("vector", lambda e, pb=pb, j=j, sp=sp: e.tensor_tensor(out=m2[pb], in0=psB[pb], in1=sg[sp][:, 8 + j, :], op=ALU.mult), [tb, lsg])
                    b_free[pb] = d2
                    d3 = P.op("vector", lambda e, pb=pb, j=j, sp=sp: e.tensor_tensor(out=mg[sp][:, j, :], in0=m1[pb], in1=m2[pb], op=ALU.add), [d1, d2] + ([mg_free[sp]] if j == 0 else []))
                    m1_free[pb] = d3
                    mg_tok.append(d3)
                if 7 in js:
                    sg_free[sp] = mg_tok[-1]
                    y_free[sp] = tb

            def mix_slot(s, tts=range(4)):
                sp = s % 2
                mg_tok = mst["mg_tok"][s]
                last_mm = None
                for tt4 in tts:
                    tt = s * 4 + tt4
                    xb = tt % 2
                    lx = P.dma("sync", xt[xb], x_in[tt * 128:(tt + 1) * 128, :], [xt_free[xb]])
                    adds = []
                    for half in range(2):
                        pb = mst["mcnt"] % 2
                        mst["mcnt"] += 1
                        mm = None
                        for k in range(8):
                            mm = P.op("tensor", lambda e, k=k, pb=pb, tt4=tt4, half=half, sp=sp: e.matmul(
                                psM[pb], lhsT=mg[sp][:, k, tt4 * 128:(tt4 + 1) * 128], rhs=wmix[:, k, half * 512:(half + 1) * 512], start=(k == 0), stop=(k == 7)),
                                ([m_free[pb]] + mg_tok + lm) if k == 0 else [])
                        last_mm = mm
                        ad = P.op("vector", lambda e, pb=pb, xb=xb, half=half: e.tensor_tensor(out=xm[xb][:, half * 512:(half + 1) * 512], in0=psM[pb], in1=xt[xb][:, half * 512:(half + 1) * 512], op=ALU.add),
                                  [mm, lx, xm_free[xb], xm_rd[xb]])
                        m_free[pb] = ad
                        adds.append(ad)
                    xt_free[xb] = adds[-1]
                    st = P.dma("sync", x_mid[tt * 128:(tt + 1) * 128, :], xm[xb], adds)
                    flush_pending()
                    P.wait_only("scalar", [xn_free[xb]])
                    fin, t4 = emit_rmsnorm_T(P, xm[xb], adds[-1], scr, ss[xb], xn[xb], pst[xb], identb, wn,
                                             lambda c, xb=xb: xnT[xb][:, c, :], eps_t, defer=True, on_act=True)
                    xm_free[xb] = st
                    xm_rd[xb] = t4
                    mst["pending"] = (fin, xb, tt)
                if 3 in tts:
                    mg_free[sp] = last_mm

            def flush_pending():
                if mst.get("pending") is None:
                    return
                fin, xb, tt = mst["pending"]
                mst["pending"] = None
                toks = fin([pst_free[xb]], [xnT_free[xb]])
                xn_free[xb] = toks[-1]
                pst_free[xb] = toks[-1]
                xnT_free[xb] = P.dma("sync", xn_dst[:, :, tt * 128:(tt + 1) * 128], xnT[xb], toks)

            gate_slot(0)
            for s in range(NS_):
                for q in range(4):
                    if s + 1 < NS_:
                        gate_slot(s + 1, range(2 * q, 2 * q + 2))
                    mix_slot(s, range(q, q + 1))
            flush_pending()
            prog_barrier(P, bar)

        def stage_ffn(l, half_i, x_mid, x_out, final):
            A.reset()
            TH = 2048
            NTH = 16
            NSH = 4
            t0 = half_i * TH
            big = A.alloc([128, 8 * (TH + 2)], BF16)
            xnx = big.rearrange("p (k t) -> p k t", t=TH + 2)
            wd1 = big[:, 0:22 * 512].rearrange("p (i c) -> p i c", c=512)
            wd0 = A.alloc([128, 22, 512], BF16)
            hT = A.alloc([128, 22, TH], BF16)
            wug = [A.alloc([128, 8, 256], BF16) for i in range(2)]
            wuv = [A.alloc([128, 8, 256], BF16) for i in range(2)]
            wk = A.alloc([128, 6, 512], F32)
            accg = [wk[:, i, :] for i in range(2)]
            accv = [wk[:, 2 + i, :] for i in range(2)]
            sgt = [wk[:, 4 + i, :] for i in range(2)]
            uc = {(w, p): A.alloc([128, 514], F32) for w in range(2) for p in range(2)}
            cw = A.alloc([128, 3, 44], F32)
            cb = A.alloc([128, 44], F32)
            xt = [A.alloc([128, 512], F32) for i in range(3)]
            xo = [A.alloc([128, D], F32) for i in range(2)]
            fw = A.alloc([128, D], F32) if final else None
            ss = [A.alloc([128, 4], F32) for i in range(2)]
            scr = wk[:, 0:2, :].rearrange("p a b -> p (a b)")
            psU = [bankf[0], bankf[1], bankf[2], bankf[3], bankf[5], bankf[6]]
            NU = 6
            psH = bankf[4]
            psD = bankf[5:8]
            if half_i == 0:
                lx0 = P.op("gpsimd", lambda e: e.memset(xnx[:, :, 0:2], 0.0))
                lx = [P.dma("sync", xnx[:, k, 2:TH + 2], xn2_d[k * 128:(k + 1) * 128, 0:TH]) for k in range(8)] + [lx0]
            else:
                lx = [P.dma("sync", xnx[:, k, :], xn2_d[k * 128:(k + 1) * 128, t0 - 2:t0 + TH]) for k in range(8)]
            lcw = [P.dma("sync", cw[:, j, :], cw_d[l, j].rearrange("(c p) -> p c", p=128), slow=True) for j in range(3)]
            lcb = P.dma("sync", cb, cb_d[l].rearrange("(c p) -> p c", p=128), slow=True)
            lfw = P.dma("sync", fw, fnw_d.partition_broadcast(128)) if final else None
            P.wait_only("vector", lcw + [lcb])
            P.wait_only("scalar", lcw + [lcb])
            lwd0 = []
            wu_free = [None, None]
            u_free = [None] * NU
            h_free = None
            pend = {"f": None}
            acc_free = {0: [None, None], 1: [None, None]}
            sgt_free = [None, None]
            uc_free = {k: [] for k in uc}
            ucnt = 0
            hT_tok = []
            last_up_mm = None
            for gi in range(11):
                i0, n = 2 * gi, 2
                wb = gi % 2
                lg = [P.dma("gpsimd", wug[wb][:, k, :], wup_d[l, k * 128:(k + 1) * 128, i0 * 128:(i0 + n) * 128], [wu_free[wb]]) for k in range(8)]
                lv = [P.dma("gpsimd", wuv[wb][:, k, :], wup_d[l, k * 128:(k + 1) * 128, DFF + i0 * 128:DFF + (i0 + n) * 128], [wu_free[wb]]) for k in range(8)]
                if gi == 1:
                    lwd0.extend([P.dma("gpsimd", wd0[:, i, :], wdn_d[l, i * 128:(i + 1) * 128, 0:512]) for i in range(22)])
                for ci in range(n):
                    i = i0 + ci
                    hm = None
                    for which, wt in ((0, wug[wb]), (1, wuv[wb])):
                        for k in range(8):
                            hm = P.op("tensor", lambda e, k=k, wt=wt, ci=ci, which=which: e.matmul(
                                psH[:, which * 2:which * 2 + 2], lhsT=wt[:, k, ci * 128:(ci + 1) * 128], rhs=xnx[:, k, 0:2], start=(k == 0), stop=(k == 7)),
                                ([h_free] + lg + lv + lx) if (k == 0 and which == 0) else [])
                    halo = {}
                    for which in range(2):
                        halo[(which, 0)] = P.op("vector", lambda e, which=which: e.tensor_copy(out=uc[(which, 0)][:, 0:2], in_=psH[:, which * 2:which * 2 + 2]),
                                                [hm] + uc_free[(which, 0)])
                    h_free = halo[(1, 0)]
                    for s in range(NSH):
                        p = s % 2
                        dlast = {}
                        for which, wt in ((0, wug[wb]), (1, wuv[wb])):
                            ub = ucnt % NU
                            ucnt += 1
                            ch = i if which == 0 else 22 + i
                            mm = None
                            for k in range(8):
                                mm = P.op("tensor", lambda e, k=k, wt=wt, ci=ci, ub=ub, s=s: e.matmul(
                                    psU[ub], lhsT=wt[:, k, ci * 128:(ci + 1) * 128], rhs=xnx[:, k, 2 + s * 512:2 + (s + 1) * 512], start=(k == 0), stop=(k == 7)),
                                    [u_free[ub]] if k == 0 else [])
                            last_up_mm = mm
                            ucb = uc[(which, p)]
                            acc = accg[p] if which == 0 else accv[p]
                            a0 = P.op("scalar", lambda e, ucb=ucb, ub=ub: e.activation(out=ucb[:, 2:514], in_=psU[ub], func=AF.Copy), [mm] + uc_free[(which, p)])
                            frees = []
                            if s < NSH - 1:
                                hcp = P.op("scalar", lambda e, ucb=ucb, which=which, p=p: e.activation(out=uc[(which, 1 - p)][:, 0:2], in_=ucb[:, 512:514], func=AF.Copy), [a0] + uc_free[(which, 1 - p)])
                                halo[(which, s + 1)] = hcp
                                frees.append(hcp)
                            d0 = P.op("scalar", lambda e, acc=acc, ub=ub, ch=ch: e.activation(out=acc, in_=psU[ub], func=AF.Identity, bias=cb[:, ch:ch + 1], scale=cw[:, 2, ch:ch + 1]),
                                      [mm, acc_free[which][p]])
                            u_free[ub] = d0
                            d1 = P.op("vector", lambda e, acc=acc, ucb=ucb, ch=ch: e.scalar_tensor_tensor(out=acc, in0=ucb[:, 1:513], scalar=cw[:, 1, ch:ch + 1], in1=acc, op0=ALU.mult, op1=ALU.add),
                                      [d0, a0, halo[(which, s)]])
                            d2 = P.op("vector", lambda e, acc=acc, ucb=ucb, ch=ch: e.scalar_tensor_tensor(out=acc, in0=ucb[:, 0:512], scalar=cw[:, 0, ch:ch + 1], in1=acc, op0=ALU.mult, op1=ALU.add), [d1])
                            frees.append(d2)
                            uc_free[(which, p)] = frees
                            dlast[which] = d2
                        def fin_glu(p=p, i=i, s=s, dg_=dlast[0], dv_=dlast[1]):
                            a2 = P.op("scalar", lambda e: e.activation(out=sgt[p], in_=accg[p], func=AF.Silu), [dg_, sgt_free[p]])
                            g1 = P.op("vector", lambda e: e.tensor_tensor(out=hT[:, i, s * 512:(s + 1) * 512], in0=sgt[p], in1=accv[p], op=ALU.mult), [a2, dv_])
                            acc_free[0][p] = a2
                            acc_free[1][p] = g1
                            sgt_free[p] = g1
                            hT_tok.append(g1)
                        if pend["f"] is not None:
                            pend["f"]()
                        pend["f"] = fin_glu
                wu_free[wb] = last_up_mm
            if pend["f"] is not None:
                pend["f"]()
                pend["f"] = None
            lwd1 = [P.dma("gpsimd", wd1[:, i, :], wdn_d[l, i * 128:(i + 1) * 128, 512:1024], [last_up_mm]) for i in range(22)]
            P.wait_only("tensor", hT_tok[-8:] + lwd0)
            d_free = [u_free[4], u_free[5], None]
            xt_free = [None] * 3
            xo_free = [None, None]
            dcnt = 0
            for tt in range(NTH):
                ob = tt % 2
                adds = []
                for half in range(2):
                    pb = dcnt % 3
                    dcnt += 1
                    wd = wd0 if half == 0 else wd1
                    lxt = P.dma("sync", xt[pb], x_mid[t0 + tt * 128:t0 + (tt + 1) * 128, half * 512:(half + 1) * 512], [xt_free[pb]])
                    mm = None
                    for i in range(22):
                        deps = []
                        if i == 0:
                            deps = [d_free[pb]] + (lwd1 if half == 1 else [])
                        mm = P.op("tensor", lambda e, i=i, pb=pb, tt=tt, wd=wd: e.matmul(psD[pb], lhsT=hT[:, i, tt * 128:(tt + 1) * 128], rhs=wd[:, i, :], start=(i == 0), stop=(i == 21)), deps)
                    ad = P.op("vector", lambda e, pb=pb, ob=ob, half=half: e.tensor_tensor(out=xo[ob][:, half * 512:(half + 1) * 512], in0=psD[pb], in1=xt[pb], op=ALU.add), [mm, lxt, xo_free[ob]])
                    d_free[pb] = ad
                    xt_free[pb] = ad
                    adds.append(ad)
                if final:
                    t1 = P.op("scalar", lambda e, ob=ob: e.activation(out=scr, in_=xo[ob], func=AF.Square, accum_out=ss[ob][:, 0:1]), adds)
                    t2 = P.op("scalar", lambda e, ob=ob: e.activation(out=ss[ob][:, 1:2], in_=ss[ob][:, 0:1], func=AF.Ln, bias=eps_t[:, 0:1], scale=1.0 / D), [t1])
                    t3 = P.op("scalar", lambda e, ob=ob: e.activation(out=ss[ob][:, 2:3], in_=ss[ob][:, 1:2], func=AF.Exp, scale=-0.5), [t2])
                    t4 = P.op("vector", lambda e, ob=ob: e.scalar_tensor_tensor(out=xo[ob], in0=xo[ob], scalar=ss[ob][:, 2:3], in1=fw, op0=ALU.mult, op1=ALU.mult), [t3, lfw])
                    adds = [t4]
                xo_free[ob] = P.dma("sync", x_out[t0 + tt * 128:t0 + (tt + 1) * 128, :], xo[ob], adds)
            prog_barrier(P, bar)

        x_cur = x_ext
        for l in range(depth):
            lam_init = 0.8 - 0.6 * math.exp(-0.3 * l)
            stage_proj(l, x_cur)
            stage_attn(l, lam_init)
            stage_mix(l, x_cur, xA)
            final = (l == depth - 1)
            x_next = out_d if final else xB
            for hi in range(2):
                stage_ffn(l, hi, xA, x_next, final)
            x_cur = x_next
        P.emit(None)
    return nc


FF_GROUPS = [(0, 4), (4, 4), (8, 4), (12, 4), (16, 4), (20, 2)]
_CACHE = {}


def kernel(**inputs):
    inp = {k: np.asarray(v) for k, v in inputs.items()}
    f32 = lambda a: np.ascontiguousarray(a, dtype=np.float32)
    dg, dd, dsw = make_tables()
    shared = {
        "w_in": f32(inp["w_in"]), "nw1": f32(inp["norm_mix_w"]),
        "lam4": f32(np.stack([inp["lambda_q1"], inp["lambda_k1"], inp["lambda_q2"], inp["lambda_k2"]], axis=1)),
        "subln": f32(inp["subln_w"]), "sinks": f32(inp["sinks"]),
        "w_br_da": f32(inp["w_br_da"]), "w_br_sw": f32(inp["w_br_sw"]), "w_mix": f32(inp["w_mix_out"]), "nw2": f32(inp["norm_ffn_w"]),
        "w_up": f32(inp["w_up"]), "conv_w": f32(inp["conv_w"]), "conv_b": f32(inp["conv_b"]), "w_down": f32(inp["w_down"]),
        "fnw": f32(inp["norm_final_w"]), "ident": np.eye(128, dtype=np.float32),
        "dg": dg, "dd": dd, "dsw": np.ascontiguousarray(dsw.reshape(128, 16, 128)), "btab": make_btab_full(),
    }
    x = f32(inp["x"])
    if "nc" not in _CACHE:
        _CACHE["nc"] = build_fused()
    nc = _CACHE["nc"]
    zero = {k: np.zeros_like(v) for k, v in shared.items()}
    in_maps = []
    for c in range(2 * B):
        if c % 2 == 0:
            m = dict(shared)
            m["x"] = np.ascontiguousarray(x[c // 2])
        else:
            m = dict(zero)
            m["x"] = np.zeros((TF, D), np.float32)
        in_maps.append(m)
    res = run_bass_kernel_spmd(nc, in_maps, core_ids=list(range(2 * B))).results
    out = np.stack([res[2 * b]["out"] for b in range(B)], axis=0)
    return np.ascontiguousarray(out, dtype=np.float32)
```
